# Optimizing a Trainium2 kernel written in Bass

```python
import jax, jax.numpy as jnp
from jax import lax
import numpy as np

D_MODEL = 1024
BATCH = 8
SEQ = 2048
DEPTH = 4
DEC_BATCH = 128
DEC_SEQ = 1
PAST_LEN = 16384
PAGE_SIZE = 128

N_MIXERS = 3
N_A = (DEPTH + 2) // 3
N_B = (DEPTH + 1) // 3
N_C = DEPTH // 3
A_WIDTH = 3
B_WIDTH = 31
B_CH = D_MODEL
C_EXPAND = 128
C_HEADS = D_MODEL // C_EXPAND
C_DK = C_EXPAND
C_DV = D_MODEL // C_HEADS
C_CHUNK = 32
D_FF = -(-8 * D_MODEL // (3 * 256)) * 256
NORM_EPS = 1e-6
LN_EPS = 1e-5

kernel_name = "hybrid_shortconv_conformer_hgrn2_step"


def rms_norm(x, g):
    xf = x.astype(jnp.float32)
    y = xf * lax.rsqrt(jnp.mean(xf * xf, axis=-1, keepdims=True) + NORM_EPS)
    return (y * g.astype(jnp.float32)).astype(x.dtype)


def layer_norm(x, g, b):
    xf = x.astype(jnp.float32)
    mu = jnp.mean(xf, axis=-1, keepdims=True)
    var = jnp.mean(jnp.square(xf - mu), axis=-1, keepdims=True)
    y = (xf - mu) * lax.rsqrt(var + LN_EPS)
    return (y * g.astype(jnp.float32) + b.astype(jnp.float32)).astype(x.dtype)


def causal_dwconv(u, buf, w):
    width, ch = w.shape
    full = jnp.concatenate([buf.astype(u.dtype), u], axis=1)
    y = lax.conv_general_dilated(full, w.astype(u.dtype)[:, None, :], window_strides=(1,),
                                 padding='VALID', dimension_numbers=('NWC', 'WIO', 'NWC'),
                                 feature_group_count=ch)
    return y, full[:, -(width - 1):, :]


def short_conv_mixer(h, buf, w_in, conv_w, w_out):
    bg, cg, hv = jnp.split(h @ w_in, 3, axis=-1)
    y, new_buf = causal_dwconv(cg * hv, buf, conv_w)
    return (bg * y) @ w_out, new_buf


def conformer_conv_mixer(h, buf, w_pw1, b_pw1, dw_w, dw_b, ln_g, ln_b, w_pw2, b_pw2):
    a, gate = jnp.split(h @ w_pw1 + b_pw1, 2, axis=-1)
    u = a * jax.nn.sigmoid(gate)
    y, new_buf = causal_dwconv(u, buf, dw_w)
    y = layer_norm(y + dw_b, ln_g, ln_b)
    return jax.nn.silu(y) @ w_pw2 + b_pw2, new_buf


def hgrn2_chunked(q, k, v, logf, s0):
    n, nh, t, dk = q.shape
    dv = v.shape[-1]
    c = min(C_CHUNK, t)
    pad = (-t) % c
    if pad:
        pw = ((0, 0), (0, 0), (0, pad), (0, 0))
        q, k, v, logf = (jnp.pad(a, pw) for a in (q, k, v, logf))
    nc = (t + pad) // c

    def blocks(a):
        return jnp.moveaxis(a.reshape(n, nh, nc, c, a.shape[-1]), 2, 0)

    mask = jnp.tril(jnp.ones((c, c), dtype=bool))[:, :, None]

    def step(s, xs):
        qc, kc, vc, gc = xs
        b = jnp.cumsum(gc, axis=2)
        diff = b[:, :, :, None, :] - b[:, :, None, :, :]
        decay = jnp.where(mask, jnp.exp(jnp.where(mask, diff, 0.0)), 0.0)
        scores = jnp.einsum('nhtk,nhsk,nhtsk->nhts', qc, kc, decay)
        o = (jnp.einsum('nhtk,nhkv->nhtv', qc * jnp.exp(b), s)
             + jnp.einsum('nhts,nhsv->nhtv', scores, vc))
        b_last = b[:, :, -1:, :]
        s_new = (jnp.exp(b_last[:, :, 0, :])[..., None] * s
                 + jnp.einsum('nhsk,nhsv->nhkv', kc * jnp.exp(b_last - b), vc))
        return s_new, o

    s_fin, o = lax.scan(step, s0, (blocks(q), blocks(k), blocks(v), blocks(logf)))
    o = jnp.moveaxis(o, 0, 2).reshape(n, nh, nc * c, dv)[:, :, :t]
    return o, s_fin


def hgrn2_mixer(h, s0, lb, w_qfig, gnorm, w_out):
    n, t, _ = h.shape
    q, f, i, g = jnp.split(h @ w_qfig, 4, axis=-1)

    def heads(a, d):
        return a.astype(jnp.float32).reshape(n, t, C_HEADS, d).transpose(0, 2, 1, 3)

    lb = lb.reshape(C_HEADS, 1, C_DK)
    logf = jnp.logaddexp(jnp.log(lb), jnp.log1p(-lb) + jax.nn.log_sigmoid(heads(f, C_DK)))
    k = -jnp.expm1(logf)
    qh = jax.nn.silu(heads(q, C_DK)) * (C_DK ** -0.5)
    o, s_new = hgrn2_chunked(qh, k, heads(i, C_DV), logf, s0.astype(jnp.float32))
    o = o * lax.rsqrt(jnp.mean(o * o, axis=-1, keepdims=True) + NORM_EPS) * gnorm.astype(jnp.float32)
    o = o.transpose(0, 2, 1, 3).reshape(n, t, D_MODEL).astype(h.dtype)
    return (o * jax.nn.silu(g)) @ w_out, s_new.astype(s0.dtype)


def swiglu(h, w_gate_up, w_down):
    gate, up = jnp.split(h @ w_gate_up, 2, axis=-1)
    return (jax.nn.silu(gate) * up) @ w_down


def trunk(x, conva, convb, hgrn, p):
    sm = jax.nn.softmax(p['c_lower_bounds'].astype(jnp.float32), axis=0)
    lower = jnp.cumsum(sm, axis=0) - sm[0]
    new_a, new_b, new_c = [], [], []
    for i in range(DEPTH):
        kind, j = i % N_MIXERS, i // N_MIXERS
        h = rms_norm(x, p['norm_mix'][i])
        if kind == 0:
            m, s = short_conv_mixer(h, conva[j], p['a_w_in'][j], p['a_conv_w'][j], p['a_w_out'][j])
            new_a.append(s)
        elif kind == 1:
            m, s = conformer_conv_mixer(h, convb[j], p['b_w_pw1'][j], p['b_b_pw1'][j], p['b_dw_w'][j],
                                        p['b_dw_b'][j], p['b_ln_g'][j], p['b_ln_b'][j],
                                        p['b_w_pw2'][j], p['b_b_pw2'][j])
            new_b.append(s)
        else:
            m, s = hgrn2_mixer(h, hgrn[j], lower[i], p['c_w_qfig'][j], p['c_gnorm'][j], p['c_w_out'][j])
            new_c.append(s)
        x = x + m
        x = x + swiglu(rms_norm(x, p['norm_ffn'][i]), p['ffn_w_gate_up'][i], p['ffn_w_down'][i])
    return rms_norm(x, p['norm_final']), jnp.stack(new_a), jnp.stack(new_b), jnp.stack(new_c)


def setup_inputs(seed: int = 0) -> dict:
    key = jax.random.key(seed)
    ks = jax.random.split(key, 32)

    def nrm(k, shape, scale):
        return jax.random.normal(k, shape, jnp.float32) * scale

    out_scale = (2 * DEPTH) ** -0.5
    d = D_MODEL
    return {
        "x_prompt": nrm(ks[0], (BATCH, SEQ, d), 1.0),
        "x_sample": nrm(ks[1], (DEC_BATCH, DEC_SEQ, d), 1.0),
        "state_conva": nrm(ks[2], (N_A, DEC_BATCH, A_WIDTH - 1, d), 1.0),
        "state_convb": nrm(ks[3], (N_B, DEC_BATCH, B_WIDTH - 1, B_CH), 0.5),
        "state_hgrn": nrm(ks[4], (N_C, DEC_BATCH, C_HEADS, C_DK, C_DV), 1.0),
        "norm_mix": 1.0 + nrm(ks[5], (DEPTH, d), 0.02),
        "a_w_in": nrm(ks[6], (N_A, d, 3 * d), d ** -0.5),
        "a_conv_w": nrm(ks[7], (N_A, A_WIDTH, d), A_WIDTH ** -0.5),
        "a_w_out": nrm(ks[8], (N_A, d, d), d ** -0.5 * out_scale),
        "b_w_pw1": nrm(ks[9], (N_B, d, 2 * B_CH), d ** -0.5),
        "b_b_pw1": nrm(ks[10], (N_B, 2 * B_CH), 0.02),
        "b_dw_w": nrm(ks[11], (N_B, B_WIDTH, B_CH), B_WIDTH ** -0.5),
        "b_dw_b": nrm(ks[12], (N_B, B_CH), 0.02),
        "b_ln_g": 1.0 + nrm(ks[13], (N_B, B_CH), 0.02),
        "b_ln_b": nrm(ks[14], (N_B, B_CH), 0.02),
        "b_w_pw2": nrm(ks[15], (N_B, B_CH, d), B_CH ** -0.5 * out_scale),
        "b_b_pw2": nrm(ks[16], (N_B, d), 0.02),
        "c_lower_bounds": nrm(ks[17], (DEPTH, d), 0.1),
        "c_w_qfig": nrm(ks[18], (N_C, d, 4 * d), d ** -0.5),
        "c_gnorm": 1.0 + nrm(ks[19], (N_C, C_DV), 0.02),
        "c_w_out": nrm(ks[20], (N_C, d, d), d ** -0.5 * out_scale),
        "norm_ffn": 1.0 + nrm(ks[21], (DEPTH, d), 0.02),
        "ffn_w_gate_up": nrm(ks[22], (DEPTH, d, 2 * D_FF), d ** -0.5),
        "ffn_w_down": nrm(ks[23], (DEPTH, D_FF, d), D_FF ** -0.5 * out_scale),
        "norm_final": 1.0 + nrm(ks[24], (d,), 0.02),
    }


def reference(x_prompt, x_sample, state_conva, state_convb, state_hgrn, norm_mix, a_w_in, a_conv_w,
              a_w_out, b_w_pw1, b_b_pw1, b_dw_w, b_dw_b, b_ln_g, b_ln_b, b_w_pw2, b_b_pw2,
              c_lower_bounds, c_w_qfig, c_gnorm, c_w_out, norm_ffn, ffn_w_gate_up, ffn_w_down,
              norm_final):
    p = dict(norm_mix=norm_mix, a_w_in=a_w_in, a_conv_w=a_conv_w, a_w_out=a_w_out,
             b_w_pw1=b_w_pw1, b_b_pw1=b_b_pw1, b_dw_w=b_dw_w, b_dw_b=b_dw_b, b_ln_g=b_ln_g,
             b_ln_b=b_ln_b, b_w_pw2=b_w_pw2, b_b_pw2=b_b_pw2, c_lower_bounds=c_lower_bounds,
             c_w_qfig=c_w_qfig, c_gnorm=c_gnorm, c_w_out=c_w_out, norm_ffn=norm_ffn,
             ffn_w_gate_up=ffn_w_gate_up, ffn_w_down=ffn_w_down, norm_final=norm_final)
    nb = x_prompt.shape[0]
    za = jnp.zeros((N_A, nb) + state_conva.shape[2:], state_conva.dtype)
    zb = jnp.zeros((N_B, nb) + state_convb.shape[2:], state_convb.dtype)
    zc = jnp.zeros((N_C, nb) + state_hgrn.shape[2:], state_hgrn.dtype)
    y_prompt, conva_prompt, convb_prompt, hgrn_prompt = trunk(x_prompt, za, zb, zc, p)
    y_sample, conva_sample, convb_sample, hgrn_sample = trunk(x_sample, state_conva, state_convb,
                                                              state_hgrn, p)
    return (y_prompt, y_sample, conva_prompt, conva_sample, convb_prompt, convb_sample,
            hgrn_prompt, hgrn_sample)
```

```python
import numpy as np
import concourse.bass as bass
import concourse.mybir as mybir
from concourse.bass_utils import run_bass_kernel_spmd

F32 = mybir.dt.float32
BF16 = mybir.dt.bfloat16
AF = mybir.ActivationFunctionType
ALU = mybir.AluOpType
AX = mybir.AxisListType

D = 1024
NPR = 2048
NS = 16
NT = NPR + NS
DFF = 2816
TILES = [(0, 512), (512, 512), (1024, 512), (1536, 512)]
STILE = (2048, 16)
R_SLOTS = 6
SLOT = 4096
PREFETCH = 3
SAME_ENG_SYNC = ('pool', 'dve', 'act')
N_LAYERS = 4
STOP_AFTER = None
SKIP_FFN = False
DG_ENG = 'pool'
SKIP_MIX = False
DEBUG_CORES = None
FFN_STOP = None
SKIP_KINDS = ()
C_STOP = None
C_TILES = None
KDMA = 8
ARENA_BYTES = 75776
NVEC = 64
R_NMIX, R_NFFN, R_NFIN, R_ACONV, R_BB1, R_BDW, R_BDWB, R_LNG, R_LNB, R_BB2, R_CLB, R_GN = 0, 4, 8, 9, 15, 17, 48, 49, 50, 51, 52, 56


class Buf:
    __slots__ = ("base", "lo", "hi", "lw", "rd", "rdd", "ov")

    def __init__(self, base, lo, hi):
        self.base, self.lo, self.hi = base, lo, hi
        self.lw = None
        self.rd = {}
        self.rdd = []
        self.ov = []


class V:
    __slots__ = ("ap", "bufs", "wt")

    def __init__(self, ap, bufs, wt=None):
        self.ap, self.bufs, self.wt = ap, bufs, wt


class Op:
    __slots__ = ("eng", "fn", "reads", "writes", "key", "dma", "deps", "tick", "sem", "val", "pre",
                 "needinc", "wreads", "wtile")


class Prog:
    def __init__(self, nc):
        self.nc = nc
        self.ops = []
        self.bufmap = {}
        self.bybase = {}
        self._bank = 0

    def buf(self, base, lo, hi):
        k = (base, lo, hi)
        b = self.bufmap.get(k)
        if b is None:
            b = Buf(base, lo, hi)
            lst = self.bybase.setdefault(base, [])
            for o in lst:
                if o.lo < hi and lo < o.hi:
                    o.ov.append(b)
                    b.ov.append(o)
            lst.append(b)
            self.bufmap[k] = b
        return b

    def add(self, eng, fn, reads, writes, dma=False, register=True):
        op = Op()
        op.eng, op.fn, op.dma = eng, fn, dma
        op.reads = [b for v in reads for b in v.bufs]
        op.writes = [b for v in writes for b in v.bufs]
        op.wreads = [v.wt for v in reads if v.wt is not None]
        op.wtile = None
        op.needinc = False
        op.tick = 0
        op.pre = 0
        op.key = float(len(self.ops))
        if register:
            self.ops.append(op)
        return op

    def bank(self):
        b = self._bank
        self._bank = (b + 1) % 8
        return b

    def mm(self, out, lhsT, rhs, start=True, stop=True, tp=None):
        kw = {} if tp is None else {"tile_position": tp}
        self.add("pe", lambda e: e.matmul(out.ap, lhsT=lhsT.ap, rhs=rhs.ap, start=start, stop=stop, **kw),
                 [lhsT, rhs], [out])

    def tr(self, out, in_, ident):
        self.add("pe", lambda e: e.transpose(out.ap, in_.ap, ident.ap), [in_, ident], [out])

    def act(self, out, in_, func, bias=None, scale=None):
        reads = [in_]
        kw = {}
        if bias is not None:
            if isinstance(bias, V):
                reads.append(bias)
                kw["bias"] = bias.ap
            else:
                kw["bias"] = float(bias)
        if scale is not None:
            if isinstance(scale, V):
                reads.append(scale)
                kw["scale"] = scale.ap
            else:
                kw["scale"] = float(scale)
        self.add("act", lambda e: e.activation(out=out.ap, in_=in_.ap, func=func, **kw), reads, [out])

    def tt(self, eng, out, in0, in1, op):
        self.add(eng, lambda e: e.tensor_tensor(out=out.ap, in0=in0.ap, in1=in1.ap, op=op), [in0, in1], [out])

    def stt(self, out, in0, scalar, in1, op0, op1):
        reads = [in0, in1]
        s = scalar
        if isinstance(scalar, V):
            reads.append(scalar)
            s = scalar.ap
        self.add("dve", lambda e: e.scalar_tensor_tensor(out=out.ap, in0=in0.ap, scalar=s, in1=in1.ap,
                                                         op0=op0, op1=op1), reads, [out])

    def ts(self, eng, out, in0, s1, op0, s2=None, op1=None):
        reads = [in0]
        a1, a2 = s1, s2
        if isinstance(s1, V):
            reads.append(s1)
            a1 = s1.ap
        if isinstance(s2, V):
            reads.append(s2)
            a2 = s2.ap
        if op1 is None:
            self.add(eng, lambda e: e.tensor_scalar(out=out.ap, in0=in0.ap, scalar1=a1, scalar2=None, op0=op0),
                     reads, [out])
        else:
            self.add(eng, lambda e: e.tensor_scalar(out=out.ap, in0=in0.ap, scalar1=a1, scalar2=a2, op0=op0,
                                                    op1=op1), reads, [out])

    def cp(self, eng, out, in_):
        if eng == "act":
            self.add("act", lambda e: e.copy(out=out.ap, in_=in_.ap), [in_], [out])
        else:
            self.add(eng, lambda e: e.tensor_copy(out=out.ap, in_=in_.ap), [in_], [out])

    def recip(self, out, in_):
        self.add("dve", lambda e: e.reciprocal(out=out.ap, in_=in_.ap), [in_], [out])

    def reduce_add(self, out, in_):
        self.add("dve", lambda e: e.tensor_reduce(out=out.ap, in_=in_.ap, axis=AX.X, op=ALU.add), [in_], [out])

    def memset(self, eng, out, val):
        self.add(eng, lambda e: e.memset(out.ap, val), [], [out])

    def asel(self, out, in_, pattern, cmp, fill, base, cm):
        self.add("pool", lambda e: e.affine_select(out=out.ap, in_=in_.ap, pattern=pattern, compare_op=cmp,
                                                   fill=fill, base=base, channel_multiplier=cm), [in_], [out])

    def dma(self, q, out_ap, in_ap, reads, writes, register=True):
        return self.add(q, lambda e: e.dma_start(out=out_ap, in_=in_ap), reads, writes, dma=True,
                        register=register)

    def finalize(self, extra_ops):
        nc = self.nc
        ops = sorted(self.ops + extra_ops, key=lambda o: o.key)
        slot_cur = {}
        for op in ops:
            deps = set()
            for b in op.reads:
                for o in [b] + b.ov:
                    if o.lw is not None:
                        deps.add(o.lw)
            for b in op.writes:
                for o in [b] + b.ov:
                    if o.lw is not None:
                        deps.add(o.lw)
                    deps.update(o.rd.values())
                    deps.update(o.rdd)
            if op.wtile is not None:
                slot_cur[op.wtile[0]] = op.wtile[1]
            for (s, i) in op.wreads:
                assert slot_cur.get(s) == i, ("weight ring hazard", s, i, slot_cur.get(s))
            for b in op.reads:
                if op.dma:
                    b.rdd.append(op)
                else:
                    b.rd[op.eng] = op
            for b in op.writes:
                b.lw = op
                b.rd = {}
                b.rdd = []
            deps.discard(op)
            keep = []
            for d in deps:
                if d.eng == op.eng and not d.dma and not op.dma:
                    if op.eng == "pe" or op.eng not in SAME_ENG_SYNC:
                        continue
                keep.append(d)
            op.deps = keep
            for d in keep:
                if not d.dma:
                    d.needinc = True
        cnt = {}
        qcnt = {}
        for op in ops:
            if op.dma:
                n = qcnt.get(op.eng, 0)
                qcnt[op.eng] = n + 1
                op.sem = (op.eng, n % KDMA)
                op.val = 16 * (n // KDMA + 1)
                op.pre = 16 * (n // KDMA)
            elif op.needinc:
                cnt[op.eng] = cnt.get(op.eng, 0) + 1
                op.tick = cnt[op.eng]
        self.sorted_ops = ops
        self.qcnt = qcnt
        return ops

    def emit(self):
        nc = self.nc
        ops = self.sorted_ops
        engs = {"pe": [], "act": [], "dve": [], "pool": [], "sp": []}
        for op in ops:
            engs[op.eng].append(op)
        import contextlib
        with contextlib.ExitStack() as es:
            esem = {k: es.enter_context(nc.semaphore("e_" + k)) for k in ("pe", "act", "dve", "pool")}
            dsem = {}
            for q in ("sp", "pool"):
                for i in range(KDMA):
                    dsem[(q, i)] = es.enter_context(nc.semaphore("d_%s%d" % (q, i)))
            block = es.enter_context(nc.Block())
            qcnt = self.qcnt

            def run(e, name):
                waited = {}
                for op in engs[name]:
                    need = {}
                    for d in op.deps:
                        if d.dma:
                            k = ("d",) + d.sem
                            s, val = dsem[d.sem], d.val
                        else:
                            k = ("e", d.eng)
                            s, val = esem[d.eng], d.tick
                        if need.get(k, (None, 0))[1] < val:
                            need[k] = (s, val)
                    if op.dma and op.pre:
                        k = ("d",) + op.sem
                        if need.get(k, (None, 0))[1] < op.pre:
                            need[k] = (dsem[op.sem], op.pre)
                    for k, (s, val) in need.items():
                        if waited.get(k, 0) < val:
                            e.wait_ge(s, val)
                            waited[k] = val
                    ins = op.fn(e)
                    if op.dma:
                        ins.then_inc(dsem[op.sem], 16)
                    elif op.needinc:
                        ins.then_inc(esem[name], 1)
                if name in ("sp", "pool"):
                    n = qcnt.get(name, 0)
                    for i in range(min(KDMA, n)):
                        tot = (n - i + KDMA - 1) // KDMA
                        e.wait_ge(dsem[(name, i)], 16 * tot)

            @block.tensor
            def _(e):
                run(e, "pe")

            @block.scalar
            def _(e):
                run(e, "act")

            @block.vector
            def _(e):
                run(e, "dve")

            @block.gpsimd
            def _(e):
                run(e, "pool")

            @block.sync
            def _(e):
                run(e, "sp")


class Mem:
    def __init__(self, P, base, ap, boff, esz):
        self.P, self.base, self.ap, self.boff, self.esz = P, base, ap, boff, esz

    def b(self, lo, hi):
        return self.P.buf(self.base, self.boff + lo * self.esz, self.boff + hi * self.esz)

    def v(self, lo, hi, p0=0, p1=128):
        return V(self.ap[p0:p1, lo:hi], [self.b(lo, hi)])

    def v3(self, lo, a, b, p0=0, p1=128):
        return V(self.ap[p0:p1, lo:lo + a * b].rearrange("p (a b) -> p a b", a=a), [self.b(lo, lo + a * b)])


class Arena:
    def __init__(self, P, ap_f32, base, nbytes):
        self.P, self.ap, self.base, self.nbytes = P, ap_f32, base, nbytes
        self.top = 0

    def alloc(self, n, dt):
        esz = 4 if dt == F32 else 2
        nb = (n * esz + 3) // 4 * 4
        off = self.top
        assert off + nb <= self.nbytes, ("arena overflow", off, nb, self.nbytes)
        self.top += nb
        ap = self.ap[:, off // 4:(off + nb) // 4]
        if dt != F32:
            ap = ap.bitcast(dt)
        return Mem(self.P, self.base, ap, off, esz)


class WStream:
    def __init__(self, P, ring):
        self.P, self.ring = P, ring
        self.n = 0
        self.first_use = []
        self.dmas = []

    def get(self, parts):
        P = self.P
        i = self.n
        self.n += 1
        slot = i % R_SLOTS
        base = slot * SLOT
        sbuf = self.ring.b(base, base + SLOT)
        self.first_use.append(len(P.ops))
        off = 0
        outs = []
        for pi, (src, a, b) in enumerate(parts):
            dst = self.ring.ap[:, base + off:base + off + a * b].rearrange("p (a b) -> p a b", a=a)
            v = V(dst, [sbuf], wt=(slot, i))
            op = P.dma("pool", dst, src, [], [V(dst, [sbuf])], register=False)
            op.wtile = (slot, i)
            op.key = (i, pi)
            self.dmas.append(op)
            outs.append(v)
            off += a * b
        assert off <= SLOT
        return outs

    def finish(self):
        for op in self.dmas:
            i, pi = op.key
            j = max(0, i - PREFETCH)
            op.key = self.first_use[j] - 0.5 + (i * 4 + pi) * 1e-6
        return self.dmas


def build_program():
    nc = bass.Bass("TRN2", target_bir_lowering=False)
    P = Prog(nc)

    def din(name, shape):
        return nc.dram_tensor(name, shape, F32, kind="ExternalInput").ap()

    def dout(name, shape):
        return nc.dram_tensor(name, shape, F32, kind="ExternalOutput").ap()

    xp_d = din("xp", [NPR, D])
    xs_d = din("xs", [NS, D])
    sta_d = din("sta", [2, 2 * NS, D])
    stb_d = din("stb", [NS * 30, D])
    sth_d = din("sth", [NS, 8, 128, 128])
    vec_d = din("vec", [NVEC, D])
    a_win = din("a_w_in", [2, D, 3 * D])
    a_wout = din("a_w_out", [2, D, D])
    b_w1 = din("b_w_pw1", [D, 2 * D])
    b_w2 = din("b_w_pw2", [D, D])
    c_wq = din("c_w_qfig", [D, 4 * D])
    c_wo = din("c_w_out", [D, D])
    f_wgu = din("ffn_w_gate_up", [4, D, 2 * DFF])
    f_wd = din("ffn_w_down", [4, DFF, D])
    yp_d = dout("yp", [NPR, D])
    ys_d = dout("ys", [NS, D])
    cap_d = dout("cap", [2, 2, D])
    cas_d = dout("cas", [2, 2 * NS, D])
    cbp_d = dout("cbp", [30, D])
    cbs_d = dout("cbs", [NS, 30, D])
    hgp_d = dout("hgp", [8, 128, 128])
    hgs_d = dout("hgs", [NS, 8, 128, 128])

    import contextlib
    es = contextlib.ExitStack()
    xt = es.enter_context(nc.sbuf_tensor("x", [128, 8, NT], F32))
    ringt = es.enter_context(nc.sbuf_tensor("ring", [128, R_SLOTS * SLOT], BF16))
    arenat = es.enter_context(nc.sbuf_tensor("arena", [128, ARENA_BYTES // 4], F32))
    CONST_F32 = 5396
    constt = es.enter_context(nc.sbuf_tensor("const", [128, CONST_F32], F32))
    pst = es.enter_context(nc.psum_tensor("ps", [128, 8, 512], F32))

    ring = Mem(P, "ring", ringt[:], 0, 2)
    W = WStream(P, ring)
    A = Arena(P, arenat[:], "arena", ARENA_BYTES)
    C = Arena(P, constt[:], "const", CONST_F32 * 4)

    def xv(c, t0, n):
        return V(xt[:, c, t0:t0 + n], [P.buf("x", (c * NT + t0) * 4, (c * NT + t0 + n) * 4)])

    def xall(t0, n, c0=0, c1=8):
        return V(xt[:, c0:c1, t0:t0 + n],
                 [P.buf("x", (c * NT + t0) * 4, (c * NT + t0 + n) * 4) for c in range(c0, c1)])

    def ps(bank, lo=0, hi=512, p0=0, p1=128):
        return V(pst[p0:p1, bank, lo:hi], [P.buf("ps", bank * 2048 + lo * 4, bank * 2048 + hi * 4)])

    ident = C.alloc(128, F32)
    maskbd = C.alloc(128, F32)
    trirev = C.alloc(128, F32)
    ones_bf = C.alloc(128, BF16)
    CF = C.alloc(8 * NVEC, F32)
    LBB = C.alloc(1024, F32)
    OMLB = C.alloc(1024, F32)
    LB = C.alloc(8, F32)
    OML = C.alloc(8, F32)
    carryA = C.alloc(16, F32)
    carryB = C.alloc(240, F32)
    S32 = C.alloc(1024, F32)
    SBF = C.alloc(2048, BF16)
    sbf_par = [0] * 8
    ind4 = C.alloc(4, F32)
    identb = C.alloc(128, BF16)

    def cf(c, r):
        return V(CF.ap[:, c * NVEC + r:c * NVEC + r + 1], [CF.b(c * NVEC, (c + 1) * NVEC)])

    iv = ident.v(0, 128)
    P.memset("pool", iv, 0.0)
    P.asel(iv, iv, [[-1, 128]], ALU.not_equal, 1.0, 0, 1)
    P.cp("dve", identb.v(0, 128), iv)
    mv = maskbd.v(0, 128)
    P.memset("pool", mv, 1.0)
    P.asel(mv, mv, [[1, 128]], ALU.is_ge, 0.0, 0, -1)
    for j in range(4):
        sv = maskbd.v(32 * j, 32 * j + 32)
        P.asel(sv, sv, [[0, 32]], ALU.is_ge, 0.0, -32 * j, 1)
    rv = trirev.v(0, 128)
    P.memset("pool", rv, 1.0)
    P.asel(rv, rv, [[-1, 128]], ALU.is_gt, 0.0, 0, 1)
    for j in range(4):
        sv = trirev.v(32 * j, 32 * j + 32)
        P.asel(sv, sv, [[0, 32]], ALU.is_gt, 0.0, 32 * j + 32, -1)
    i4 = ind4.v(0, 4)
    P.memset("pool", i4, 1.0)
    P.asel(i4, i4, [[-32, 4]], ALU.is_ge, 0.0, 0, 1)
    P.asel(i4, i4, [[32, 4]], ALU.is_ge, 0.0, 31, -1)
    P.memset("dve", ones_bf.v(0, 128), 1.0)
    P.memset("dve", carryA.v(0, 16), 0.0)
    P.memset("dve", carryB.v(0, 240), 0.0)
    P.memset("dve", S32.v(0, 1024), 0.0)
    P.memset("dve", SBF.v(0, 2048), 0.0)

    rr = {"s": 0, "o": 0}

    def done():
        P.finalize(W.finish())
        P.emit()
        es.close()
        return nc

    if STOP_AFTER == 0:
        return done()

    def in_rows(dram_rows, n, dst4):
        st = A_stage[rr["s"] % 2]
        rr["s"] += 1
        sv_ = st.v(0, 1024, 0, n)
        P.dma("sp", sv_.ap, dram_rows, [], [sv_])
        for cg in range(2):
            bk = P.bank()
            for c4 in range(4):
                c = cg * 4 + c4
                P.tr(ps(bk, c4 * 128, c4 * 128 + n), V(st.ap[0:n, c * 128:(c + 1) * 128], [st.b(0, 1024)]),
                     V(ident.ap[0:n, 0:n], [ident.b(0, 128)]))
            src = V(pst[:, bk, :].rearrange("p (a b) -> p a b", a=4)[:, :, 0:n], [P.buf("ps", bk * 2048, bk * 2048 + 2048)])
            P.cp("act" if cg == 0 else "dve", dst4(cg), src)

    def out_rows(src, n, dram_rows):
        st = A_stage[rr["s"] % 2]
        rr["s"] += 1
        if n < 128:
            pad = A.alloc(1024, F32)
            P.memset("dve", pad.v(0, 1024), 0.0)
            for c in range(8):
                P.cp("dve", pad.v(c * 128, c * 128 + n), src(c))
            src = lambda c: pad.v(c * 128, (c + 1) * 128)
        for cg in range(2):
            bk = P.bank()
            for c4 in range(4):
                c = cg * 4 + c4
                P.tr(ps(bk, c4 * 128, c4 * 128 + 128), src(c), iv)
            P.cp("act" if cg == 0 else "dve", st.v(cg * 512, cg * 512 + 512, 0, n), ps(bk, 0, 512, 0, n))
        sv_ = st.v(0, 1024, 0, n)
        P.dma("sp", dram_rows, sv_.ap, [sv_], [])

    A.top = 0
    A_stage = [A.alloc(1024, F32), A.alloc(1024, F32)]
    clb = A.alloc(4096, F32)
    tmp1 = A.alloc(1024, F32)
    st = A_stage[0]
    sv_ = st.v(0, 1024, 0, NVEC)
    P.dma("sp", sv_.ap, vec_d[:, :], [], [sv_])
    bk = P.bank()
    for c in range(8):
        P.tr(ps(bk, c * NVEC, (c + 1) * NVEC), V(st.ap[0:NVEC, c * 128:(c + 1) * 128], [st.b(0, 1024)]),
             V(ident.ap[0:NVEC, 0:NVEC], [ident.b(0, 128)]))
    P.cp("dve", CF.v(0, 8 * NVEC), ps(bk))
    rr["s"] = 1
    if STOP_AFTER == 1:
        return done()
    e4 = A.alloc(32, F32)
    ssum = A.alloc(8, F32)
    CF3 = CF.ap[:, :].rearrange("p (c r) -> p c r", c=8)
    P.act(V(e4.ap[:, 0:32].rearrange("p (c r) -> p c r", c=8), [e4.b(0, 32)]),
          V(CF3[:, :, R_CLB:R_CLB + 4], [CF.b(0, 8 * NVEC)]), AF.Exp)
    e43 = e4.ap[:, 0:32].rearrange("p (c r) -> p c r", c=8)
    P.reduce_add(ssum.v(0, 8), V(e43, [e4.b(0, 32)]))
    P.recip(ssum.v(0, 8), ssum.v(0, 8))
    P.tt("dve", LB.v(0, 8), V(e43[:, :, 1], [e4.b(0, 32)]), V(e43[:, :, 2], [e4.b(0, 32)]), ALU.add)
    P.tt("dve", LB.v(0, 8), LB.v(0, 8), ssum.v(0, 8), ALU.mult)
    P.ts("dve", OML.v(0, 8), LB.v(0, 8), -1.0, ALU.mult, 1.0, ALU.add)
    if STOP_AFTER == 2:
        return done()
    cv = clb.v(0, 4096)
    P.dma("sp", clb.ap[:, 0:4096].rearrange("p (a b) -> p a b", a=4), vec_d[R_CLB:R_CLB + 4, :].partition_broadcast(128),
          [], [cv])
    P.act(cv, cv, AF.Exp)
    P.tt("dve", LBB.v(0, 1024), clb.v(1024, 2048), clb.v(2048, 3072), ALU.add)
    P.tt("dve", tmp1.v(0, 1024), clb.v(0, 1024), clb.v(3072, 4096), ALU.add)
    P.tt("dve", tmp1.v(0, 1024), tmp1.v(0, 1024), LBB.v(0, 1024), ALU.add)
    P.recip(tmp1.v(0, 1024), tmp1.v(0, 1024))
    P.tt("dve", LBB.v(0, 1024), LBB.v(0, 1024), tmp1.v(0, 1024), ALU.mult)
    P.ts("dve", OMLB.v(0, 1024), LBB.v(0, 1024), -1.0, ALU.mult, 1.0, ALU.add)
    if STOP_AFTER == 3:
        return done()
    xq = {"next": 0}

    def load_next_block():
        i = xq["next"]
        if i < 16:
            in_rows(xp_d[i * 128:(i + 1) * 128, :], 128, lambda cg, i=i: xall(i * 128, 128, cg * 4, cg * 4 + 4))
        elif i == 16:
            in_rows(xs_d[:, :], NS, lambda cg: xall(NPR, NS, cg * 4, cg * 4 + 4))
        xq["next"] = i + 1

    lazy_x = N_LAYERS >= 1 and not SKIP_MIX and 0 not in SKIP_KINDS
    for i in range(4 if lazy_x else 17):
        load_next_block()

    if STOP_AFTER == 4:
        return done()
    def rmsnorm(t0, n, grow, hout, sq, scr, out_f32=False):
        P.act(sq.v3(0, 8, n), xall(t0, n), AF.Square)
        bk = P.bank()
        for c in range(8):
            P.mm(ps(bk, 0, n), ones_bf.v(0, 128), sq.v(c * n, (c + 1) * n), start=(c == 0), stop=(c == 7))
        rs = scr.v(0, n)
        P.act(rs, ps(bk, 0, n), AF.Ln, bias=1e-6, scale=1.0 / D)
        P.act(rs, rs, AF.Exp, scale=-0.5)
        for c in range(8):
            P.stt(hout(c), xv(c, t0, n), cf(c, grow), rs, ALU.mult, ALU.mult)

    def resid_add(o, t0, n, bk, bias=None):
        if bias is None:
            P.tt("dve", xv(o, t0, n), xv(o, t0, n), ps(bk, 0, n), ALU.add)
        else:
            P.stt(xv(o, t0, n), ps(bk, 0, n), bias, xv(o, t0, n), ALU.add, ALU.add)

    def out_proj(wd, zmem, t0, n, bias_row=None):
        for half in range(2):
            (wt,) = W.get([(wd[:, half * 512:(half + 1) * 512].rearrange("(k p) n -> p k n", p=128), 8, 512)])
            for o4 in range(4):
                o = half * 4 + o4
                bk = P.bank()
                for k in range(8):
                    P.mm(ps(bk, 0, n), V(wt.ap[:, k, o4 * 128:(o4 + 1) * 128], wt.bufs, wt.wt),
                         zmem.v(k * n, (k + 1) * n), start=(k == 0), stop=(k == 7))
                resid_add(o, t0, n, bk, None if bias_row is None else cf(o, bias_row))

    def wsub(wt, k, lo, hi):
        return V(wt.ap[:, k, lo:hi], wt.bufs, wt.wt)

    def proj(bk, wt, hmem, n, col0=0, ncol=128):
        for k in range(8):
            P.mm(ps(bk, 0, n), wsub(wt, k, col0, col0 + ncol), hmem.v(k * n, (k + 1) * n), start=(k == 0), stop=(k == 7))

    def mixer_a(layer, j, t0, n, sample):
        A.top = 0
        stg = [A.alloc(1024, F32), A.alloc(1024, F32)]
        A_stage[0], A_stage[1] = stg
        h = A.alloc(8 * n, BF16)
        sq = A.alloc(8 * n, BF16)
        z = A.alloc(8 * n, BF16)
        scr = [A.alloc(512, F32) for _ in range(4)]
        ub = [A.alloc(516, F32) for _ in range(2)]
        if sample:
            sta = A.alloc(8 * 32, F32)
            newa = A.alloc(8 * 32, F32)
            in_rows(sta_d[j, :, :], 32, lambda cg: sta.v3(cg * 128, 4, 32))
        rmsnorm(t0, n, R_NMIX + layer, lambda c: h.v(c * n, (c + 1) * n), sq, scr[0])
        wi = a_win[j]
        for c in range(8):
            if c % 4 == 0:
                cq = c // 4
                wts = [W.get([(wi[:, g * 1024 + cq * 512:g * 1024 + (cq + 1) * 512].rearrange("(k p) n -> p k n", p=128), 8, 512)])[0]
                       for g in range(3)]
            cc0 = (c % 4) * 128
            bb, bc, bh = P.bank(), P.bank(), P.bank()
            proj(bb, wts[0], h, n, cc0)
            proj(bc, wts[1], h, n, cc0)
            proj(bh, wts[2], h, n, cc0)
            cgs = scr[1 + c % 2].v(0, n)
            tmp = scr[3].v(0, n)
            P.cp("act", cgs, ps(bc, 0, n))
            w0, w1, w2 = (cf(c, R_ACONV + 3 * j + r) for r in range(3))
            if not sample:
                u = ub[c % 2]
                P.cp("dve", u.v(0, 2), carryA.v(c * 2, c * 2 + 2))
                P.tt("dve", u.v(2, 2 + n), cgs, ps(bh, 0, n), ALU.mult)
                P.cp("dve", carryA.v(c * 2, c * 2 + 2), u.v(n, n + 2))
                P.ts("dve", tmp, u.v(0, n), w0, ALU.mult)
                P.stt(tmp, u.v(1, n + 1), w1, tmp, ALU.mult, ALU.add)
                P.stt(tmp, u.v(2, n + 2), w2, tmp, ALU.mult, ALU.add)
            else:
                u = ub[c % 2].v(0, n)
                P.tt("dve", u, cgs, ps(bh, 0, n), ALU.mult)
                s3 = sta.ap[:, c * 32:(c + 1) * 32].rearrange("p (s r) -> p s r", r=2)
                n3 = newa.ap[:, c * 32:(c + 1) * 32].rearrange("p (s r) -> p s r", r=2)
                sb_, nb_ = [sta.b(c * 32, c * 32 + 32)], [newa.b(c * 32, c * 32 + 32)]
                P.ts("dve", tmp, V(s3[:, :, 0], sb_), w0, ALU.mult)
                P.stt(tmp, V(s3[:, :, 1], sb_), w1, tmp, ALU.mult, ALU.add)
                P.stt(tmp, u, w2, tmp, ALU.mult, ALU.add)
                P.cp("dve", V(n3[:, :, 0], nb_), V(s3[:, :, 1], sb_))
                P.cp("dve", V(n3[:, :, 1], nb_), u)
            P.tt("dve", z.v(c * n, (c + 1) * n), tmp, ps(bb, 0, n), ALU.mult)
            if layer == 0 and not sample and c % 2 == 1:
                load_next_block()
        out_proj(a_wout[j], z, t0, n)
        if sample:
            out_rows(lambda c: newa.v(c * 32, c * 32 + 32), 32, cas_d[j, :, :])

    def mixer_b(layer, t0, n, sample):
        A.top = 0
        if sample:
            stg = [A.alloc(1024, F32), A.alloc(1024, F32)]
            A_stage[0], A_stage[1] = stg
        h = A.alloc(8 * n, BF16)
        sq = A.alloc(8 * n, BF16)
        ybf = A.alloc(8 * n, BF16)
        z = h
        y = A.alloc(8 * n, F32)
        scr = [A.alloc(512, F32) for _ in range(5)]
        ub = [A.alloc(544, F32) for _ in range(2)]
        if not sample:
            ubfb = [A.alloc(544, BF16) for _ in range(2)]
            dgb = [A.alloc(31 * 128, BF16) for _ in range(2)]
        if sample:
            stb = A.alloc(8 * 480, F32)
            us = A.alloc(8 * NS, F32)
            prod = A.alloc(480, F32)
            for q in range(4):
                in_rows(stb_d[q * 120:(q + 1) * 120, :], 120,
                        lambda cg, q=q: V(stb.ap[:, :].rearrange("p (c m) -> p c m", c=8)[:, cg * 4:cg * 4 + 4, q * 120:(q + 1) * 120],
                                          [stb.b(c * 480 + q * 120, c * 480 + (q + 1) * 120) for c in range(cg * 4, cg * 4 + 4)]))
            P.dma("sp", cbs_d[:, 0:29, :], stb_d[:, :].rearrange("(s r) d -> s r d", r=30)[:, 1:30, :], [], [])
        rmsnorm(t0, n, R_NMIX + layer, lambda c: h.v(c * n, (c + 1) * n), sq, scr[0])
        wst = {}

        def s1(c):
            if c % 4 == 0:
                cq = c // 4
                wst["w"] = [W.get([(b_w1[:, g * 1024 + cq * 512:g * 1024 + (cq + 1) * 512].rearrange("(k p) n -> p k n", p=128), 8, 512)])[0]
                            for g in range(2)]
            wts = wst["w"]
            cc0 = (c % 4) * 128
            ba, bg = P.bank(), P.bank()
            proj(ba, wts[0], h, n, cc0)
            proj(bg, wts[1], h, n, cc0)
            sig = scr[1 + c % 2].v(0, n)
            P.act(sig, ps(bg, 0, n), AF.Sigmoid, bias=cf(c, R_BB1 + 1))
            yc = y.v(c * n, (c + 1) * n)
            if not sample:
                ubf = ubfb[c % 2]
                P.cp("dve", ubf.v(0, 30), carryB.v(c * 30, c * 30 + 30))
                P.stt(ubf.v(30, 30 + n), ps(ba, 0, n), cf(c, R_BB1), sig, ALU.add, ALU.mult)
                P.stt(carryB.v(c * 30, c * 30 + 30), ps(ba, n - 30, n), cf(c, R_BB1),
                      V(sig.ap[:, n - 30:n], sig.bufs), ALU.add, ALU.mult)
                dg = dgb[c % 2]
                P.tt(DG_ENG, dg.v3(0, 31, 128),
                     V(identb.ap[:, 0:128].unsqueeze(1).to_broadcast([128, 31, 128]), [identb.b(0, 128)]),
                     V(CF.ap[:, c * NVEC + R_BDW:c * NVEC + R_BDW + 31].unsqueeze(2).to_broadcast([128, 31, 128]),
                       [CF.b(c * NVEC, (c + 1) * NVEC)]), ALU.mult)
            else:
                u = us.v(c * NS, (c + 1) * NS)
                P.stt(u, ps(ba, 0, n), cf(c, R_BB1), sig, ALU.add, ALU.mult)
                wv = V(CF.ap[:, c * NVEC + R_BDW:c * NVEC + R_BDW + 30].unsqueeze(1).to_broadcast([128, NS, 30]),
                       [CF.b(c * NVEC, (c + 1) * NVEC)])
                P.tt("dve", prod.v3(0, NS, 30), stb.v3(c * 480, NS, 30), wv, ALU.mult)
                red = scr[3].v(0, NS)
                P.reduce_add(red, prod.v3(0, NS, 30))
                P.stt(yc, u, cf(c, R_BDW + 30), red, ALU.mult, ALU.add)
                P.ts("dve", yc, yc, cf(c, R_BDWB), ALU.add)

        def s2(c):
            yc = y.v(c * n, (c + 1) * n)
            ubf, dg = ubfb[c % 2], dgb[c % 2]
            by = P.bank()
            for jj in range(31):
                P.mm(ps(by, 0, n), dg.v(jj * 128, (jj + 1) * 128), ubf.v(jj, jj + n), start=(jj == 0), stop=(jj == 30))
            P.act(yc, ps(by, 0, n), AF.Identity, bias=cf(c, R_BDWB))

        if sample:
            for c in range(8):
                s1(c)
        else:
            s1(0)
            for c in range(8):
                if c + 1 < 8:
                    s1(c + 1)
                s2(c)
        P.act(sq.v3(0, 8, n), y.v3(0, 8, n), AF.Square)
        P.cp("act", ybf.v3(0, 8, n), y.v3(0, 8, n))
        b1, b2 = P.bank(), P.bank()
        for c in range(8):
            P.mm(ps(b1, 0, n), ones_bf.v(0, 128), ybf.v(c * n, (c + 1) * n), start=(c == 0), stop=(c == 7))
        for c in range(8):
            P.mm(ps(b2, 0, n), ones_bf.v(0, 128), sq.v(c * n, (c + 1) * n), start=(c == 0), stop=(c == 7))
        mean, msq, var = scr[0].v(0, n), scr[3].v(0, n), scr[4].v(0, n)
        P.ts("dve", mean, ps(b1, 0, n), 1.0 / D, ALU.mult)
        P.tt("dve", msq, mean, mean, ALU.mult)
        P.stt(var, ps(b2, 0, n), 1.0 / D, msq, ALU.mult, ALU.subtract)
        P.act(var, var, AF.Ln, bias=1e-5)
        P.act(var, var, AF.Exp, scale=-0.5)
        for c in range(8):
            yc = y.v(c * n, (c + 1) * n)
            P.tt("dve", yc, yc, mean, ALU.subtract)
            P.tt("dve", yc, yc, var, ALU.mult)
            P.act(z.v(c * n, (c + 1) * n), yc, AF.Silu, bias=cf(c, R_LNB), scale=cf(c, R_LNG))
        out_proj(b_w2, z, t0, n, bias_row=R_BB2)
        if sample:
            out_rows(lambda c: us.v(c * NS, (c + 1) * NS), NS, cbs_d[:, 29, :])

    def mixer_c(layer, t0, n, sample):
        A.top = 0
        h = A.alloc(8 * n, BF16)
        z = A.alloc(8 * n, BF16)
        sq = z
        nscr = 6
        scrp = [A.alloc(512, F32) for _ in range(nscr)]
        sc = {"i": 0}

        def scratch():
            m = scrp[sc["i"] % nscr]
            sc["i"] += 1
            return m

        rmsnorm(t0, n, R_NMIX + layer, lambda c: h.v(c * n, (c + 1) * n), sq, scratch())
        if C_STOP == 0:
            return
        osq = A.alloc(512, BF16)
        qt = A.alloc(512, BF16)
        kt = A.alloc(512, BF16)
        if not sample:
            nst = n // 128
            logf = A.alloc(nst * 1024, F32)
            khat = A.alloc(nst * 1024, BF16)
            vtok = A.alloc(nst * 1024, BF16)
            scm = [A.alloc(128, BF16) for _ in range(4)]
            vmb = [A.alloc(512, BF16) for _ in range(2)]
            epd = [A.alloc(512, F32) for _ in range(2)]
            qtd = [qt, A.alloc(512, BF16)]
            ktd = [kt, A.alloc(512, BF16)]
            its = [(hg, st_) for hg in range(2) for st_ in range(nst)]
            prs = [its[i:i + 2] for i in range(0, len(its), 2)]
            wtok = {}
            tst = {}

            def tbanks(p, i):
                return (p % 2) * 4 + 2 * i, (p % 2) * 4 + 2 * i + 1

            def PA(p):
                for i, (hg, st_) in enumerate(prs[p]):
                    if hg not in wtok:
                        (wf_,) = W.get([(c_wq[:, 1024 + hg * 512:1024 + (hg + 1) * 512].rearrange("(k p) n -> p k n", p=128), 8, 512)])
                        (wi_,) = W.get([(c_wq[:, 2048 + hg * 512:2048 + (hg + 1) * 512].rearrange("(k p) n -> p k n", p=128), 8, 512)])
                        wtok[hg] = (wf_, wi_)
                    wf_, wi_ = wtok[hg]
                    bf_, bi = tbanks(p, i)
                    for k in range(8):
                        P.mm(ps(bf_), h.v(k * n + st_ * 128, k * n + (st_ + 1) * 128), wsub(wf_, k, 0, 512),
                             start=(k == 0), stop=(k == 7))
                    for k in range(8):
                        P.mm(ps(bi), h.v(k * n + st_ * 128, k * n + (st_ + 1) * 128), wsub(wi_, k, 0, 512),
                             start=(k == 0), stop=(k == 7))

            def SA(p):
                for i, (hg, st_) in enumerate(prs[p]):
                    bf_, bi = tbanks(p, i)
                    lo = st_ * 1024 + hg * 512
                    s1 = scratch().v(0, 512)
                    tst[(p, i)] = [s1]
                    P.act(s1, ps(bf_), AF.Sigmoid)
                    P.cp("act", vtok.v(lo, lo + 512), ps(bi))
                    P.tt("dve", s1, s1, OMLB.v(hg * 512, (hg + 1) * 512), ALU.mult)
                    P.tt("dve", s1, s1, LBB.v(hg * 512, (hg + 1) * 512), ALU.add)

            def SB(p):
                for i, (hg, st_) in enumerate(prs[p]):
                    lo = st_ * 1024 + hg * 512
                    P.act(logf.v(lo, lo + 512), tst[(p, i)][0], AF.Ln)
                for i, (hg, st_) in enumerate(prs[p]):
                    bf_, bi = tbanks(p, i)
                    lo = st_ * 1024 + hg * 512
                    s2 = scratch().v(0, 512)
                    tst[(p, i)].append(s2)
                    P.ts("dve", s2, tst[(p, i)][0], -1.0, ALU.mult, 1.0, ALU.add)
                    P.mm(ps(bf_), trirev.v(0, 128), logf.v(lo, lo + 512))

            def SC(p):
                for i, (hg, st_) in enumerate(prs[p]):
                    bf_, bi = tbanks(p, i)
                    lo = st_ * 1024 + hg * 512
                    s3 = scratch().v(0, 512)
                    P.act(s3, ps(bf_), AF.Exp)
                    P.tt("dve", khat.v(lo, lo + 512), tst[(p, i)][1], s3, ALU.mult)

            PA(0)
            for p in range(len(prs)):
                if p + 1 < len(prs):
                    PA(p + 1)
                SA(p)
                SB(p)
                SC(p)
        else:
            ktok = A.alloc(1024, F32)
            vtk = A.alloc(1024, F32)
            vd = A.alloc(NS * 128, F32)
            dm = A.alloc(NS * 128, F32)
            s0b = [A.alloc(1024, F32) for _ in range(4)]
            snb = [A.alloc(1024, F32) for _ in range(4)]

            def load_s0(hd_):
                for half_ in range(2):
                    s0v_ = s0b[(hd_ % 2) * 2 + half_].v3(0, 8, 128)
                    P.dma("sp", s0v_.ap, sth_d[half_ * 8:(half_ + 1) * 8, hd_, :, :].rearrange("s k v -> k s v"), [], [s0v_])

            load_s0(0)
            qs = A.alloc(NS, F32)
            fgf = A.alloc(NS, F32)
            dmv = dm.v3(0, NS, 128, 0, NS)
            P.memset("pool", dmv, 1.0)
            P.asel(dmv, dmv, [[1, NS], [0, 128]], ALU.is_equal, 0.0, 0, -1)
            for hg in range(2):
                (wf,) = W.get([(c_wq[:, 1024 + hg * 512:1024 + (hg + 1) * 512].rearrange("(k p) n -> p k n", p=128), 8, 512)])
                (wi,) = W.get([(c_wq[:, 2048 + hg * 512:2048 + (hg + 1) * 512].rearrange("(k p) n -> p k n", p=128), 8, 512)])
                bf_, bi = P.bank(), P.bank()
                for k in range(8):
                    P.mm(ps(bf_, 0, 512, 0, NS), h.v(k * n, k * n + NS), wsub(wf, k, 0, 512), start=(k == 0), stop=(k == 7))
                for k in range(8):
                    P.mm(ps(bi, 0, 512, 0, NS), h.v(k * n, k * n + NS), wsub(wi, k, 0, 512), start=(k == 0), stop=(k == 7))
                s1 = scratch().v(0, 512, 0, NS)
                P.act(s1, ps(bf_, 0, 512, 0, NS), AF.Sigmoid, scale=-1.0)
                P.tt("dve", ktok.v(hg * 512, (hg + 1) * 512, 0, NS), s1, OMLB.v(hg * 512, (hg + 1) * 512, 0, NS), ALU.mult)
                P.cp("act", vtk.v(hg * 512, (hg + 1) * 512, 0, NS), ps(bi, 0, 512, 0, NS))
        if C_STOP == 1:
            return
        if sample:
            NH = 8 * NS
            for hd in range(8):
                if hd % 4 == 0:
                    hq = hd // 4
                    wts = [W.get([(c_wq[:, g * 1024 + hq * 512:g * 1024 + (hq + 1) * 512].rearrange("(k p) n -> p k n", p=128), 8, 512)])[0]
                           for g in (0, 1, 3)]
                cc0 = (hd % 4) * 128
                for gi in range(3):
                    for k in range(8):
                        P.mm(ps(gi, hd * NS, (hd + 1) * NS), wsub(wts[gi], k, cc0, cc0 + 128), h.v(k * n, k * n + NS),
                             start=(k == 0), stop=(k == 7))
            qs_all = A.alloc(NH, F32)
            sg_all = A.alloc(NH, F32)
            fg_all = A.alloc(NH, F32)
            P.act(qs_all.v(0, NH), ps(0, 0, NH), AF.Silu)
            P.act(sg_all.v(0, NH), ps(2, 0, NH), AF.Silu)
            P.act(fg_all.v(0, NH), ps(1, 0, NH), AF.Sigmoid, scale=-1.0)
            P.ts("dve", qs_all.v(0, NH), qs_all.v(0, NH), 128.0 ** -0.5, ALU.mult)
            P.tt("dve", fg_all.v3(0, 8, NS), fg_all.v3(0, 8, NS),
                 V(OML.ap[:, 0:8].unsqueeze(2).to_broadcast([128, 8, NS]), [OML.b(0, 8)]), ALU.mult)
            P.ts("dve", fg_all.v(0, NH), fg_all.v(0, NH), -1.0, ALU.mult, 1.0, ALU.add)
            BO = 4
            for hd in range(8):
                if hd + 1 < 8:
                    load_s0(hd + 1)
                vb = V(vtk.ap[0:NS, hd * 128:(hd + 1) * 128].unsqueeze(1).to_broadcast([NS, NS, 128]),
                       [vtk.b(hd * 128, (hd + 1) * 128)])
                P.tt("dve", vd.v3(0, NS, 128, 0, NS), vb, dmv, ALU.mult)
                for half in range(2):
                    s0 = s0b[(hd % 2) * 2 + half]
                    sn = snb[(hd % 2) * 2 + half]
                    for i2 in range(2):
                        bkv = 6 + i2
                        P.mm(ps(bkv), ktok.v(hd * 128, (hd + 1) * 128, 0, NS),
                             vd.v((half * 8 + i2 * 4) * 128, (half * 8 + i2 * 4 + 4) * 128, 0, NS))
                    for jj in range(8):
                        col = hd * NS + half * 8 + jj
                        P.stt(sn.v(jj * 128, (jj + 1) * 128), s0.v(jj * 128, (jj + 1) * 128), fg_all.v(col, col + 1),
                              ps(6 + jj // 4, (jj % 4) * 128, (jj % 4 + 1) * 128), ALU.mult, ALU.add)
                    snv = sn.v3(0, 8, 128)
                    P.dma("sp", hgs_d[half * 8:(half + 1) * 8, hd, :, :].rearrange("s k v -> k s v"), snv.ap, [snv], [])
                    for jj in range(8):
                        col = hd * NS + half * 8 + jj
                        P.mm(ps(BO, col, col + 1), sn.v(jj * 128, (jj + 1) * 128), qs_all.v(col, col + 1))
            osq_all = A.alloc(NH, BF16)
            sd_all = A.alloc(NH, F32)
            P.act(osq_all.v(0, NH), ps(BO, 0, NH), AF.Square)
            P.mm(ps(5, 0, NH), ones_bf.v(0, 128), osq_all.v(0, NH))
            P.act(sd_all.v(0, NH), ps(5, 0, NH), AF.Ln, bias=1e-6, scale=1.0 / 128)
            P.act(sd_all.v(0, NH), sd_all.v(0, NH), AF.Exp, scale=-0.5)
            P.stt(sd_all.v(0, NH), ps(BO, 0, NH), cf(0, R_GN), sd_all.v(0, NH), ALU.mult, ALU.mult)
            P.tt("dve", z.v(0, NH), sd_all.v(0, NH), sg_all.v(0, NH), ALU.mult)

        def prep_pair(hds, wts):
            sqv, sgv, env = [], [], []
            for i, hd in enumerate(hds):
                cc0 = (hd % 4) * 128
                proj(2 * i, wts[0], h, n, cc0)
                proj(2 * i + 1, wts[1], h, n, cc0)
                for st_ in range(nst):
                    lo = st_ * 1024 + hd * 128
                    P.mm(ps(6 + i, st_ * 128, (st_ + 1) * 128), logf.v(lo, lo + 128), maskbd.v(0, 128))
            for i, hd in enumerate(hds):
                sqv.append(scratch().v(0, n))
                P.act(sqv[i], ps(2 * i, 0, n), AF.Silu)
            for i, hd in enumerate(hds):
                sgv.append(scratch().v(0, n))
                P.act(sgv[i], ps(2 * i + 1, 0, n), AF.Sigmoid, scale=-1.0)
            for i, hd in enumerate(hds):
                env.append(scratch().v(0, n))
                P.act(epd[i].v(0, n), ps(6 + i, 0, n), AF.Exp)
                P.act(env[i], ps(6 + i, 0, n), AF.Exp, scale=-1.0)
            for i, hd in enumerate(hds):
                omlv = V(OML.ap[:, hd:hd + 1], [OML.b(0, 8)])
                P.stt(qtd[i].v(0, n), sqv[i], 128.0 ** -0.5, epd[i].v(0, n), ALU.mult, ALU.mult)
                P.stt(ktd[i].v(0, n), sgv[i], omlv, env[i], ALU.mult, ALU.mult)

        def chain(hd, BO_, BKV_, epm, qt, kt, scm2, vm):
            for st_ in range(nst):
                c0 = st_ * 128
                lo = st_ * 1024 + hd * 128
                P.mm(ps(BS, 0, 128), kt.v(c0, c0 + 128), qt.v(c0, c0 + 128))
                sm = scm2[st_ % 2].v(0, 128)
                P.tt("dve", sm, ps(BS, 0, 128), maskbd.v(0, 128), ALU.mult)
                P.mm(ps(BO_, c0, c0 + 128), vtok.v(lo, lo + 128), sm, start=True, stop=False)
                P.tt("dve", vm.v3(0, 4, 128),
                     V(vtok.ap[:, lo:lo + 128].unsqueeze(1).to_broadcast([128, 4, 128]), [vtok.b(lo, lo + 128)]),
                     V(ind4.ap[:, 0:4].unsqueeze(2).to_broadcast([128, 4, 128]), [ind4.b(0, 4)]), ALU.mult)
                P.mm(ps(BKV_, 0, 512), khat.v(lo, lo + 128), vm.v(0, 512))
                yield
                for j in range(4):
                    par = sbf_par[hd]
                    so = hd * 256 + par * 128
                    P.mm(ps(BO_, c0 + 32 * j, c0 + 32 * j + 32), SBF.v(so, so + 128),
                         qt.v(c0 + 32 * j, c0 + 32 * j + 32), start=False, stop=(j == 3))
                    col = c0 + 32 * j + 31
                    s32v = S32.v(hd * 128, (hd + 1) * 128)
                    P.stt(s32v, s32v, epm.v(col, col + 1), ps(BKV_, j * 128, (j + 1) * 128), ALU.mult, ALU.add)
                    par ^= 1
                    sbf_par[hd] = par
                    so = hd * 256 + par * 128
                    P.cp("act", SBF.v(so, so + 128), s32v)
                    yield

        def norm_pair(hds, wts):
            sdv, sgv = [], []
            for i, hd in enumerate(hds):
                proj(2 * i, wts[2], h, n, (hd % 4) * 128)
                P.act(osqd[i].v(0, n), ps(4 + i, 0, n), AF.Square)
                P.mm(ps(2 * i + 1, 0, n), ones_bf.v(0, 128), osqd[i].v(0, n))
            for i, hd in enumerate(hds):
                sdv.append(scratch().v(0, n))
                P.act(sdv[i], ps(2 * i + 1, 0, n), AF.Ln, bias=1e-6, scale=1.0 / 128)
            for i, hd in enumerate(hds):
                P.act(sdv[i], sdv[i], AF.Exp, scale=-0.5)
            for i, hd in enumerate(hds):
                sgv.append(scratch().v(0, n))
                P.act(sgv[i], ps(2 * i, 0, n), AF.Silu)
            for i, hd in enumerate(hds):
                t1 = scratch().v(0, n)
                P.stt(t1, ps(4 + i, 0, n), cf(0, R_GN), sdv[i], ALU.mult, ALU.mult)
                P.tt("dve", z.v(hd * n, (hd + 1) * n), t1, sgv[i], ALU.mult)

        if not sample:
            BS = 3
            osqd = [osq, A.alloc(512, BF16)]
            for pair in range(4):
                if pair % 2 == 0:
                    hq = pair // 2
                    wts = [W.get([(c_wq[:, g * 1024 + hq * 512:g * 1024 + (hq + 1) * 512].rearrange("(k p) n -> p k n", p=128), 8, 512)])[0]
                           for g in (0, 1, 3)]
                hds = (2 * pair, 2 * pair + 1)
                prep_pair(hds, wts)
                gens = [chain(hds[i], 4 + i, 6 + i, epd[i], qtd[i], ktd[i], scm[2 * i:2 * i + 2], vmb[i])
                        for i in range(2)]
                alive = list(gens)
                while alive:
                    for g in list(alive):
                        try:
                            next(g)
                        except StopIteration:
                            alive.remove(g)
                norm_pair(hds, wts)
        out_proj(c_wo, z, t0, n)

    def ffn(layer):
        A.top = 0
        h = A.alloc(8 * NT, BF16)
        actb = A.alloc(6 * NT, BF16)
        sq = A.alloc(8 * 512, BF16)
        scr = [A.alloc(512, F32) for _ in range(4)]
        alltiles = TILES + [STILE]
        for (t0, n) in alltiles:
            rmsnorm(t0, n, R_NFFN + layer,
                    lambda c, t0=t0, n=n: V(h.ap[:, c * NT + t0:c * NT + t0 + n], [h.b(c * NT + t0, c * NT + t0 + n)]),
                    sq, scr[0])
        wgu = f_wgu[layer]
        wdn = f_wd[layer]
        si = 0
        if FFN_STOP == 0:
            return
        for gi, (g0, gs) in enumerate([(0, 6), (6, 6), (12, 5), (17, 5)]):
            cc = g0
            while cc < g0 + gs:
                nch = min(4, g0 + gs - cc)
                (wg,) = W.get([(wgu[:, cc * 128:(cc + nch) * 128].rearrange("(k p) n -> p k n", p=128), 8, nch * 128)])
                (wu,) = W.get([(wgu[:, DFF + cc * 128:DFF + (cc + nch) * 128].rearrange("(k p) n -> p k n", p=128), 8, nch * 128)])
                order = [(ci, tl) for ci in range(nch) for tl in alltiles]
                if cc == 0:
                    order = [(ci, tl) for tl in alltiles for ci in range(nch)]
                for ci, (t0, n) in order:
                    a_i = cc + ci - g0
                    if True:
                        bg, bu = P.bank(), P.bank()
                        for k in range(8):
                            P.mm(ps(bg, 0, n), wsub(wg, k, ci * 128, (ci + 1) * 128),
                                 h.v(k * NT + t0, k * NT + t0 + n), start=(k == 0), stop=(k == 7))
                        for k in range(8):
                            P.mm(ps(bu, 0, n), wsub(wu, k, ci * 128, (ci + 1) * 128),
                                 h.v(k * NT + t0, k * NT + t0 + n), start=(k == 0), stop=(k == 7))
                        sg = scr[1 + si % 3].v(0, n)
                        si += 1
                        P.act(sg, ps(bg, 0, n), AF.Silu)
                        P.tt("dve", actb.v(a_i * NT + t0, a_i * NT + t0 + n), sg, ps(bu, 0, n), ALU.mult)
                cc += nch
            if FFN_STOP == 1:
                return
            if layer == N_LAYERS - 1 and gi == 3 and FFN_STOP is None:
                wt2 = [W.get([(wdn[g0 * 128:(g0 + gs) * 128, hf * 512:(hf + 1) * 512].rearrange("(j p) n -> p j n", p=128), gs, 512)])[0]
                       for hf in range(2)]
                keep_top = A.top
                A.top = 0
                A_stage[0], A_stage[1] = A.alloc(1024, F32), A.alloc(1024, F32)
                yfin = A.alloc(8 * 512, F32)
                def dproj(t0, n):
                    for o in range(8):
                        bk = P.bank()
                        for jx in range(gs):
                            P.mm(ps(bk, 0, n), wsub(wt2[o // 4], jx, (o % 4) * 128, (o % 4 + 1) * 128),
                                 actb.v(jx * NT + t0, jx * NT + t0 + n), start=(jx == 0), stop=(jx == gs - 1))
                        resid_add(o, t0, n, bk)

                def fin_tile(t0, n):
                    rmsnorm(t0, n, R_NFIN, lambda c, n=n: yfin.v(c * 512, c * 512 + n), sq, scr[0])
                    A.top = 24576
                    if n == 512:
                        for q in range(4):
                            i = t0 // 128 + q
                            out_rows(lambda c, q=q: yfin.v(c * 512 + q * 128, c * 512 + (q + 1) * 128), 128,
                                     yp_d[i * 128:(i + 1) * 128, :])
                    else:
                        out_rows(lambda c: yfin.v(c * 512, c * 512 + NS), NS, ys_d[:, :])

                dproj(*alltiles[0])
                for ti_ in range(len(alltiles)):
                    if ti_ + 1 < len(alltiles):
                        dproj(*alltiles[ti_ + 1])
                    fin_tile(*alltiles[ti_])
                A.top = keep_top
                fin["done"] = True
                continue
            for half in range(2):
                (wt,) = W.get([(wdn[g0 * 128:(g0 + gs) * 128, half * 512:(half + 1) * 512].rearrange("(j p) n -> p j n", p=128), gs, 512)])
                for o4 in range(4):
                    o = half * 4 + o4
                    for (t0, n) in alltiles:
                        bk = P.bank()
                        for jx in range(gs):
                            P.mm(ps(bk, 0, n), wsub(wt, jx, o4 * 128, (o4 + 1) * 128),
                                 actb.v(jx * NT + t0, jx * NT + t0 + n), start=(jx == 0), stop=(jx == gs - 1))
                        resid_add(o, t0, n, bk)
            if FFN_STOP is not None and FFN_STOP >= 2 and gi == FFN_STOP - 2:
                return

    fin = {"done": False}
    for layer in range(N_LAYERS):
        kind, j = layer % 3, layer // 3
        if kind == 0:
            P.memset("dve", carryA.v(0, 16), 0.0)
        for (t0, n) in TILES + [STILE]:
            sample = (t0 == NPR)
            if SKIP_MIX or kind in SKIP_KINDS:
                continue
            if kind == 2 and C_TILES == 'p0' and t0 != 0:
                continue
            if kind == 2 and C_TILES == 's' and not sample:
                continue
            if kind == 0:
                mixer_a(layer, j, t0, n, sample)
            elif kind == 1:
                mixer_b(layer, t0, n, sample)
            else:
                mixer_c(layer, t0, n, sample)
        A.top = 0
        A_stage[0], A_stage[1] = A.alloc(1024, F32), A.alloc(1024, F32)
        if kind == 0:
            out_rows(lambda c: carryA.v(c * 2, c * 2 + 2), 2, cap_d[j, :, :])
        elif kind == 1:
            out_rows(lambda c: carryB.v(c * 30, c * 30 + 30), 30, cbp_d[:, :])
        else:
            sv = S32.v3(0, 8, 128)
            P.dma("sp", hgp_d[:, :, :].rearrange("h k v -> k h v"), sv.ap, [sv], [])
        if not SKIP_FFN:
            ffn(layer)

    if fin["done"]:
        return done()
    A.top = 0
    A_stage[0], A_stage[1] = A.alloc(1024, F32), A.alloc(1024, F32)
    sq = A.alloc(8 * 512, BF16)
    scr0 = A.alloc(512, F32)
    yf = [A.alloc(8 * 512, F32) for _ in range(2)]
    fi = 0
    for T in range(4):
        yb = yf[fi % 2]
        fi += 1
        rmsnorm(T * 512, 512, R_NFIN, lambda c, yb=yb: yb.v(c * 512, (c + 1) * 512), sq, scr0)
        for q in range(4):
            i = T * 4 + q
            out_rows(lambda c, yb=yb, q=q: yb.v(c * 512 + q * 128, c * 512 + (q + 1) * 128), 128,
                     yp_d[i * 128:(i + 1) * 128, :])
    if STOP_AFTER == 8:
        return done()
    yb = yf[fi % 2]
    rmsnorm(NPR, NS, R_NFIN, lambda c: yb.v(c * NS, (c + 1) * NS), sq, scr0)
    if STOP_AFTER == 9:
        return done()
    out_rows(lambda c: yb.v(c * NS, (c + 1) * NS), NS, ys_d[:, :])

    return done()


_CACHE = {}


def kernel(x_prompt, x_sample, state_conva, state_convb, state_hgrn, norm_mix, a_w_in, a_conv_w, a_w_out,
           b_w_pw1, b_b_pw1, b_dw_w, b_dw_b, b_ln_g, b_ln_b, b_w_pw2, b_b_pw2, c_lower_bounds, c_w_qfig,
           c_gnorm, c_w_out, norm_ffn, ffn_w_gate_up, ffn_w_down, norm_final):
    f = lambda a: np.ascontiguousarray(np.asarray(a, dtype=np.float32))
    nco = 8
    vec = np.zeros((NVEC, D), np.float32)
    vec[R_NMIX:R_NMIX + 4] = f(norm_mix)
    vec[R_NFFN:R_NFFN + 4] = f(norm_ffn)
    vec[R_NFIN] = f(norm_final)
    vec[R_ACONV:R_ACONV + 6] = f(a_conv_w).reshape(6, D)
    vec[R_BB1:R_BB1 + 2] = f(b_b_pw1).reshape(2, D)
    vec[R_BDW:R_BDW + 31] = f(b_dw_w).reshape(31, D)
    vec[R_BDWB] = f(b_dw_b).reshape(D)
    vec[R_LNG] = f(b_ln_g).reshape(D)
    vec[R_LNB] = f(b_ln_b).reshape(D)
    vec[R_BB2] = f(b_b_pw2).reshape(D)
    vec[R_CLB:R_CLB + 4] = f(c_lower_bounds)
    vec[R_GN, 0:128] = f(c_gnorm).reshape(128)
    shared = {
        "vec": vec, "a_w_in": f(a_w_in), "a_w_out": f(a_w_out), "b_w_pw1": f(b_w_pw1)[0], "b_w_pw2": f(b_w_pw2)[0],
        "c_w_qfig": f(c_w_qfig)[0], "c_w_out": f(c_w_out)[0], "ffn_w_gate_up": f(ffn_w_gate_up),
        "ffn_w_down": f(ffn_w_down),
    }
    xp, xs, sa, sb, sh = f(x_prompt), f(x_sample), f(state_conva), f(state_convb), f(state_hgrn)
    in_maps = []
    for c in range(nco):
        s0, s1 = c * NS, (c + 1) * NS
        m = dict(shared)
        m["xp"] = xp[c]
        m["xs"] = np.ascontiguousarray(xs[s0:s1, 0, :])
        m["sta"] = np.ascontiguousarray(sa[:, s0:s1]).reshape(2, 2 * NS, D)
        m["stb"] = np.ascontiguousarray(sb[0, s0:s1]).reshape(NS * 30, D)
        m["sth"] = np.ascontiguousarray(sh[0, s0:s1])
        in_maps.append(m)
    if "nc" not in _CACHE:
        _CACHE["nc"] = build_program()
    nc = _CACHE["nc"]
    if DEBUG_CORES is not None:
        res = run_bass_kernel_spmd(nc, in_maps[:DEBUG_CORES], core_ids=list(range(DEBUG_CORES)))
        r = list(res.results) + [res.results[0]] * (nco - DEBUG_CORES)
    else:
        res = run_bass_kernel_spmd(nc, in_maps, core_ids=list(range(nco)))
        r = res.results
    y_prompt = np.stack([r[c]["yp"] for c in range(nco)], 0)
    y_sample = np.concatenate([r[c]["ys"] for c in range(nco)], 0).reshape(128, 1, D)
    conva_prompt = np.stack([r[c]["cap"] for c in range(nco)], 1)
    conva_sample = np.concatenate([r[c]["cas"].reshape(2, NS, 2, D) for c in range(nco)], 1)
    convb_prompt = np.stack([r[c]["cbp"] for c in range(nco)], 0)[None]
    convb_sample = np.concatenate([r[c]["cbs"] for c in range(nco)], 0)[None]
    hgrn_prompt = np.stack([r[c]["hgp"] for c in range(nco)], 0)[None]
    hgrn_sample = np.concatenate([r[c]["hgs"] for c in range(nco)], 0)[None]
    return tuple(np.asarray(a, np.float32) for a in (y_prompt, y_sample, conva_prompt, conva_sample, convb_prompt,
                                                     convb_sample, hgrn_prompt, hgrn_sample))
```

```python
import numpy as np
import concourse.bass as bass
import concourse.mybir as mybir
from concourse.bass_utils import run_bass_kernel_spmd

F32 = mybir.dt.float32
BF16 = mybir.dt.bfloat16
AF = mybir.ActivationFunctionType
ALU = mybir.AluOpType
AX = mybir.AxisListType

D = 1024
NPR = 2048
NS = 16
NT = NPR + NS
DFF = 2816
TILES = [(0, 512), (512, 512), (1024, 512), (1536, 512)]
STILE = (2048, 16)
R_SLOTS = 6
SLOT = 4096
PREFETCH = 3
SAME_ENG_SYNC = ('pool', 'dve', 'act')
N_LAYERS = 4
STOP_AFTER = None
SKIP_FFN = False
DG_ENG = 'pool'
SKIP_MIX = False
DEBUG_CORES = None
FFN_STOP = None
SKIP_KINDS = ()
C_STOP = None
C_TILES = None
KDMA = 8
ARENA_BYTES = 75776
NVEC = 64
R_NMIX, R_NFFN, R_NFIN, R_ACONV, R_BB1, R_BDW, R_BDWB, R_LNG, R_LNB, R_BB2, R_CLB, R_GN = 0, 4, 8, 9, 15, 17, 48, 49, 50, 51, 52, 56


class Buf:
    __slots__ = ("base", "lo", "hi", "lw", "rd", "rdd", "ov")

    def __init__(self, base, lo, hi):
        self.base, self.lo, self.hi = base, lo, hi
        self.lw = None
        self.rd = {}
        self.rdd = []
        self.ov = []


class V:
    __slots__ = ("ap", "bufs", "wt")

    def __init__(self, ap, bufs, wt=None):
        self.ap, self.bufs, self.wt = ap, bufs, wt


class Op:
    __slots__ = ("eng", "fn", "reads", "writes", "key", "dma", "deps", "tick", "sem", "val", "pre",
                 "needinc", "wreads", "wtile")


class Prog:
    def __init__(self, nc):
        self.nc = nc
        self.ops = []
        self.bufmap = {}
        self.bybase = {}
        self._bank = 0

    def buf(self, base, lo, hi):
        k = (base, lo, hi)
        b = self.bufmap.get(k)
        if b is None:
            b = Buf(base, lo, hi)
            lst = self.bybase.setdefault(base, [])
            for o in lst:
                if o.lo < hi and lo < o.hi:
                    o.ov.append(b)
                    b.ov.append(o)
            lst.append(b)
            self.bufmap[k] = b
        return b

    def add(self, eng, fn, reads, writes, dma=False, register=True):
        op = Op()
        op.eng, op.fn, op.dma = eng, fn, dma
        op.reads = [b for v in reads for b in v.bufs]
        op.writes = [b for v in writes for b in v.bufs]
        op.wreads = [v.wt for v in reads if v.wt is not None]
        op.wtile = None
        op.needinc = False
        op.tick = 0
        op.pre = 0
        op.key = float(len(self.ops))
        if register:
            self.ops.append(op)
        return op

    def bank(self):
        b = self._bank
        self._bank = (b + 1) % 8
        return b

    def mm(self, out, lhsT, rhs, start=True, stop=True, tp=None):
        kw = {} if tp is None else {"tile_position": tp}
        self.add("pe", lambda e: e.matmul(out.ap, lhsT=lhsT.ap, rhs=rhs.ap, start=start, stop=stop, **kw),
                 [lhsT, rhs], [out])

    def tr(self, out, in_, ident):
        self.add("pe", lambda e: e.transpose(out.ap, in_.ap, ident.ap), [in_, ident], [out])

    def act(self, out, in_, func, bias=None, scale=None):
        reads = [in_]
        kw = {}
        if bias is not None:
            if isinstance(bias, V):
                reads.append(bias)
                kw["bias"] = bias.ap
            else:
                kw["bias"] = float(bias)
        if scale is not None:
            if isinstance(scale, V):
                reads.append(scale)
                kw["scale"] = scale.ap
            else:
                kw["scale"] = float(scale)
        self.add("act", lambda e: e.activation(out=out.ap, in_=in_.ap, func=func, **kw), reads, [out])

    def tt(self, eng, out, in0, in1, op):
        self.add(eng, lambda e: e.tensor_tensor(out=out.ap, in0=in0.ap, in1=in1.ap, op=op), [in0, in1], [out])

    def stt(self, out, in0, scalar, in1, op0, op1):
        reads = [in0, in1]
        s = scalar
        if isinstance(scalar, V):
            reads.append(scalar)
            s = scalar.ap
        self.add("dve", lambda e: e.scalar_tensor_tensor(out=out.ap, in0=in0.ap, scalar=s, in1=in1.ap,
                                                         op0=op0, op1=op1), reads, [out])

    def ts(self, eng, out, in0, s1, op0, s2=None, op1=None):
        reads = [in0]
        a1, a2 = s1, s2
        if isinstance(s1, V):
            reads.append(s1)
            a1 = s1.ap
        if isinstance(s2, V):
            reads.append(s2)
            a2 = s2.ap
        if op1 is None:
            self.add(eng, lambda e: e.tensor_scalar(out=out.ap, in0=in0.ap, scalar1=a1, scalar2=None, op0=op0),
                     reads, [out])
        else:
            self.add(eng, lambda e: e.tensor_scalar(out=out.ap, in0=in0.ap, scalar1=a1, scalar2=a2, op0=op0,
                                                    op1=op1), reads, [out])

    def cp(self, eng, out, in_):
        if eng == "act":
            self.add("act", lambda e: e.copy(out=out.ap, in_=in_.ap), [in_], [out])
        else:
            self.add(eng, lambda e: e.tensor_copy(out=out.ap, in_=in_.ap), [in_], [out])

    def recip(self, out, in_):
        self.add("dve", lambda e: e.reciprocal(out=out.ap, in_=in_.ap), [in_], [out])

    def reduce_add(self, out, in_):
        self.add("dve", lambda e: e.tensor_reduce(out=out.ap, in_=in_.ap, axis=AX.X, op=ALU.add), [in_], [out])

    def memset(self, eng, out, val):
        self.add(eng, lambda e: e.memset(out.ap, val), [], [out])

    def asel(self, out, in_, pattern, cmp, fill, base, cm):
        self.add("pool", lambda e: e.affine_select(out=out.ap, in_=in_.ap, pattern=pattern, compare_op=cmp,
                                                   fill=fill, base=base, channel_multiplier=cm), [in_], [out])

    def dma(self, q, out_ap, in_ap, reads, writes, register=True):
        return self.add(q, lambda e: e.dma_start(out=out_ap, in_=in_ap), reads, writes, dma=True,
                        register=register)

    def finalize(self, extra_ops):
        nc = self.nc
        ops = sorted(self.ops + extra_ops, key=lambda o: o.key)
        slot_cur = {}
        for op in ops:
            deps = set()
            for b in op.reads:
                for o in [b] + b.ov:
                    if o.lw is not None:
                        deps.add(o.lw)
            for b in op.writes:
                for o in [b] + b.ov:
                    if o.lw is not None:
                        deps.add(o.lw)
                    deps.update(o.rd.values())
                    deps.update(o.rdd)
            if op.wtile is not None:
                slot_cur[op.wtile[0]] = op.wtile[1]
            for (s, i) in op.wreads:
                assert slot_cur.get(s) == i, ("weight ring hazard", s, i, slot_cur.get(s))
            for b in op.reads:
                if op.dma:
                    b.rdd.append(op)
                else:
                    b.rd[op.eng] = op
            for b in op.writes:
                b.lw = op
                b.rd = {}
                b.rdd = []
            deps.discard(op)
            keep = []
            for d in deps:
                if d.eng == op.eng and not d.dma and not op.dma:
                    if op.eng == "pe" or op.eng not in SAME_ENG_SYNC:
                        continue
                keep.append(d)
            op.deps = keep
            for d in keep:
                if not d.dma:
                    d.needinc = True
        cnt = {}
        qcnt = {}
        for op in ops:
            if op.dma:
                n = qcnt.get(op.eng, 0)
                qcnt[op.eng] = n + 1
                op.sem = (op.eng, n % KDMA)
                op.val = 16 * (n // KDMA + 1)
                op.pre = 16 * (n // KDMA)
            elif op.needinc:
                cnt[op.eng] = cnt.get(op.eng, 0) + 1
                op.tick = cnt[op.eng]
        self.sorted_ops = ops
        self.qcnt = qcnt
        return ops

    def emit(self):
        nc = self.nc
        ops = self.sorted_ops
        engs = {"pe": [], "act": [], "dve": [], "pool": [], "sp": []}
        for op in ops:
            engs[op.eng].append(op)
        import contextlib
        with contextlib.ExitStack() as es:
            esem = {k: es.enter_context(nc.semaphore("e_" + k)) for k in ("pe", "act", "dve", "pool")}
            dsem = {}
            for q in ("sp", "pool"):
                for i in range(KDMA):
                    dsem[(q, i)] = es.enter_context(nc.semaphore("d_%s%d" % (q, i)))
            block = es.enter_context(nc.Block())
            qcnt = self.qcnt

            def run(e, name):
                waited = {}
                for op in engs[name]:
                    need = {}
                    for d in op.deps:
                        if d.dma:
                            k = ("d",) + d.sem
                            s, val = dsem[d.sem], d.val
                        else:
                            k = ("e", d.eng)
                            s, val = esem[d.eng], d.tick
                        if need.get(k, (None, 0))[1] < val:
                            need[k] = (s, val)
                    if op.dma and op.pre:
                        k = ("d",) + op.sem
                        if need.get(k, (None, 0))[1] < op.pre:
                            need[k] = (dsem[op.sem], op.pre)
                    for k, (s, val) in need.items():
                        if waited.get(k, 0) < val:
                            e.wait_ge(s, val)
                            waited[k] = val
                    ins = op.fn(e)
                    if op.dma:
                        ins.then_inc(dsem[op.sem], 16)
                    elif op.needinc:
                        ins.then_inc(esem[name], 1)
                if name in ("sp", "pool"):
                    n = qcnt.get(name, 0)
                    for i in range(min(KDMA, n)):
                        tot = (n - i + KDMA - 1) // KDMA
                        e.wait_ge(dsem[(name, i)], 16 * tot)

            @block.tensor
            def _(e):
                run(e, "pe")

            @block.scalar
            def _(e):
                run(e, "act")

            @block.vector
            def _(e):
                run(e, "dve")

            @block.gpsimd
            def _(e):
                run(e, "pool")

            @block.sync
            def _(e):
                run(e, "sp")


class Mem:
    def __init__(self, P, base, ap, boff, esz):
        self.P, self.base, self.ap, self.boff, self.esz = P, base, ap, boff, esz

    def b(self, lo, hi):
        return self.P.buf(self.base, self.boff + lo * self.esz, self.boff + hi * self.esz)

    def v(self, lo, hi, p0=0, p1=128):
        return V(self.ap[p0:p1, lo:hi], [self.b(lo, hi)])

    def v3(self, lo, a, b, p0=0, p1=128):
        return V(self.ap[p0:p1, lo:lo + a * b].rearrange("p (a b) -> p a b", a=a), [self.b(lo, lo + a * b)])


class Arena:
    def __init__(self, P, ap_f32, base, nbytes):
        self.P, self.ap, self.base, self.nbytes = P, ap_f32, base, nbytes
        self.top = 0

    def alloc(self, n, dt):
        esz = 4 if dt == F32 else 2
        nb = (n * esz + 3) // 4 * 4
        off = self.top
        assert off + nb <= self.nbytes, ("arena overflow", off, nb, self.nbytes)
        self.top += nb
        ap = self.ap[:, off // 4:(off + nb) // 4]
        if dt != F32:
            ap = ap.bitcast(dt)
        return Mem(self.P, self.base, ap, off, esz)


class WStream:
    def __init__(self, P, ring):
        self.P, self.ring = P, ring
        self.n = 0
        self.first_use = []
        self.dmas = []

    def get(self, parts):
        P = self.P
        i = self.n
        self.n += 1
        slot = i % R_SLOTS
        base = slot * SLOT
        sbuf = self.ring.b(base, base + SLOT)
        self.first_use.append(len(P.ops))
        off = 0
        outs = []
        for pi, (src, a, b) in enumerate(parts):
            dst = self.ring.ap[:, base + off:base + off + a * b].rearrange("p (a b) -> p a b", a=a)
            v = V(dst, [sbuf], wt=(slot, i))
            op = P.dma("pool", dst, src, [], [V(dst, [sbuf])], register=False)
            op.wtile = (slot, i)
            op.key = (i, pi)
            self.dmas.append(op)
            outs.append(v)
            off += a * b
        assert off <= SLOT
        return outs

    def finish(self):
        for op in self.dmas:
            i, pi = op.key
            j = max(0, i - PREFETCH)
            op.key = self.first_use[j] - 0.5 + (i * 4 + pi) * 1e-6
        return self.dmas


def build_program():
    nc = bass.Bass("TRN2", target_bir_lowering=False)
    P = Prog(nc)

    def din(name, shape):
        return nc.dram_tensor(name, shape, F32, kind="ExternalInput").ap()

    def dout(name, shape):
        return nc.dram_tensor(name, shape, F32, kind="ExternalOutput").ap()

    xp_d = din("xp", [NPR, D])
    xs_d = din("xs", [NS, D])
    sta_d = din("sta", [2, 2 * NS, D])
    stb_d = din("stb", [NS * 30, D])
    sth_d = din("sth", [NS, 8, 128, 128])
    vec_d = din("vec", [NVEC, D])
    a_win = din("a_w_in", [2, D, 3 * D])
    a_wout = din("a_w_out", [2, D, D])
    b_w1 = din("b_w_pw1", [D, 2 * D])
    b_w2 = din("b_w_pw2", [D, D])
    c_wq = din("c_w_qfig", [D, 4 * D])
    c_wo = din("c_w_out", [D, D])
    f_wgu = din("ffn_w_gate_up", [4, D, 2 * DFF])
    f_wd = din("ffn_w_down", [4, DFF, D])
    yp_d = dout("yp", [NPR, D])
    ys_d = dout("ys", [NS, D])
    cap_d = dout("cap", [2, 2, D])
    cas_d = dout("cas", [2, 2 * NS, D])
    cbp_d = dout("cbp", [30, D])
    cbs_d = dout("cbs", [NS, 30, D])
    hgp_d = dout("hgp", [8, 128, 128])
    hgs_d = dout("hgs", [NS, 8, 128, 128])

    import contextlib
    es = contextlib.ExitStack()
    xt = es.enter_context(nc.sbuf_tensor("x", [128, 8, NT], F32))
    ringt = es.enter_context(nc.sbuf_tensor("ring", [128, R_SLOTS * SLOT], BF16))
    arenat = es.enter_context(nc.sbuf_tensor("arena", [128, ARENA_BYTES // 4], F32))
    CONST_F32 = 5396
    constt = es.enter_context(nc.sbuf_tensor("const", [128, CONST_F32], F32))
    pst = es.enter_context(nc.psum_tensor("ps", [128, 8, 512], F32))

    ring = Mem(P, "ring", ringt[:], 0, 2)
    W = WStream(P, ring)
    A = Arena(P, arenat[:], "arena", ARENA_BYTES)
    C = Arena(P, constt[:], "const", CONST_F32 * 4)

    def xv(c, t0, n):
        return V(xt[:, c, t0:t0 + n], [P.buf("x", (c * NT + t0) * 4, (c * NT + t0 + n) * 4)])

    def xall(t0, n, c0=0, c1=8):
        return V(xt[:, c0:c1, t0:t0 + n],
                 [P.buf("x", (c * NT + t0) * 4, (c * NT + t0 + n) * 4) for c in range(c0, c1)])

    def ps(bank, lo=0, hi=512, p0=0, p1=128):
        return V(pst[p0:p1, bank, lo:hi], [P.buf("ps", bank * 2048 + lo * 4, bank * 2048 + hi * 4)])

    ident = C.alloc(128, F32)
    maskbd = C.alloc(128, F32)
    trirev = C.alloc(128, F32)
    ones_bf = C.alloc(128, BF16)
    CF = C.alloc(8 * NVEC, F32)
    LBB = C.alloc(1024, F32)
    OMLB = C.alloc(1024, F32)
    LB = C.alloc(8, F32)
    OML = C.alloc(8, F32)
    carryA = C.alloc(16, F32)
    carryB = C.alloc(240, F32)
    S32 = C.alloc(1024, F32)
    SBF = C.alloc(2048, BF16)
    sbf_par = [0] * 8
    ind4 = C.alloc(4, F32)
    identb = C.alloc(128, BF16)

    def cf(c, r):
        return V(CF.ap[:, c * NVEC + r:c * NVEC + r + 1], [CF.b(c * NVEC, (c + 1) * NVEC)])

    iv = ident.v(0, 128)
    P.memset("pool", iv, 0.0)
    P.asel(iv, iv, [[-1, 128]], ALU.not_equal, 1.0, 0, 1)
    P.cp("dve", identb.v(0, 128), iv)
    mv = maskbd.v(0, 128)
    P.memset("pool", mv, 1.0)
    P.asel(mv, mv, [[1, 128]], ALU.is_ge, 0.0, 0, -1)
    for j in range(4):
        sv = maskbd.v(32 * j, 32 * j + 32)
        P.asel(sv, sv, [[0, 32]], ALU.is_ge, 0.0, -32 * j, 1)
    rv = trirev.v(0, 128)
    P.memset("pool", rv, 1.0)
    P.asel(rv, rv, [[-1, 128]], ALU.is_gt, 0.0, 0, 1)
    for j in range(4):
        sv = trirev.v(32 * j, 32 * j + 32)
        P.asel(sv, sv, [[0, 32]], ALU.is_gt, 0.0, 32 * j + 32, -1)
    i4 = ind4.v(0, 4)
    P.memset("pool", i4, 1.0)
    P.asel(i4, i4, [[-32, 4]], ALU.is_ge, 0.0, 0, 1)
    P.asel(i4, i4, [[32, 4]], ALU.is_ge, 0.0, 31, -1)
    P.memset("dve", ones_bf.v(0, 128), 1.0)
    P.memset("dve", carryA.v(0, 16), 0.0)
    P.memset("dve", carryB.v(0, 240), 0.0)
    P.memset("dve", S32.v(0, 1024), 0.0)
    P.memset("dve", SBF.v(0, 2048), 0.0)

    rr = {"s": 0, "o": 0}

    def done():
        P.finalize(W.finish())
        P.emit()
        es.close()
        return nc

    if STOP_AFTER == 0:
        return done()

    def in_rows(dram_rows, n, dst4):
        st = A_stage[rr["s"] % 2]
        rr["s"] += 1
        sv_ = st.v(0, 1024, 0, n)
        P.dma("sp", sv_.ap, dram_rows, [], [sv_])
        for cg in range(2):
            bk = P.bank()
            for c4 in range(4):
                c = cg * 4 + c4
                P.tr(ps(bk, c4 * 128, c4 * 128 + n), V(st.ap[0:n, c * 128:(c + 1) * 128], [st.b(0, 1024)]),
                     V(ident.ap[0:n, 0:n], [ident.b(0, 128)]))
            src = V(pst[:, bk, :].rearrange("p (a b) -> p a b", a=4)[:, :, 0:n], [P.buf("ps", bk * 2048, bk * 2048 + 2048)])
            P.cp("act" if cg == 0 else "dve", dst4(cg), src)

    def out_rows(src, n, dram_rows):
        st = A_stage[rr["s"] % 2]
        rr["s"] += 1
        if n < 128:
            pad = A.alloc(1024, F32)
            P.memset("dve", pad.v(0, 1024), 0.0)
            for c in range(8):
                P.cp("dve", pad.v(c * 128, c * 128 + n), src(c))
            src = lambda c: pad.v(c * 128, (c + 1) * 128)
        for cg in range(2):
            bk = P.bank()
            for c4 in range(4):
                c = cg * 4 + c4
                P.tr(ps(bk, c4 * 128, c4 * 128 + 128), src(c), iv)
            P.cp("act" if cg == 0 else "dve", st.v(cg * 512, cg * 512 + 512, 0, n), ps(bk, 0, 512, 0, n))
        sv_ = st.v(0, 1024, 0, n)
        P.dma("sp", dram_rows, sv_.ap, [sv_], [])

    A.top = 0
    A_stage = [A.alloc(1024, F32), A.alloc(1024, F32)]
    clb = A.alloc(4096, F32)
    tmp1 = A.alloc(1024, F32)
    st = A_stage[0]
    sv_ = st.v(0, 1024, 0, NVEC)
    P.dma("sp", sv_.ap, vec_d[:, :], [], [sv_])
    bk = P.bank()
    for c in range(8):
        P.tr(ps(bk, c * NVEC, (c + 1) * NVEC), V(st.ap[0:NVEC, c * 128:(c + 1) * 128], [st.b(0, 1024)]),
             V(ident.ap[0:NVEC, 0:NVEC], [ident.b(0, 128)]))
    P.cp("dve", CF.v(0, 8 * NVEC), ps(bk))
    rr["s"] = 1
    if STOP_AFTER == 1:
        return done()
    e4 = A.alloc(32, F32)
    ssum = A.alloc(8, F32)
    CF3 = CF.ap[:, :].rearrange("p (c r) -> p c r", c=8)
    P.act(V(e4.ap[:, 0:32].rearrange("p (c r) -> p c r", c=8), [e4.b(0, 32)]),
          V(CF3[:, :, R_CLB:R_CLB + 4], [CF.b(0, 8 * NVEC)]), AF.Exp)
    e43 = e4.ap[:, 0:32].rearrange("p (c r) -> p c r", c=8)
    P.reduce_add(ssum.v(0, 8), V(e43, [e4.b(0, 32)]))
    P.recip(ssum.v(0, 8), ssum.v(0, 8))
    P.tt("dve", LB.v(0, 8), V(e43[:, :, 1], [e4.b(0, 32)]), V(e43[:, :, 2], [e4.b(0, 32)]), ALU.add)
    P.tt("dve", LB.v(0, 8), LB.v(0, 8), ssum.v(0, 8), ALU.mult)
    P.ts("dve", OML.v(0, 8), LB.v(0, 8), -1.0, ALU.mult, 1.0, ALU.add)
    if STOP_AFTER == 2:
        return done()
    cv = clb.v(0, 4096)
    P.dma("sp", clb.ap[:, 0:4096].rearrange("p (a b) -> p a b", a=4), vec_d[R_CLB:R_CLB + 4, :].partition_broadcast(128),
          [], [cv])
    P.act(cv, cv, AF.Exp)
    P.tt("dve", LBB.v(0, 1024), clb.v(1024, 2048), clb.v(2048, 3072), ALU.add)
    P.tt("dve", tmp1.v(0, 1024), clb.v(0, 1024), clb.v(3072, 4096), ALU.add)
    P.tt("dve", tmp1.v(0, 1024), tmp1.v(0, 1024), LBB.v(0, 1024), ALU.add)
    P.recip(tmp1.v(0, 1024), tmp1.v(0, 1024))
    P.tt("dve", LBB.v(0, 1024), LBB.v(0, 1024), tmp1.v(0, 1024), ALU.mult)
    P.ts("dve", OMLB.v(0, 1024), LBB.v(0, 1024), -1.0, ALU.mult, 1.0, ALU.add)
    if STOP_AFTER == 3:
        return done()
    xq = {"next": 0}

    def load_next_block():
        i = xq["next"]
        if i < 16:
            in_rows(xp_d[i * 128:(i + 1) * 128, :], 128, lambda cg, i=i: xall(i * 128, 128, cg * 4, cg * 4 + 4))
        elif i == 16:
            in_rows(xs_d[:, :], NS, lambda cg: xall(NPR, NS, cg * 4, cg * 4 + 4))
        xq["next"] = i + 1

    lazy_x = N_LAYERS >= 1 and not SKIP_MIX and 0 not in SKIP_KINDS
    for i in range(4 if lazy_x else 17):
        load_next_block()

    if STOP_AFTER == 4:
        return done()
    def rmsnorm(t0, n, grow, hout, sq, scr, out_f32=False):
        P.act(sq.v3(0, 8, n), xall(t0, n), AF.Square)
        bk = P.bank()
        for c in range(8):
            P.mm(ps(bk, 0, n), ones_bf.v(0, 128), sq.v(c * n, (c + 1) * n), start=(c == 0), stop=(c == 7))
        rs = scr.v(0, n)
        P.act(rs, ps(bk, 0, n), AF.Ln, bias=1e-6, scale=1.0 / D)
        P.act(rs, rs, AF.Exp, scale=-0.5)
        for c in range(8):
            P.stt(hout(c), xv(c, t0, n), cf(c, grow), rs, ALU.mult, ALU.mult)

    def resid_add(o, t0, n, bk, bias=None):
        if bias is None:
            P.tt("dve", xv(o, t0, n), xv(o, t0, n), ps(bk, 0, n), ALU.add)
        else:
            P.stt(xv(o, t0, n), ps(bk, 0, n), bias, xv(o, t0, n), ALU.add, ALU.add)

    def out_proj(wd, zmem, t0, n, bias_row=None):
        for half in range(2):
            (wt,) = W.get([(wd[:, half * 512:(half + 1) * 512].rearrange("(k p) n -> p k n", p=128), 8, 512)])
            for o4 in range(4):
                o = half * 4 + o4
                bk = P.bank()
                for k in range(8):
                    P.mm(ps(bk, 0, n), V(wt.ap[:, k, o4 * 128:(o4 + 1) * 128], wt.bufs, wt.wt),
                         zmem.v(k * n, (k + 1) * n), start=(k == 0), stop=(k == 7))
                resid_add(o, t0, n, bk, None if bias_row is None else cf(o, bias_row))

    def wsub(wt, k, lo, hi):
        return V(wt.ap[:, k, lo:hi], wt.bufs, wt.wt)

    def proj(bk, wt, hmem, n, col0=0, ncol=128):
        for k in range(8):
            P.mm(ps(bk, 0, n), wsub(wt, k, col0, col0 + ncol), hmem.v(k * n, (k + 1) * n), start=(k == 0), stop=(k == 7))

    def mixer_a(layer, j, tiles):
        A.top = 0
        stg = [A.alloc(1024, F32), A.alloc(1024, F32)]
        A_stage[0], A_stage[1] = stg
        if any(sm_ for (_, _, sm_) in tiles):
            while xq["next"] <= 16:
                load_next_block()
        ctxs = []
        for (t0, n, sample) in tiles:
            cx = {"t0": t0, "n": n, "sample": sample}
            cx["h"] = A.alloc(8 * n, BF16)
            cx["sq"] = A.alloc(8 * n, BF16)
            cx["z"] = A.alloc(8 * n, BF16)
            cx["scr"] = [A.alloc(n, F32) for _ in range(4)]
            cx["ub"] = [A.alloc(n + 4, F32) for _ in range(2)]
            if sample:
                cx["sta"] = A.alloc(8 * 32, F32)
                cx["newa"] = A.alloc(8 * 32, F32)
                sta_ = cx["sta"]
                in_rows(sta_d[j, :, :], 32, lambda cg, sta_=sta_: sta_.v3(cg * 128, 4, 32))
            h_ = cx["h"]
            rmsnorm(t0, n, R_NMIX + layer, lambda c, h_=h_, n=n: h_.v(c * n, (c + 1) * n), cx["sq"], cx["scr"][0])
            ctxs.append(cx)
        wi = a_win[j]

        def chunk_body(cx, c, wts):
            t0, n, sample = cx["t0"], cx["n"], cx["sample"]
            h, z, scr, ub = cx["h"], cx["z"], cx["scr"], cx["ub"]
            cc0 = (c % 4) * 128
            bb, bc, bh = P.bank(), P.bank(), P.bank()
            proj(bb, wts[0], h, n, cc0)
            proj(bc, wts[1], h, n, cc0)
            proj(bh, wts[2], h, n, cc0)
            cgs = scr[1 + c % 2].v(0, n)
            tmp = scr[3].v(0, n)
            P.cp("act", cgs, ps(bc, 0, n))
            w0, w1, w2 = (cf(c, R_ACONV + 3 * j + r) for r in range(3))
            if not sample:
                u = ub[c % 2]
                P.cp("dve", u.v(0, 2), carryA.v(c * 2, c * 2 + 2))
                P.tt("dve", u.v(2, 2 + n), cgs, ps(bh, 0, n), ALU.mult)
                P.cp("dve", carryA.v(c * 2, c * 2 + 2), u.v(n, n + 2))
                P.ts("dve", tmp, u.v(0, n), w0, ALU.mult)
                P.stt(tmp, u.v(1, n + 1), w1, tmp, ALU.mult, ALU.add)
                P.stt(tmp, u.v(2, n + 2), w2, tmp, ALU.mult, ALU.add)
            else:
                sta, newa = cx["sta"], cx["newa"]
                u = ub[c % 2].v(0, n)
                P.tt("dve", u, cgs, ps(bh, 0, n), ALU.mult)
                s3 = sta.ap[:, c * 32:(c + 1) * 32].rearrange("p (s r) -> p s r", r=2)
                n3 = newa.ap[:, c * 32:(c + 1) * 32].rearrange("p (s r) -> p s r", r=2)
                sb_, nb_ = [sta.b(c * 32, c * 32 + 32)], [newa.b(c * 32, c * 32 + 32)]
                P.ts("dve", tmp, V(s3[:, :, 0], sb_), w0, ALU.mult)
                P.stt(tmp, V(s3[:, :, 1], sb_), w1, tmp, ALU.mult, ALU.add)
                P.stt(tmp, u, w2, tmp, ALU.mult, ALU.add)
                P.cp("dve", V(n3[:, :, 0], nb_), V(s3[:, :, 1], sb_))
                P.cp("dve", V(n3[:, :, 1], nb_), u)
            P.tt("dve", z.v(c * n, (c + 1) * n), tmp, ps(bb, 0, n), ALU.mult)

        for c in range(8):
            if c % 4 == 0:
                cq = c // 4
                wts = [W.get([(wi[:, g * 1024 + cq * 512:g * 1024 + (cq + 1) * 512].rearrange("(k p) n -> p k n", p=128), 8, 512)])[0]
                       for g in range(3)]
            for cx in ctxs:
                chunk_body(cx, c, wts)
            if layer == 0 and c % 2 == 1 and xq["next"] <= 16:
                load_next_block()
        for half in range(2):
            (wt,) = W.get([(a_wout[j][:, half * 512:(half + 1) * 512].rearrange("(k p) n -> p k n", p=128), 8, 512)])
            for o4 in range(4):
                o = half * 4 + o4
                for cx in ctxs:
                    n = cx["n"]
                    bk = P.bank()
                    for k in range(8):
                        P.mm(ps(bk, 0, n), V(wt.ap[:, k, o4 * 128:(o4 + 1) * 128], wt.bufs, wt.wt),
                             cx["z"].v(k * n, (k + 1) * n), start=(k == 0), stop=(k == 7))
                    resid_add(o, cx["t0"], n, bk)
        for cx in ctxs:
            if cx["sample"]:
                newa = cx["newa"]
                out_rows(lambda c, newa=newa: newa.v(c * 32, c * 32 + 32), 32, cas_d[j, :, :])

    def mixer_b(layer, t0, n, sample):
        A.top = 0
        if sample:
            stg = [A.alloc(1024, F32), A.alloc(1024, F32)]
            A_stage[0], A_stage[1] = stg
        h = A.alloc(8 * n, BF16)
        sq = A.alloc(8 * n, BF16)
        ybf = A.alloc(8 * n, BF16)
        z = h
        y = A.alloc(8 * n, F32)
        scr = [A.alloc(512, F32) for _ in range(5)]
        ub = [A.alloc(544, F32) for _ in range(2)]
        if not sample:
            ubfb = [A.alloc(544, BF16) for _ in range(2)]
            dgb = [A.alloc(31 * 128, BF16) for _ in range(2)]
        if sample:
            stb = A.alloc(8 * 480, F32)
            us = A.alloc(8 * NS, F32)
            prod = A.alloc(480, F32)
            for q in range(4):
                in_rows(stb_d[q * 120:(q + 1) * 120, :], 120,
                        lambda cg, q=q: V(stb.ap[:, :].rearrange("p (c m) -> p c m", c=8)[:, cg * 4:cg * 4 + 4, q * 120:(q + 1) * 120],
                                          [stb.b(c * 480 + q * 120, c * 480 + (q + 1) * 120) for c in range(cg * 4, cg * 4 + 4)]))
            P.dma("sp", cbs_d[:, 0:29, :], stb_d[:, :].rearrange("(s r) d -> s r d", r=30)[:, 1:30, :], [], [])
        rmsnorm(t0, n, R_NMIX + layer, lambda c: h.v(c * n, (c + 1) * n), sq, scr[0])
        wst = {}

        def s1(c):
            if c % 4 == 0:
                cq = c // 4
                wst["w"] = [W.get([(b_w1[:, g * 1024 + cq * 512:g * 1024 + (cq + 1) * 512].rearrange("(k p) n -> p k n", p=128), 8, 512)])[0]
                            for g in range(2)]
            wts = wst["w"]
            cc0 = (c % 4) * 128
            ba, bg = P.bank(), P.bank()
            proj(ba, wts[0], h, n, cc0)
            proj(bg, wts[1], h, n, cc0)
            sig = scr[1 + c % 2].v(0, n)
            P.act(sig, ps(bg, 0, n), AF.Sigmoid, bias=cf(c, R_BB1 + 1))
            yc = y.v(c * n, (c + 1) * n)
            if not sample:
                ubf = ubfb[c % 2]
                P.cp("dve", ubf.v(0, 30), carryB.v(c * 30, c * 30 + 30))
                P.stt(ubf.v(30, 30 + n), ps(ba, 0, n), cf(c, R_BB1), sig, ALU.add, ALU.mult)
                P.stt(carryB.v(c * 30, c * 30 + 30), ps(ba, n - 30, n), cf(c, R_BB1),
                      V(sig.ap[:, n - 30:n], sig.bufs), ALU.add, ALU.mult)
                dg = dgb[c % 2]
                P.tt(DG_ENG, dg.v3(0, 31, 128),
                     V(identb.ap[:, 0:128].unsqueeze(1).to_broadcast([128, 31, 128]), [identb.b(0, 128)]),
                     V(CF.ap[:, c * NVEC + R_BDW:c * NVEC + R_BDW + 31].unsqueeze(2).to_broadcast([128, 31, 128]),
                       [CF.b(c * NVEC, (c + 1) * NVEC)]), ALU.mult)
            else:
                u = us.v(c * NS, (c + 1) * NS)
                P.stt(u, ps(ba, 0, n), cf(c, R_BB1), sig, ALU.add, ALU.mult)
                wv = V(CF.ap[:, c * NVEC + R_BDW:c * NVEC + R_BDW + 30].unsqueeze(1).to_broadcast([128, NS, 30]),
                       [CF.b(c * NVEC, (c + 1) * NVEC)])
                P.tt("dve", prod.v3(0, NS, 30), stb.v3(c * 480, NS, 30), wv, ALU.mult)
                red = scr[3].v(0, NS)
                P.reduce_add(red, prod.v3(0, NS, 30))
                P.stt(yc, u, cf(c, R_BDW + 30), red, ALU.mult, ALU.add)
                P.ts("dve", yc, yc, cf(c, R_BDWB), ALU.add)

        def s2(c):
            yc = y.v(c * n, (c + 1) * n)
            ubf, dg = ubfb[c % 2], dgb[c % 2]
            by = P.bank()
            for jj in range(31):
                P.mm(ps(by, 0, n), dg.v(jj * 128, (jj + 1) * 128), ubf.v(jj, jj + n), start=(jj == 0), stop=(jj == 30))
            P.act(yc, ps(by, 0, n), AF.Identity, bias=cf(c, R_BDWB))

        if sample:
            for c in range(8):
                s1(c)
        else:
            s1(0)
            for c in range(8):
                if c + 1 < 8:
                    s1(c + 1)
                s2(c)
        P.act(sq.v3(0, 8, n), y.v3(0, 8, n), AF.Square)
        P.cp("act", ybf.v3(0, 8, n), y.v3(0, 8, n))
        b1, b2 = P.bank(), P.bank()
        for c in range(8):
            P.mm(ps(b1, 0, n), ones_bf.v(0, 128), ybf.v(c * n, (c + 1) * n), start=(c == 0), stop=(c == 7))
        for c in range(8):
            P.mm(ps(b2, 0, n), ones_bf.v(0, 128), sq.v(c * n, (c + 1) * n), start=(c == 0), stop=(c == 7))
        mean, msq, var = scr[0].v(0, n), scr[3].v(0, n), scr[4].v(0, n)
        P.ts("dve", mean, ps(b1, 0, n), 1.0 / D, ALU.mult)
        P.tt("dve", msq, mean, mean, ALU.mult)
        P.stt(var, ps(b2, 0, n), 1.0 / D, msq, ALU.mult, ALU.subtract)
        P.act(var, var, AF.Ln, bias=1e-5)
        P.act(var, var, AF.Exp, scale=-0.5)
        for c in range(8):
            yc = y.v(c * n, (c + 1) * n)
            P.tt("dve", yc, yc, mean, ALU.subtract)
            P.tt("dve", yc, yc, var, ALU.mult)
            P.act(z.v(c * n, (c + 1) * n), yc, AF.Silu, bias=cf(c, R_LNB), scale=cf(c, R_LNG))
        out_proj(b_w2, z, t0, n, bias_row=R_BB2)
        if sample:
            out_rows(lambda c: us.v(c * NS, (c + 1) * NS), NS, cbs_d[:, 29, :])

    def mixer_c(layer, t0, n, sample):
        A.top = 0
        h = A.alloc(8 * n, BF16)
        z = A.alloc(8 * n, BF16)
        sq = z
        nscr = 6
        scrp = [A.alloc(512, F32) for _ in range(nscr)]
        sc = {"i": 0}

        def scratch():
            m = scrp[sc["i"] % nscr]
            sc["i"] += 1
            return m

        rmsnorm(t0, n, R_NMIX + layer, lambda c: h.v(c * n, (c + 1) * n), sq, scratch())
        if C_STOP == 0:
            return
        osq = A.alloc(512, BF16)
        qt = A.alloc(512, BF16)
        kt = A.alloc(512, BF16)
        if not sample:
            nst = n // 128
            logf = A.alloc(nst * 1024, F32)
            khat = A.alloc(nst * 1024, BF16)
            vtok = A.alloc(nst * 1024, BF16)
            scm = [A.alloc(128, BF16) for _ in range(4)]
            vmb = [A.alloc(512, BF16) for _ in range(2)]
            epd = [A.alloc(512, F32) for _ in range(2)]
            qtd = [qt, A.alloc(512, BF16)]
            ktd = [kt, A.alloc(512, BF16)]
            its = [(hg, st_) for hg in range(2) for st_ in range(nst)]
            prs = [its[i:i + 2] for i in range(0, len(its), 2)]
            wtok = {}
            tst = {}

            def tbanks(p, i):
                return (p % 2) * 4 + 2 * i, (p % 2) * 4 + 2 * i + 1

            def PA(p):
                for i, (hg, st_) in enumerate(prs[p]):
                    if hg not in wtok:
                        (wf_,) = W.get([(c_wq[:, 1024 + hg * 512:1024 + (hg + 1) * 512].rearrange("(k p) n -> p k n", p=128), 8, 512)])
                        (wi_,) = W.get([(c_wq[:, 2048 + hg * 512:2048 + (hg + 1) * 512].rearrange("(k p) n -> p k n", p=128), 8, 512)])
                        wtok[hg] = (wf_, wi_)
                    wf_, wi_ = wtok[hg]
                    bf_, bi = tbanks(p, i)
                    for k in range(8):
                        P.mm(ps(bf_), h.v(k * n + st_ * 128, k * n + (st_ + 1) * 128), wsub(wf_, k, 0, 512),
                             start=(k == 0), stop=(k == 7))
                    for k in range(8):
                        P.mm(ps(bi), h.v(k * n + st_ * 128, k * n + (st_ + 1) * 128), wsub(wi_, k, 0, 512),
                             start=(k == 0), stop=(k == 7))

            def SA(p):
                for i, (hg, st_) in enumerate(prs[p]):
                    bf_, bi = tbanks(p, i)
                    lo = st_ * 1024 + hg * 512
                    s1 = scratch().v(0, 512)
                    tst[(p, i)] = [s1]
                    P.act(s1, ps(bf_), AF.Sigmoid)
                    P.cp("act", vtok.v(lo, lo + 512), ps(bi))
                    P.tt("dve", s1, s1, OMLB.v(hg * 512, (hg + 1) * 512), ALU.mult)
                    P.tt("dve", s1, s1, LBB.v(hg * 512, (hg + 1) * 512), ALU.add)

            def SB(p):
                for i, (hg, st_) in enumerate(prs[p]):
                    lo = st_ * 1024 + hg * 512
                    P.act(logf.v(lo, lo + 512), tst[(p, i)][0], AF.Ln)
                for i, (hg, st_) in enumerate(prs[p]):
                    bf_, bi = tbanks(p, i)
                    lo = st_ * 1024 + hg * 512
                    s2 = scratch().v(0, 512)
                    tst[(p, i)].append(s2)
                    P.ts("dve", s2, tst[(p, i)][0], -1.0, ALU.mult, 1.0, ALU.add)
                    P.mm(ps(bf_), trirev.v(0, 128), logf.v(lo, lo + 512))

            def SC(p):
                for i, (hg, st_) in enumerate(prs[p]):
                    bf_, bi = tbanks(p, i)
                    lo = st_ * 1024 + hg * 512
                    s3 = scratch().v(0, 512)
                    P.act(s3, ps(bf_), AF.Exp)
                    P.tt("dve", khat.v(lo, lo + 512), tst[(p, i)][1], s3, ALU.mult)

            PA(0)
            for p in range(len(prs)):
                if p + 1 < len(prs):
                    PA(p + 1)
                SA(p)
                SB(p)
                SC(p)
        else:
            ktok = A.alloc(1024, F32)
            vtk = A.alloc(1024, F32)
            vd = A.alloc(NS * 128, F32)
            dm = A.alloc(NS * 128, F32)
            s0b = [A.alloc(1024, F32) for _ in range(4)]
            snb = [A.alloc(1024, F32) for _ in range(4)]

            def load_s0(hd_):
                for half_ in range(2):
                    s0v_ = s0b[(hd_ % 2) * 2 + half_].v3(0, 8, 128)
                    P.dma("sp", s0v_.ap, sth_d[half_ * 8:(half_ + 1) * 8, hd_, :, :].rearrange("s k v -> k s v"), [], [s0v_])

            load_s0(0)
            qs = A.alloc(NS, F32)
            fgf = A.alloc(NS, F32)
            dmv = dm.v3(0, NS, 128, 0, NS)
            P.memset("pool", dmv, 1.0)
            P.asel(dmv, dmv, [[1, NS], [0, 128]], ALU.is_equal, 0.0, 0, -1)
            for hg in range(2):
                (wf,) = W.get([(c_wq[:, 1024 + hg * 512:1024 + (hg + 1) * 512].rearrange("(k p) n -> p k n", p=128), 8, 512)])
                (wi,) = W.get([(c_wq[:, 2048 + hg * 512:2048 + (hg + 1) * 512].rearrange("(k p) n -> p k n", p=128), 8, 512)])
                bf_, bi = P.bank(), P.bank()
                for k in range(8):
                    P.mm(ps(bf_, 0, 512, 0, NS), h.v(k * n, k * n + NS), wsub(wf, k, 0, 512), start=(k == 0), stop=(k == 7))
                for k in range(8):
                    P.mm(ps(bi, 0, 512, 0, NS), h.v(k * n, k * n + NS), wsub(wi, k, 0, 512), start=(k == 0), stop=(k == 7))
                s1 = scratch().v(0, 512, 0, NS)
                P.act(s1, ps(bf_, 0, 512, 0, NS), AF.Sigmoid, scale=-1.0)
                P.tt("dve", ktok.v(hg * 512, (hg + 1) * 512, 0, NS), s1, OMLB.v(hg * 512, (hg + 1) * 512, 0, NS), ALU.mult)
                P.cp("act", vtk.v(hg * 512, (hg + 1) * 512, 0, NS), ps(bi, 0, 512, 0, NS))
        if C_STOP == 1:
            return
        if sample:
            NH = 8 * NS
            for hd in range(8):
                if hd % 4 == 0:
                    hq = hd // 4
                    wts = [W.get([(c_wq[:, g * 1024 + hq * 512:g * 1024 + (hq + 1) * 512].rearrange("(k p) n -> p k n", p=128), 8, 512)])[0]
                           for g in (0, 1, 3)]
                cc0 = (hd % 4) * 128
                for gi in range(3):
                    for k in range(8):
                        P.mm(ps(gi, hd * NS, (hd + 1) * NS), wsub(wts[gi], k, cc0, cc0 + 128), h.v(k * n, k * n + NS),
                             start=(k == 0), stop=(k == 7))
            qs_all = A.alloc(NH, F32)
            sg_all = A.alloc(NH, F32)
            fg_all = A.alloc(NH, F32)
            P.act(qs_all.v(0, NH), ps(0, 0, NH), AF.Silu)
            P.act(sg_all.v(0, NH), ps(2, 0, NH), AF.Silu)
            P.act(fg_all.v(0, NH), ps(1, 0, NH), AF.Sigmoid, scale=-1.0)
            P.ts("dve", qs_all.v(0, NH), qs_all.v(0, NH), 128.0 ** -0.5, ALU.mult)
            P.tt("dve", fg_all.v3(0, 8, NS), fg_all.v3(0, 8, NS),
                 V(OML.ap[:, 0:8].unsqueeze(2).to_broadcast([128, 8, NS]), [OML.b(0, 8)]), ALU.mult)
            P.ts("dve", fg_all.v(0, NH), fg_all.v(0, NH), -1.0, ALU.mult, 1.0, ALU.add)
            BO = 4
            for hd in range(8):
                if hd + 1 < 8:
                    load_s0(hd + 1)
                vb = V(vtk.ap[0:NS, hd * 128:(hd + 1) * 128].unsqueeze(1).to_broadcast([NS, NS, 128]),
                       [vtk.b(hd * 128, (hd + 1) * 128)])
                P.tt("dve", vd.v3(0, NS, 128, 0, NS), vb, dmv, ALU.mult)
                for half in range(2):
                    s0 = s0b[(hd % 2) * 2 + half]
                    sn = snb[(hd % 2) * 2 + half]
                    for i2 in range(2):
                        bkv = 6 + i2
                        P.mm(ps(bkv), ktok.v(hd * 128, (hd + 1) * 128, 0, NS),
                             vd.v((half * 8 + i2 * 4) * 128, (half * 8 + i2 * 4 + 4) * 128, 0, NS))
                    for jj in range(8):
                        col = hd * NS + half * 8 + jj
                        P.stt(sn.v(jj * 128, (jj + 1) * 128), s0.v(jj * 128, (jj + 1) * 128), fg_all.v(col, col + 1),
                              ps(6 + jj // 4, (jj % 4) * 128, (jj % 4 + 1) * 128), ALU.mult, ALU.add)
                    snv = sn.v3(0, 8, 128)
                    P.dma("sp", hgs_d[half * 8:(half + 1) * 8, hd, :, :].rearrange("s k v -> k s v"), snv.ap, [snv], [])
                    for jj in range(8):
                        col = hd * NS + half * 8 + jj
                        P.mm(ps(BO, col, col + 1), sn.v(jj * 128, (jj + 1) * 128), qs_all.v(col, col + 1))
            osq_all = A.alloc(NH, BF16)
            sd_all = A.alloc(NH, F32)
            P.act(osq_all.v(0, NH), ps(BO, 0, NH), AF.Square)
            P.mm(ps(5, 0, NH), ones_bf.v(0, 128), osq_all.v(0, NH))
            P.act(sd_all.v(0, NH), ps(5, 0, NH), AF.Ln, bias=1e-6, scale=1.0 / 128)
            P.act(sd_all.v(0, NH), sd_all.v(0, NH), AF.Exp, scale=-0.5)
            P.stt(sd_all.v(0, NH), ps(BO, 0, NH), cf(0, R_GN), sd_all.v(0, NH), ALU.mult, ALU.mult)
            P.tt("dve", z.v(0, NH), sd_all.v(0, NH), sg_all.v(0, NH), ALU.mult)

        def prep_pair(hds, wts):
            sqv, sgv, env = [], [], []
            for i, hd in enumerate(hds):
                cc0 = (hd % 4) * 128
                proj(2 * i, wts[0], h, n, cc0)
                proj(2 * i + 1, wts[1], h, n, cc0)
                for st_ in range(nst):
                    lo = st_ * 1024 + hd * 128
                    P.mm(ps(6 + i, st_ * 128, (st_ + 1) * 128), logf.v(lo, lo + 128), maskbd.v(0, 128))
            for i, hd in enumerate(hds):
                sqv.append(scratch().v(0, n))
                P.act(sqv[i], ps(2 * i, 0, n), AF.Silu)
            for i, hd in enumerate(hds):
                sgv.append(scratch().v(0, n))
                P.act(sgv[i], ps(2 * i + 1, 0, n), AF.Sigmoid, scale=-1.0)
            for i, hd in enumerate(hds):
                env.append(scratch().v(0, n))
                P.act(epd[i].v(0, n), ps(6 + i, 0, n), AF.Exp)
                P.act(env[i], ps(6 + i, 0, n), AF.Exp, scale=-1.0)
            for i, hd in enumerate(hds):
                omlv = V(OML.ap[:, hd:hd + 1], [OML.b(0, 8)])
                P.stt(qtd[i].v(0, n), sqv[i], 128.0 ** -0.5, epd[i].v(0, n), ALU.mult, ALU.mult)
                P.stt(ktd[i].v(0, n), sgv[i], omlv, env[i], ALU.mult, ALU.mult)

        def chain(hd, BO_, BKV_, epm, qt, kt, scm2, vm):
            for st_ in range(nst):
                c0 = st_ * 128
                lo = st_ * 1024 + hd * 128
                P.mm(ps(BS, 0, 128), kt.v(c0, c0 + 128), qt.v(c0, c0 + 128))
                sm = scm2[st_ % 2].v(0, 128)
                P.tt("dve", sm, ps(BS, 0, 128), maskbd.v(0, 128), ALU.mult)
                P.mm(ps(BO_, c0, c0 + 128), vtok.v(lo, lo + 128), sm, start=True, stop=False)
                P.tt("dve", vm.v3(0, 4, 128),
                     V(vtok.ap[:, lo:lo + 128].unsqueeze(1).to_broadcast([128, 4, 128]), [vtok.b(lo, lo + 128)]),
                     V(ind4.ap[:, 0:4].unsqueeze(2).to_broadcast([128, 4, 128]), [ind4.b(0, 4)]), ALU.mult)
                P.mm(ps(BKV_, 0, 512), khat.v(lo, lo + 128), vm.v(0, 512))
                yield
                for j in range(4):
                    par = sbf_par[hd]
                    so = hd * 256 + par * 128
                    P.mm(ps(BO_, c0 + 32 * j, c0 + 32 * j + 32), SBF.v(so, so + 128),
                         qt.v(c0 + 32 * j, c0 + 32 * j + 32), start=False, stop=(j == 3))
                    col = c0 + 32 * j + 31
                    s32v = S32.v(hd * 128, (hd + 1) * 128)
                    P.stt(s32v, s32v, epm.v(col, col + 1), ps(BKV_, j * 128, (j + 1) * 128), ALU.mult, ALU.add)
                    par ^= 1
                    sbf_par[hd] = par
                    so = hd * 256 + par * 128
                    P.cp("act", SBF.v(so, so + 128), s32v)
                    yield

        def norm_pair(hds, wts):
            sdv, sgv = [], []
            for i, hd in enumerate(hds):
                proj(2 * i, wts[2], h, n, (hd % 4) * 128)
                P.act(osqd[i].v(0, n), ps(4 + i, 0, n), AF.Square)
                P.mm(ps(2 * i + 1, 0, n), ones_bf.v(0, 128), osqd[i].v(0, n))
            for i, hd in enumerate(hds):
                sdv.append(scratch().v(0, n))
                P.act(sdv[i], ps(2 * i + 1, 0, n), AF.Ln, bias=1e-6, scale=1.0 / 128)
            for i, hd in enumerate(hds):
                P.act(sdv[i], sdv[i], AF.Exp, scale=-0.5)
            for i, hd in enumerate(hds):
                sgv.append(scratch().v(0, n))
                P.act(sgv[i], ps(2 * i, 0, n), AF.Silu)
            for i, hd in enumerate(hds):
                t1 = scratch().v(0, n)
                P.stt(t1, ps(4 + i, 0, n), cf(0, R_GN), sdv[i], ALU.mult, ALU.mult)
                P.tt("dve", z.v(hd * n, (hd + 1) * n), t1, sgv[i], ALU.mult)

        if not sample:
            BS = 3
            osqd = [osq, A.alloc(512, BF16)]
            for pair in range(4):
                if pair % 2 == 0:
                    hq = pair // 2
                    wts = [W.get([(c_wq[:, g * 1024 + hq * 512:g * 1024 + (hq + 1) * 512].rearrange("(k p) n -> p k n", p=128), 8, 512)])[0]
                           for g in (0, 1, 3)]
                hds = (2 * pair, 2 * pair + 1)
                prep_pair(hds, wts)
                gens = [chain(hds[i], 4 + i, 6 + i, epd[i], qtd[i], ktd[i], scm[2 * i:2 * i + 2], vmb[i])
                        for i in range(2)]
                alive = list(gens)
                while alive:
                    for g in list(alive):
                        try:
                            next(g)
                        except StopIteration:
                            alive.remove(g)
                norm_pair(hds, wts)
        out_proj(c_wo, z, t0, n)

    def ffn(layer):
        A.top = 0
        h = A.alloc(8 * NT, BF16)
        actb = A.alloc(6 * NT, BF16)
        sq = A.alloc(8 * 512, BF16)
        scr = [A.alloc(512, F32) for _ in range(4)]
        alltiles = TILES + [STILE]
        for (t0, n) in alltiles:
            rmsnorm(t0, n, R_NFFN + layer,
                    lambda c, t0=t0, n=n: V(h.ap[:, c * NT + t0:c * NT + t0 + n], [h.b(c * NT + t0, c * NT + t0 + n)]),
                    sq, scr[0])
        wgu = f_wgu[layer]
        wdn = f_wd[layer]
        si = 0
        if FFN_STOP == 0:
            return
        for gi, (g0, gs) in enumerate([(0, 6), (6, 6), (12, 5), (17, 5)]):
            cc = g0
            while cc < g0 + gs:
                nch = min(4, g0 + gs - cc)
                (wg,) = W.get([(wgu[:, cc * 128:(cc + nch) * 128].rearrange("(k p) n -> p k n", p=128), 8, nch * 128)])
                (wu,) = W.get([(wgu[:, DFF + cc * 128:DFF + (cc + nch) * 128].rearrange("(k p) n -> p k n", p=128), 8, nch * 128)])
                order = [(ci, tl) for ci in range(nch) for tl in alltiles]
                if cc == 0:
                    order = [(ci, tl) for tl in alltiles for ci in range(nch)]
                for ci, (t0, n) in order:
                    a_i = cc + ci - g0
                    if True:
                        bg, bu = P.bank(), P.bank()
                        for k in range(8):
                            P.mm(ps(bg, 0, n), wsub(wg, k, ci * 128, (ci + 1) * 128),
                                 h.v(k * NT + t0, k * NT + t0 + n), start=(k == 0), stop=(k == 7))
                        for k in range(8):
                            P.mm(ps(bu, 0, n), wsub(wu, k, ci * 128, (ci + 1) * 128),
                                 h.v(k * NT + t0, k * NT + t0 + n), start=(k == 0), stop=(k == 7))
                        sg = scr[1 + si % 3].v(0, n)
                        si += 1
                        P.act(sg, ps(bg, 0, n), AF.Silu)
                        P.tt("dve", actb.v(a_i * NT + t0, a_i * NT + t0 + n), sg, ps(bu, 0, n), ALU.mult)
                cc += nch
            if FFN_STOP == 1:
                return
            if layer == N_LAYERS - 1 and gi == 3 and FFN_STOP is None:
                wt2 = [W.get([(wdn[g0 * 128:(g0 + gs) * 128, hf * 512:(hf + 1) * 512].rearrange("(j p) n -> p j n", p=128), gs, 512)])[0]
                       for hf in range(2)]
                keep_top = A.top
                A.top = 0
                A_stage[0], A_stage[1] = A.alloc(1024, F32), A.alloc(1024, F32)
                yfin = A.alloc(8 * 512, F32)
                def dproj(t0, n):
                    for o in range(8):
                        bk = P.bank()
                        for jx in range(gs):
                            P.mm(ps(bk, 0, n), wsub(wt2[o // 4], jx, (o % 4) * 128, (o % 4 + 1) * 128),
                                 actb.v(jx * NT + t0, jx * NT + t0 + n), start=(jx == 0), stop=(jx == gs - 1))
                        resid_add(o, t0, n, bk)

                def fin_tile(t0, n):
                    rmsnorm(t0, n, R_NFIN, lambda c, n=n: yfin.v(c * 512, c * 512 + n), sq, scr[0])
                    A.top = 24576
                    if n == 512:
                        for q in range(4):
                            i = t0 // 128 + q
                            out_rows(lambda c, q=q: yfin.v(c * 512 + q * 128, c * 512 + (q + 1) * 128), 128,
                                     yp_d[i * 128:(i + 1) * 128, :])
                    else:
                        out_rows(lambda c: yfin.v(c * 512, c * 512 + NS), NS, ys_d[:, :])

                dproj(*alltiles[0])
                for ti_ in range(len(alltiles)):
                    if ti_ + 1 < len(alltiles):
                        dproj(*alltiles[ti_ + 1])
                    fin_tile(*alltiles[ti_])
                A.top = keep_top
                fin["done"] = True
                continue
            for half in range(2):
                (wt,) = W.get([(wdn[g0 * 128:(g0 + gs) * 128, half * 512:(half + 1) * 512].rearrange("(j p) n -> p j n", p=128), gs, 512)])
                for o4 in range(4):
                    o = half * 4 + o4
                    for (t0, n) in alltiles:
                        bk = P.bank()
                        for jx in range(gs):
                            P.mm(ps(bk, 0, n), wsub(wt, jx, o4 * 128, (o4 + 1) * 128),
                                 actb.v(jx * NT + t0, jx * NT + t0 + n), start=(jx == 0), stop=(jx == gs - 1))
                        resid_add(o, t0, n, bk)
            if FFN_STOP is not None and FFN_STOP >= 2 and gi == FFN_STOP - 2:
                return

    fin = {"done": False}
    for layer in range(N_LAYERS):
        kind, j = layer % 3, layer // 3
        if kind == 0:
            P.memset("dve", carryA.v(0, 16), 0.0)
        if kind == 0 and not (SKIP_MIX or kind in SKIP_KINDS):
            for (t0, n) in TILES[:3]:
                mixer_a(layer, j, [(t0, n, False)])
            mixer_a(layer, j, [(TILES[3][0], TILES[3][1], False), (STILE[0], STILE[1], True)])
        for (t0, n) in TILES + [STILE]:
            sample = (t0 == NPR)
            if SKIP_MIX or kind in SKIP_KINDS or kind == 0:
                continue
            if kind == 2 and C_TILES == 'p0' and t0 != 0:
                continue
            if kind == 2 and C_TILES == 's' and not sample:
                continue
            if kind == 0:
                mixer_a(layer, j, t0, n, sample)
            elif kind == 1:
                mixer_b(layer, t0, n, sample)
            else:
                mixer_c(layer, t0, n, sample)
        A.top = 0
        A_stage[0], A_stage[1] = A.alloc(1024, F32), A.alloc(1024, F32)
        if kind == 0:
            out_rows(lambda c: carryA.v(c * 2, c * 2 + 2), 2, cap_d[j, :, :])
        elif kind == 1:
            out_rows(lambda c: carryB.v(c * 30, c * 30 + 30), 30, cbp_d[:, :])
        else:
            sv = S32.v3(0, 8, 128)
            P.dma("sp", hgp_d[:, :, :].rearrange("h k v -> k h v"), sv.ap, [sv], [])
        if not SKIP_FFN:
            ffn(layer)

    if fin["done"]:
        return done()
    A.top = 0
    A_stage[0], A_stage[1] = A.alloc(1024, F32), A.alloc(1024, F32)
    sq = A.alloc(8 * 512, BF16)
    scr0 = A.alloc(512, F32)
    yf = [A.alloc(8 * 512, F32) for _ in range(2)]
    fi = 0
    for T in range(4):
        yb = yf[fi % 2]
        fi += 1
        rmsnorm(T * 512, 512, R_NFIN, lambda c, yb=yb: yb.v(c * 512, (c + 1) * 512), sq, scr0)
        for q in range(4):
            i = T * 4 + q
            out_rows(lambda c, yb=yb, q=q: yb.v(c * 512 + q * 128, c * 512 + (q + 1) * 128), 128,
                     yp_d[i * 128:(i + 1) * 128, :])
    if STOP_AFTER == 8:
        return done()
    yb = yf[fi % 2]
    rmsnorm(NPR, NS, R_NFIN, lambda c: yb.v(c * NS, (c + 1) * NS), sq, scr0)
    if STOP_AFTER == 9:
        return done()
    out_rows(lambda c: yb.v(c * NS, (c + 1) * NS), NS, ys_d[:, :])

    return done()


_CACHE = {}


def kernel(x_prompt, x_sample, state_conva, state_convb, state_hgrn, norm_mix, a_w_in, a_conv_w, a_w_out,
           b_w_pw1, b_b_pw1, b_dw_w, b_dw_b, b_ln_g, b_ln_b, b_w_pw2, b_b_pw2, c_lower_bounds, c_w_qfig,
           c_gnorm, c_w_out, norm_ffn, ffn_w_gate_up, ffn_w_down, norm_final):
    f = lambda a: np.ascontiguousarray(np.asarray(a, dtype=np.float32))
    nco = 8
    vec = np.zeros((NVEC, D), np.float32)
    vec[R_NMIX:R_NMIX + 4] = f(norm_mix)
    vec[R_NFFN:R_NFFN + 4] = f(norm_ffn)
    vec[R_NFIN] = f(norm_final)
    vec[R_ACONV:R_ACONV + 6] = f(a_conv_w).reshape(6, D)
    vec[R_BB1:R_BB1 + 2] = f(b_b_pw1).reshape(2, D)
    vec[R_BDW:R_BDW + 31] = f(b_dw_w).reshape(31, D)
    vec[R_BDWB] = f(b_dw_b).reshape(D)
    vec[R_LNG] = f(b_ln_g).reshape(D)
    vec[R_LNB] = f(b_ln_b).reshape(D)
    vec[R_BB2] = f(b_b_pw2).reshape(D)
    vec[R_CLB:R_CLB + 4] = f(c_lower_bounds)
    vec[R_GN, 0:128] = f(c_gnorm).reshape(128)
    shared = {
        "vec": vec, "a_w_in": f(a_w_in), "a_w_out": f(a_w_out), "b_w_pw1": f(b_w_pw1)[0], "b_w_pw2": f(b_w_pw2)[0],
        "c_w_qfig": f(c_w_qfig)[0], "c_w_out": f(c_w_out)[0], "ffn_w_gate_up": f(ffn_w_gate_up),
        "ffn_w_down": f(ffn_w_down),
    }
    xp, xs, sa, sb, sh = f(x_prompt), f(x_sample), f(state_conva), f(state_convb), f(state_hgrn)
    in_maps = []
    for c in range(nco):
        s0, s1 = c * NS, (c + 1) * NS
        m = dict(shared)
        m["xp"] = xp[c]
        m["xs"] = np.ascontiguousarray(xs[s0:s1, 0, :])
        m["sta"] = np.ascontiguousarray(sa[:, s0:s1]).reshape(2, 2 * NS, D)
        m["stb"] = np.ascontiguousarray(sb[0, s0:s1]).reshape(NS * 30, D)
        m["sth"] = np.ascontiguousarray(sh[0, s0:s1])
        in_maps.append(m)
    if "nc" not in _CACHE:
        _CACHE["nc"] = build_program()
    nc = _CACHE["nc"]
    if DEBUG_CORES is not None:
        res = run_bass_kernel_spmd(nc, in_maps[:DEBUG_CORES], core_ids=list(range(DEBUG_CORES)))
        r = list(res.results) + [res.results[0]] * (nco - DEBUG_CORES)
    else:
        res = run_bass_kernel_spmd(nc, in_maps, core_ids=list(range(nco)))
        r = res.results
    y_prompt = np.stack([r[c]["yp"] for c in range(nco)], 0)
    y_sample = np.concatenate([r[c]["ys"] for c in range(nco)], 0).reshape(128, 1, D)
    conva_prompt = np.stack([r[c]["cap"] for c in range(nco)], 1)
    conva_sample = np.concatenate([r[c]["cas"].reshape(2, NS, 2, D) for c in range(nco)], 1)
    convb_prompt = np.stack([r[c]["cbp"] for c in range(nco)], 0)[None]
    convb_sample = np.concatenate([r[c]["cbs"] for c in range(nco)], 0)[None]
    hgrn_prompt = np.stack([r[c]["hgp"] for c in range(nco)], 0)[None]
    hgrn_sample = np.concatenate([r[c]["hgs"] for c in range(nco)], 0)[None]
    return tuple(np.asarray(a, np.float32) for a in (y_prompt, y_sample, conva_prompt, conva_sample, convb_prompt,
                                                     convb_sample, hgrn_prompt, hgrn_sample))
```

```python
import numpy as np
import concourse.bass as bass
import concourse.mybir as mybir
from concourse.bass_utils import run_bass_kernel_spmd

F32 = mybir.dt.float32
BF16 = mybir.dt.bfloat16
AF = mybir.ActivationFunctionType
ALU = mybir.AluOpType
AX = mybir.AxisListType

D = 1024
NPR = 2048
NS = 16
NT = NPR + NS
DFF = 2816
TILES = [(0, 512), (512, 512), (1024, 512), (1536, 512)]
STILE = (2048, 16)
R_SLOTS = 6
SLOT = 4096
PREFETCH = 3
SAME_ENG_SYNC = ('pool', 'dve', 'act')
N_LAYERS = 4
STOP_AFTER = None
SKIP_FFN = False
DG_ENG = 'pool'
SKIP_MIX = False
DEBUG_CORES = None
FFN_STOP = None
SKIP_KINDS = ()
C_STOP = None
C_TILES = None
KDMA = 8
ARENA_BYTES = 75776
NVEC = 64
R_NMIX, R_NFFN, R_NFIN, R_ACONV, R_BB1, R_BDW, R_BDWB, R_LNG, R_LNB, R_BB2, R_CLB, R_GN = 0, 4, 8, 9, 15, 17, 48, 49, 50, 51, 52, 56


class Buf:
    __slots__ = ("base", "lo", "hi", "lw", "rd", "rdd", "ov")

    def __init__(self, base, lo, hi):
        self.base, self.lo, self.hi = base, lo, hi
        self.lw = None
        self.rd = {}
        self.rdd = []
        self.ov = []


class V:
    __slots__ = ("ap", "bufs", "wt")

    def __init__(self, ap, bufs, wt=None):
        self.ap, self.bufs, self.wt = ap, bufs, wt


class Op:
    __slots__ = ("eng", "fn", "reads", "writes", "key", "dma", "deps", "tick", "sem", "val", "pre",
                 "needinc", "wreads", "wtile")


class Prog:
    def __init__(self, nc):
        self.nc = nc
        self.ops = []
        self.bufmap = {}
        self.bybase = {}
        self._bank = 0

    def buf(self, base, lo, hi):
        k = (base, lo, hi)
        b = self.bufmap.get(k)
        if b is None:
            b = Buf(base, lo, hi)
            lst = self.bybase.setdefault(base, [])
            for o in lst:
                if o.lo < hi and lo < o.hi:
                    o.ov.append(b)
                    b.ov.append(o)
            lst.append(b)
            self.bufmap[k] = b
        return b

    def add(self, eng, fn, reads, writes, dma=False, register=True):
        op = Op()
        op.eng, op.fn, op.dma = eng, fn, dma
        op.reads = [b for v in reads for b in v.bufs]
        op.writes = [b for v in writes for b in v.bufs]
        op.wreads = [v.wt for v in reads if v.wt is not None]
        op.wtile = None
        op.needinc = False
        op.tick = 0
        op.pre = 0
        op.key = float(len(self.ops))
        if register:
            self.ops.append(op)
        return op

    def bank(self):
        b = self._bank
        self._bank = (b + 1) % 8
        return b

    def mm(self, out, lhsT, rhs, start=True, stop=True, tp=None):
        kw = {} if tp is None else {"tile_position": tp}
        self.add("pe", lambda e: e.matmul(out.ap, lhsT=lhsT.ap, rhs=rhs.ap, start=start, stop=stop, **kw),
                 [lhsT, rhs], [out])

    def tr(self, out, in_, ident):
        self.add("pe", lambda e: e.transpose(out.ap, in_.ap, ident.ap), [in_, ident], [out])

    def act(self, out, in_, func, bias=None, scale=None):
        reads = [in_]
        kw = {}
        if bias is not None:
            if isinstance(bias, V):
                reads.append(bias)
                kw["bias"] = bias.ap
            else:
                kw["bias"] = float(bias)
        if scale is not None:
            if isinstance(scale, V):
                reads.append(scale)
                kw["scale"] = scale.ap
            else:
                kw["scale"] = float(scale)
        self.add("act", lambda e: e.activation(out=out.ap, in_=in_.ap, func=func, **kw), reads, [out])

    def tt(self, eng, out, in0, in1, op):
        self.add(eng, lambda e: e.tensor_tensor(out=out.ap, in0=in0.ap, in1=in1.ap, op=op), [in0, in1], [out])

    def stt(self, out, in0, scalar, in1, op0, op1):
        reads = [in0, in1]
        s = scalar
        if isinstance(scalar, V):
            reads.append(scalar)
            s = scalar.ap
        self.add("dve", lambda e: e.scalar_tensor_tensor(out=out.ap, in0=in0.ap, scalar=s, in1=in1.ap,
                                                         op0=op0, op1=op1), reads, [out])

    def ts(self, eng, out, in0, s1, op0, s2=None, op1=None):
        reads = [in0]
        a1, a2 = s1, s2
        if isinstance(s1, V):
            reads.append(s1)
            a1 = s1.ap
        if isinstance(s2, V):
            reads.append(s2)
            a2 = s2.ap
        if op1 is None:
            self.add(eng, lambda e: e.tensor_scalar(out=out.ap, in0=in0.ap, scalar1=a1, scalar2=None, op0=op0),
                     reads, [out])
        else:
            self.add(eng, lambda e: e.tensor_scalar(out=out.ap, in0=in0.ap, scalar1=a1, scalar2=a2, op0=op0,
                                                    op1=op1), reads, [out])

    def cp(self, eng, out, in_):
        if eng == "act":
            self.add("act", lambda e: e.copy(out=out.ap, in_=in_.ap), [in_], [out])
        else:
            self.add(eng, lambda e: e.tensor_copy(out=out.ap, in_=in_.ap), [in_], [out])

    def recip(self, out, in_):
        self.add("dve", lambda e: e.reciprocal(out=out.ap, in_=in_.ap), [in_], [out])

    def reduce_add(self, out, in_):
        self.add("dve", lambda e: e.tensor_reduce(out=out.ap, in_=in_.ap, axis=AX.X, op=ALU.add), [in_], [out])

    def memset(self, eng, out, val):
        self.add(eng, lambda e: e.memset(out.ap, val), [], [out])

    def asel(self, out, in_, pattern, cmp, fill, base, cm):
        self.add("pool", lambda e: e.affine_select(out=out.ap, in_=in_.ap, pattern=pattern, compare_op=cmp,
                                                   fill=fill, base=base, channel_multiplier=cm), [in_], [out])

    def dma(self, q, out_ap, in_ap, reads, writes, register=True):
        return self.add(q, lambda e: e.dma_start(out=out_ap, in_=in_ap), reads, writes, dma=True,
                        register=register)

    def finalize(self, extra_ops):
        nc = self.nc
        ops = sorted(self.ops + extra_ops, key=lambda o: o.key)
        slot_cur = {}
        for op in ops:
            deps = set()
            for b in op.reads:
                for o in [b] + b.ov:
                    if o.lw is not None:
                        deps.add(o.lw)
            for b in op.writes:
                for o in [b] + b.ov:
                    if o.lw is not None:
                        deps.add(o.lw)
                    deps.update(o.rd.values())
                    deps.update(o.rdd)
            if op.wtile is not None:
                slot_cur[op.wtile[0]] = op.wtile[1]
            for (s, i) in op.wreads:
                assert slot_cur.get(s) == i, ("weight ring hazard", s, i, slot_cur.get(s))
            for b in op.reads:
                if op.dma:
                    b.rdd.append(op)
                else:
                    b.rd[op.eng] = op
            for b in op.writes:
                b.lw = op
                b.rd = {}
                b.rdd = []
            deps.discard(op)
            keep = []
            for d in deps:
                if d.eng == op.eng and not d.dma and not op.dma:
                    if op.eng == "pe" or op.eng not in SAME_ENG_SYNC:
                        continue
                keep.append(d)
            op.deps = keep
            for d in keep:
                if not d.dma:
                    d.needinc = True
        cnt = {}
        qcnt = {}
        for op in ops:
            if op.dma:
                n = qcnt.get(op.eng, 0)
                qcnt[op.eng] = n + 1
                op.sem = (op.eng, n % KDMA)
                op.val = 16 * (n // KDMA + 1)
                op.pre = 16 * (n // KDMA)
            elif op.needinc:
                cnt[op.eng] = cnt.get(op.eng, 0) + 1
                op.tick = cnt[op.eng]
        self.sorted_ops = ops
        self.qcnt = qcnt
        return ops

    def emit(self):
        nc = self.nc
        ops = self.sorted_ops
        engs = {"pe": [], "act": [], "dve": [], "pool": [], "sp": []}
        for op in ops:
            engs[op.eng].append(op)
        import contextlib
        with contextlib.ExitStack() as es:
            esem = {k: es.enter_context(nc.semaphore("e_" + k)) for k in ("pe", "act", "dve", "pool")}
            dsem = {}
            for q in ("sp", "pool"):
                for i in range(KDMA):
                    dsem[(q, i)] = es.enter_context(nc.semaphore("d_%s%d" % (q, i)))
            block = es.enter_context(nc.Block())
            qcnt = self.qcnt

            def run(e, name):
                waited = {}
                for op in engs[name]:
                    need = {}
                    for d in op.deps:
                        if d.dma:
                            k = ("d",) + d.sem
                            s, val = dsem[d.sem], d.val
                        else:
                            k = ("e", d.eng)
                            s, val = esem[d.eng], d.tick
                        if need.get(k, (None, 0))[1] < val:
                            need[k] = (s, val)
                    if op.dma and op.pre:
                        k = ("d",) + op.sem
                        if need.get(k, (None, 0))[1] < op.pre:
                            need[k] = (dsem[op.sem], op.pre)
                    for k, (s, val) in need.items():
                        if waited.get(k, 0) < val:
                            e.wait_ge(s, val)
                            waited[k] = val
                    ins = op.fn(e)
                    if op.dma:
                        ins.then_inc(dsem[op.sem], 16)
                    elif op.needinc:
                        ins.then_inc(esem[name], 1)
                if name in ("sp", "pool"):
                    n = qcnt.get(name, 0)
                    for i in range(min(KDMA, n)):
                        tot = (n - i + KDMA - 1) // KDMA
                        e.wait_ge(dsem[(name, i)], 16 * tot)

            @block.tensor
            def _(e):
                run(e, "pe")

            @block.scalar
            def _(e):
                run(e, "act")

            @block.vector
            def _(e):
                run(e, "dve")

            @block.gpsimd
            def _(e):
                run(e, "pool")

            @block.sync
            def _(e):
                run(e, "sp")


class Mem:
    def __init__(self, P, base, ap, boff, esz):
        self.P, self.base, self.ap, self.boff, self.esz = P, base, ap, boff, esz

    def b(self, lo, hi):
        return self.P.buf(self.base, self.boff + lo * self.esz, self.boff + hi * self.esz)

    def v(self, lo, hi, p0=0, p1=128):
        return V(self.ap[p0:p1, lo:hi], [self.b(lo, hi)])

    def v3(self, lo, a, b, p0=0, p1=128):
        return V(self.ap[p0:p1, lo:lo + a * b].rearrange("p (a b) -> p a b", a=a), [self.b(lo, lo + a * b)])


class Arena:
    def __init__(self, P, ap_f32, base, nbytes):
        self.P, self.ap, self.base, self.nbytes = P, ap_f32, base, nbytes
        self.top = 0

    def alloc(self, n, dt):
        esz = 4 if dt == F32 else 2
        nb = (n * esz + 3) // 4 * 4
        off = self.top
        assert off + nb <= self.nbytes, ("arena overflow", off, nb, self.nbytes)
        self.top += nb
        ap = self.ap[:, off // 4:(off + nb) // 4]
        if dt != F32:
            ap = ap.bitcast(dt)
        return Mem(self.P, self.base, ap, off, esz)


class WStream:
    def __init__(self, P, ring):
        self.P, self.ring = P, ring
        self.n = 0
        self.first_use = []
        self.dmas = []

    def get(self, parts):
        P = self.P
        i = self.n
        self.n += 1
        slot = i % R_SLOTS
        base = slot * SLOT
        sbuf = self.ring.b(base, base + SLOT)
        self.first_use.append(len(P.ops))
        off = 0
        outs = []
        for pi, (src, a, b) in enumerate(parts):
            dst = self.ring.ap[:, base + off:base + off + a * b].rearrange("p (a b) -> p a b", a=a)
            v = V(dst, [sbuf], wt=(slot, i))
            op = P.dma("pool", dst, src, [], [V(dst, [sbuf])], register=False)
            op.wtile = (slot, i)
            op.key = (i, pi)
            self.dmas.append(op)
            outs.append(v)
            off += a * b
        assert off <= SLOT
        return outs

    def finish(self):
        for op in self.dmas:
            i, pi = op.key
            j = max(0, i - PREFETCH)
            op.key = self.first_use[j] - 0.5 + (i * 4 + pi) * 1e-6
        return self.dmas


def build_program():
    nc = bass.Bass("TRN2", target_bir_lowering=False)
    P = Prog(nc)

    def din(name, shape):
        return nc.dram_tensor(name, shape, F32, kind="ExternalInput").ap()

    def dout(name, shape):
        return nc.dram_tensor(name, shape, F32, kind="ExternalOutput").ap()

    xp_d = din("xp", [NPR, D])
    xs_d = din("xs", [NS, D])
    sta_d = din("sta", [2, 2 * NS, D])
    stb_d = din("stb", [NS * 30, D])
    sth_d = din("sth", [NS, 8, 128, 128])
    vec_d = din("vec", [NVEC, D])
    a_win = din("a_w_in", [2, D, 3 * D])
    a_wout = din("a_w_out", [2, D, D])
    b_w1 = din("b_w_pw1", [D, 2 * D])
    b_w2 = din("b_w_pw2", [D, D])
    c_wq = din("c_w_qfig", [D, 4 * D])
    c_wo = din("c_w_out", [D, D])
    f_wgu = din("ffn_w_gate_up", [4, D, 2 * DFF])
    f_wd = din("ffn_w_down", [4, DFF, D])
    yp_d = dout("yp", [NPR, D])
    ys_d = dout("ys", [NS, D])
    cap_d = dout("cap", [2, 2, D])
    cas_d = dout("cas", [2, 2 * NS, D])
    cbp_d = dout("cbp", [30, D])
    cbs_d = dout("cbs", [NS, 30, D])
    hgp_d = dout("hgp", [8, 128, 128])
    hgs_d = dout("hgs", [NS, 8, 128, 128])

    import contextlib
    es = contextlib.ExitStack()
    xt = es.enter_context(nc.sbuf_tensor("x", [128, 8, NT], F32))
    ringt = es.enter_context(nc.sbuf_tensor("ring", [128, R_SLOTS * SLOT], BF16))
    arenat = es.enter_context(nc.sbuf_tensor("arena", [128, ARENA_BYTES // 4], F32))
    CONST_F32 = 5396
    constt = es.enter_context(nc.sbuf_tensor("const", [128, CONST_F32], F32))
    pst = es.enter_context(nc.psum_tensor("ps", [128, 8, 512], F32))

    ring = Mem(P, "ring", ringt[:], 0, 2)
    W = WStream(P, ring)
    A = Arena(P, arenat[:], "arena", ARENA_BYTES)
    C = Arena(P, constt[:], "const", CONST_F32 * 4)

    def xv(c, t0, n):
        return V(xt[:, c, t0:t0 + n], [P.buf("x", (c * NT + t0) * 4, (c * NT + t0 + n) * 4)])

    def xall(t0, n, c0=0, c1=8):
        return V(xt[:, c0:c1, t0:t0 + n],
                 [P.buf("x", (c * NT + t0) * 4, (c * NT + t0 + n) * 4) for c in range(c0, c1)])

    def ps(bank, lo=0, hi=512, p0=0, p1=128):
        return V(pst[p0:p1, bank, lo:hi], [P.buf("ps", bank * 2048 + lo * 4, bank * 2048 + hi * 4)])

    ident = C.alloc(128, F32)
    maskbd = C.alloc(128, F32)
    trirev = C.alloc(128, F32)
    ones_bf = C.alloc(128, BF16)
    CF = C.alloc(8 * NVEC, F32)
    LBB = C.alloc(1024, F32)
    OMLB = C.alloc(1024, F32)
    LB = C.alloc(8, F32)
    OML = C.alloc(8, F32)
    carryA = C.alloc(16, F32)
    carryB = C.alloc(240, F32)
    S32 = C.alloc(1024, F32)
    SBF = C.alloc(2048, BF16)
    sbf_par = [0] * 8
    ind4 = C.alloc(4, F32)
    identb = C.alloc(128, BF16)

    def cf(c, r):
        return V(CF.ap[:, c * NVEC + r:c * NVEC + r + 1], [CF.b(c * NVEC, (c + 1) * NVEC)])

    iv = ident.v(0, 128)
    P.memset("pool", iv, 0.0)
    P.asel(iv, iv, [[-1, 128]], ALU.not_equal, 1.0, 0, 1)
    P.cp("dve", identb.v(0, 128), iv)
    mv = maskbd.v(0, 128)
    P.memset("pool", mv, 1.0)
    P.asel(mv, mv, [[1, 128]], ALU.is_ge, 0.0, 0, -1)
    for j in range(4):
        sv = maskbd.v(32 * j, 32 * j + 32)
        P.asel(sv, sv, [[0, 32]], ALU.is_ge, 0.0, -32 * j, 1)
    rv = trirev.v(0, 128)
    P.memset("pool", rv, 1.0)
    P.asel(rv, rv, [[-1, 128]], ALU.is_gt, 0.0, 0, 1)
    for j in range(4):
        sv = trirev.v(32 * j, 32 * j + 32)
        P.asel(sv, sv, [[0, 32]], ALU.is_gt, 0.0, 32 * j + 32, -1)
    i4 = ind4.v(0, 4)
    P.memset("pool", i4, 1.0)
    P.asel(i4, i4, [[-32, 4]], ALU.is_ge, 0.0, 0, 1)
    P.asel(i4, i4, [[32, 4]], ALU.is_ge, 0.0, 31, -1)
    P.memset("dve", ones_bf.v(0, 128), 1.0)
    P.memset("dve", carryA.v(0, 16), 0.0)
    P.memset("dve", carryB.v(0, 240), 0.0)
    P.memset("dve", S32.v(0, 1024), 0.0)
    P.memset("dve", SBF.v(0, 2048), 0.0)

    rr = {"s": 0, "o": 0}

    def done():
        P.finalize(W.finish())
        P.emit()
        es.close()
        return nc

    if STOP_AFTER == 0:
        return done()

    def in_rows(dram_rows, n, dst4):
        st = A_stage[rr["s"] % 2]
        rr["s"] += 1
        sv_ = st.v(0, 1024, 0, n)
        P.dma("sp", sv_.ap, dram_rows, [], [sv_])
        for cg in range(2):
            bk = P.bank()
            for c4 in range(4):
                c = cg * 4 + c4
                P.tr(ps(bk, c4 * 128, c4 * 128 + n), V(st.ap[0:n, c * 128:(c + 1) * 128], [st.b(0, 1024)]),
                     V(ident.ap[0:n, 0:n], [ident.b(0, 128)]))
            src = V(pst[:, bk, :].rearrange("p (a b) -> p a b", a=4)[:, :, 0:n], [P.buf("ps", bk * 2048, bk * 2048 + 2048)])
            P.cp("act" if cg == 0 else "dve", dst4(cg), src)

    def out_rows(src, n, dram_rows):
        st = A_stage[rr["s"] % 2]
        rr["s"] += 1
        if n < 128:
            pad = A.alloc(1024, F32)
            P.memset("dve", pad.v(0, 1024), 0.0)
            for c in range(8):
                P.cp("dve", pad.v(c * 128, c * 128 + n), src(c))
            src = lambda c: pad.v(c * 128, (c + 1) * 128)
        for cg in range(2):
            bk = P.bank()
            for c4 in range(4):
                c = cg * 4 + c4
                P.tr(ps(bk, c4 * 128, c4 * 128 + 128), src(c), iv)
            P.cp("act" if cg == 0 else "dve", st.v(cg * 512, cg * 512 + 512, 0, n), ps(bk, 0, 512, 0, n))
        sv_ = st.v(0, 1024, 0, n)
        P.dma("sp", dram_rows, sv_.ap, [sv_], [])

    A.top = 0
    A_stage = [A.alloc(1024, F32), A.alloc(1024, F32)]
    clb = A.alloc(4096, F32)
    tmp1 = A.alloc(1024, F32)
    st = A_stage[0]
    sv_ = st.v(0, 1024, 0, NVEC)
    P.dma("sp", sv_.ap, vec_d[:, :], [], [sv_])
    bk = P.bank()
    for c in range(8):
        P.tr(ps(bk, c * NVEC, (c + 1) * NVEC), V(st.ap[0:NVEC, c * 128:(c + 1) * 128], [st.b(0, 1024)]),
             V(ident.ap[0:NVEC, 0:NVEC], [ident.b(0, 128)]))
    P.cp("dve", CF.v(0, 8 * NVEC), ps(bk))
    rr["s"] = 1
    if STOP_AFTER == 1:
        return done()
    e4 = A.alloc(32, F32)
    ssum = A.alloc(8, F32)
    CF3 = CF.ap[:, :].rearrange("p (c r) -> p c r", c=8)
    P.act(V(e4.ap[:, 0:32].rearrange("p (c r) -> p c r", c=8), [e4.b(0, 32)]),
          V(CF3[:, :, R_CLB:R_CLB + 4], [CF.b(0, 8 * NVEC)]), AF.Exp)
    e43 = e4.ap[:, 0:32].rearrange("p (c r) -> p c r", c=8)
    P.reduce_add(ssum.v(0, 8), V(e43, [e4.b(0, 32)]))
    P.recip(ssum.v(0, 8), ssum.v(0, 8))
    P.tt("dve", LB.v(0, 8), V(e43[:, :, 1], [e4.b(0, 32)]), V(e43[:, :, 2], [e4.b(0, 32)]), ALU.add)
    P.tt("dve", LB.v(0, 8), LB.v(0, 8), ssum.v(0, 8), ALU.mult)
    P.ts("dve", OML.v(0, 8), LB.v(0, 8), -1.0, ALU.mult, 1.0, ALU.add)
    if STOP_AFTER == 2:
        return done()
    cv = clb.v(0, 4096)
    P.dma("sp", clb.ap[:, 0:4096].rearrange("p (a b) -> p a b", a=4), vec_d[R_CLB:R_CLB + 4, :].partition_broadcast(128),
          [], [cv])
    P.act(cv, cv, AF.Exp)
    P.tt("dve", LBB.v(0, 1024), clb.v(1024, 2048), clb.v(2048, 3072), ALU.add)
    P.tt("dve", tmp1.v(0, 1024), clb.v(0, 1024), clb.v(3072, 4096), ALU.add)
    P.tt("dve", tmp1.v(0, 1024), tmp1.v(0, 1024), LBB.v(0, 1024), ALU.add)
    P.recip(tmp1.v(0, 1024), tmp1.v(0, 1024))
    P.tt("dve", LBB.v(0, 1024), LBB.v(0, 1024), tmp1.v(0, 1024), ALU.mult)
    P.ts("dve", OMLB.v(0, 1024), LBB.v(0, 1024), -1.0, ALU.mult, 1.0, ALU.add)
    if STOP_AFTER == 3:
        return done()
    xq = {"next": 0}

    def load_next_block():
        i = xq["next"]
        if i < 16:
            in_rows(xp_d[i * 128:(i + 1) * 128, :], 128, lambda cg, i=i: xall(i * 128, 128, cg * 4, cg * 4 + 4))
        elif i == 16:
            in_rows(xs_d[:, :], NS, lambda cg: xall(NPR, NS, cg * 4, cg * 4 + 4))
        xq["next"] = i + 1

    lazy_x = N_LAYERS >= 1 and not SKIP_MIX and 0 not in SKIP_KINDS
    for i in range(4 if lazy_x else 17):
        load_next_block()

    if STOP_AFTER == 4:
        return done()
    def rmsnorm(t0, n, grow, hout, sq, scr, out_f32=False):
        P.act(sq.v3(0, 8, n), xall(t0, n), AF.Square)
        bk = P.bank()
        for c in range(8):
            P.mm(ps(bk, 0, n), ones_bf.v(0, 128), sq.v(c * n, (c + 1) * n), start=(c == 0), stop=(c == 7))
        rs = scr.v(0, n)
        P.act(rs, ps(bk, 0, n), AF.Ln, bias=1e-6, scale=1.0 / D)
        P.act(rs, rs, AF.Exp, scale=-0.5)
        for c in range(8):
            P.stt(hout(c), xv(c, t0, n), cf(c, grow), rs, ALU.mult, ALU.mult)

    def resid_add(o, t0, n, bk, bias=None):
        if bias is None:
            P.tt("dve", xv(o, t0, n), xv(o, t0, n), ps(bk, 0, n), ALU.add)
        else:
            P.stt(xv(o, t0, n), ps(bk, 0, n), bias, xv(o, t0, n), ALU.add, ALU.add)

    def out_proj(wd, zmem, t0, n, bias_row=None):
        for half in range(2):
            (wt,) = W.get([(wd[:, half * 512:(half + 1) * 512].rearrange("(k p) n -> p k n", p=128), 8, 512)])
            for o4 in range(4):
                o = half * 4 + o4
                bk = P.bank()
                for k in range(8):
                    P.mm(ps(bk, 0, n), V(wt.ap[:, k, o4 * 128:(o4 + 1) * 128], wt.bufs, wt.wt),
                         zmem.v(k * n, (k + 1) * n), start=(k == 0), stop=(k == 7))
                resid_add(o, t0, n, bk, None if bias_row is None else cf(o, bias_row))

    def wsub(wt, k, lo, hi):
        return V(wt.ap[:, k, lo:hi], wt.bufs, wt.wt)

    def proj(bk, wt, hmem, n, col0=0, ncol=128):
        for k in range(8):
            P.mm(ps(bk, 0, n), wsub(wt, k, col0, col0 + ncol), hmem.v(k * n, (k + 1) * n), start=(k == 0), stop=(k == 7))

    def mixer_a(layer, j, tiles):
        A.top = 0
        stg = [A.alloc(1024, F32), A.alloc(1024, F32)]
        A_stage[0], A_stage[1] = stg
        if any(sm_ for (_, _, sm_) in tiles):
            while xq["next"] <= 16:
                load_next_block()
        ctxs = []
        for (t0, n, sample) in tiles:
            cx = {"t0": t0, "n": n, "sample": sample}
            cx["h"] = A.alloc(8 * n, BF16)
            cx["sq"] = A.alloc(8 * n, BF16)
            cx["z"] = A.alloc(8 * n, BF16)
            cx["scr"] = [A.alloc(n, F32) for _ in range(4)]
            cx["ub"] = [A.alloc(n + 4, F32) for _ in range(2)]
            if sample:
                cx["sta"] = A.alloc(8 * 32, F32)
                cx["newa"] = A.alloc(8 * 32, F32)
                sta_ = cx["sta"]
                in_rows(sta_d[j, :, :], 32, lambda cg, sta_=sta_: sta_.v3(cg * 128, 4, 32))
            h_ = cx["h"]
            rmsnorm(t0, n, R_NMIX + layer, lambda c, h_=h_, n=n: h_.v(c * n, (c + 1) * n), cx["sq"], cx["scr"][0])
            ctxs.append(cx)
        wi = a_win[j]

        def chunk_body(cx, c, wts):
            t0, n, sample = cx["t0"], cx["n"], cx["sample"]
            h, z, scr, ub = cx["h"], cx["z"], cx["scr"], cx["ub"]
            cc0 = (c % 4) * 128
            bb, bc, bh = P.bank(), P.bank(), P.bank()
            proj(bb, wts[0], h, n, cc0)
            proj(bc, wts[1], h, n, cc0)
            proj(bh, wts[2], h, n, cc0)
            cgs = scr[1 + c % 2].v(0, n)
            tmp = scr[3].v(0, n)
            P.cp("act", cgs, ps(bc, 0, n))
            w0, w1, w2 = (cf(c, R_ACONV + 3 * j + r) for r in range(3))
            if not sample:
                u = ub[c % 2]
                P.cp("dve", u.v(0, 2), carryA.v(c * 2, c * 2 + 2))
                P.tt("dve", u.v(2, 2 + n), cgs, ps(bh, 0, n), ALU.mult)
                P.cp("dve", carryA.v(c * 2, c * 2 + 2), u.v(n, n + 2))
                P.ts("dve", tmp, u.v(0, n), w0, ALU.mult)
                P.stt(tmp, u.v(1, n + 1), w1, tmp, ALU.mult, ALU.add)
                P.stt(tmp, u.v(2, n + 2), w2, tmp, ALU.mult, ALU.add)
            else:
                sta, newa = cx["sta"], cx["newa"]
                u = ub[c % 2].v(0, n)
                P.tt("dve", u, cgs, ps(bh, 0, n), ALU.mult)
                s3 = sta.ap[:, c * 32:(c + 1) * 32].rearrange("p (s r) -> p s r", r=2)
                n3 = newa.ap[:, c * 32:(c + 1) * 32].rearrange("p (s r) -> p s r", r=2)
                sb_, nb_ = [sta.b(c * 32, c * 32 + 32)], [newa.b(c * 32, c * 32 + 32)]
                P.ts("dve", tmp, V(s3[:, :, 0], sb_), w0, ALU.mult)
                P.stt(tmp, V(s3[:, :, 1], sb_), w1, tmp, ALU.mult, ALU.add)
                P.stt(tmp, u, w2, tmp, ALU.mult, ALU.add)
                P.cp("dve", V(n3[:, :, 0], nb_), V(s3[:, :, 1], sb_))
                P.cp("dve", V(n3[:, :, 1], nb_), u)
            P.tt("dve", z.v(c * n, (c + 1) * n), tmp, ps(bb, 0, n), ALU.mult)

        for c in range(8):
            if c % 4 == 0:
                cq = c // 4
                wts = [W.get([(wi[:, g * 1024 + cq * 512:g * 1024 + (cq + 1) * 512].rearrange("(k p) n -> p k n", p=128), 8, 512)])[0]
                       for g in range(3)]
            for cx in ctxs:
                chunk_body(cx, c, wts)
            if layer == 0 and c % 2 == 1 and xq["next"] <= 16:
                load_next_block()
        for half in range(2):
            (wt,) = W.get([(a_wout[j][:, half * 512:(half + 1) * 512].rearrange("(k p) n -> p k n", p=128), 8, 512)])
            for o4 in range(4):
                o = half * 4 + o4
                for cx in ctxs:
                    n = cx["n"]
                    bk = P.bank()
                    for k in range(8):
                        P.mm(ps(bk, 0, n), V(wt.ap[:, k, o4 * 128:(o4 + 1) * 128], wt.bufs, wt.wt),
                             cx["z"].v(k * n, (k + 1) * n), start=(k == 0), stop=(k == 7))
                    resid_add(o, cx["t0"], n, bk)
        for cx in ctxs:
            if cx["sample"]:
                newa = cx["newa"]
                out_rows(lambda c, newa=newa: newa.v(c * 32, c * 32 + 32), 32, cas_d[j, :, :])

    def mixer_b(layer, t0, n, sample):
        A.top = 0
        if sample:
            stg = [A.alloc(1024, F32), A.alloc(1024, F32)]
            A_stage[0], A_stage[1] = stg
        h = A.alloc(8 * n, BF16)
        sq = A.alloc(8 * n, BF16)
        ybf = A.alloc(8 * n, BF16)
        z = h
        y = A.alloc(8 * n, F32)
        scr = [A.alloc(512, F32) for _ in range(5)]
        ub = [A.alloc(544, F32) for _ in range(2)]
        if not sample:
            ubfb = [A.alloc(544, BF16) for _ in range(2)]
            dgb = [A.alloc(31 * 128, BF16) for _ in range(2)]
        if sample:
            stb = A.alloc(8 * 480, F32)
            us = A.alloc(8 * NS, F32)
            prod = A.alloc(480, F32)
            for q in range(4):
                in_rows(stb_d[q * 120:(q + 1) * 120, :], 120,
                        lambda cg, q=q: V(stb.ap[:, :].rearrange("p (c m) -> p c m", c=8)[:, cg * 4:cg * 4 + 4, q * 120:(q + 1) * 120],
                                          [stb.b(c * 480 + q * 120, c * 480 + (q + 1) * 120) for c in range(cg * 4, cg * 4 + 4)]))
            P.dma("sp", cbs_d[:, 0:29, :], stb_d[:, :].rearrange("(s r) d -> s r d", r=30)[:, 1:30, :], [], [])
        rmsnorm(t0, n, R_NMIX + layer, lambda c: h.v(c * n, (c + 1) * n), sq, scr[0])
        wst = {}

        def s1(c):
            if c % 4 == 0:
                cq = c // 4
                wst["w"] = [W.get([(b_w1[:, g * 1024 + cq * 512:g * 1024 + (cq + 1) * 512].rearrange("(k p) n -> p k n", p=128), 8, 512)])[0]
                            for g in range(2)]
            wts = wst["w"]
            cc0 = (c % 4) * 128
            ba, bg = P.bank(), P.bank()
            proj(ba, wts[0], h, n, cc0)
            proj(bg, wts[1], h, n, cc0)
            sig = scr[1 + c % 2].v(0, n)
            P.act(sig, ps(bg, 0, n), AF.Sigmoid, bias=cf(c, R_BB1 + 1))
            yc = y.v(c * n, (c + 1) * n)
            if not sample:
                ubf = ubfb[c % 2]
                P.cp("dve", ubf.v(0, 30), carryB.v(c * 30, c * 30 + 30))
                P.stt(ubf.v(30, 30 + n), ps(ba, 0, n), cf(c, R_BB1), sig, ALU.add, ALU.mult)
                P.stt(carryB.v(c * 30, c * 30 + 30), ps(ba, n - 30, n), cf(c, R_BB1),
                      V(sig.ap[:, n - 30:n], sig.bufs), ALU.add, ALU.mult)
                dg = dgb[c % 2]
                P.tt(DG_ENG, dg.v3(0, 31, 128),
                     V(identb.ap[:, 0:128].unsqueeze(1).to_broadcast([128, 31, 128]), [identb.b(0, 128)]),
                     V(CF.ap[:, c * NVEC + R_BDW:c * NVEC + R_BDW + 31].unsqueeze(2).to_broadcast([128, 31, 128]),
                       [CF.b(c * NVEC, (c + 1) * NVEC)]), ALU.mult)
            else:
                u = us.v(c * NS, (c + 1) * NS)
                P.stt(u, ps(ba, 0, n), cf(c, R_BB1), sig, ALU.add, ALU.mult)
                wv = V(CF.ap[:, c * NVEC + R_BDW:c * NVEC + R_BDW + 30].unsqueeze(1).to_broadcast([128, NS, 30]),
                       [CF.b(c * NVEC, (c + 1) * NVEC)])
                P.tt("dve", prod.v3(0, NS, 30), stb.v3(c * 480, NS, 30), wv, ALU.mult)
                red = scr[3].v(0, NS)
                P.reduce_add(red, prod.v3(0, NS, 30))
                P.stt(yc, u, cf(c, R_BDW + 30), red, ALU.mult, ALU.add)
                P.ts("dve", yc, yc, cf(c, R_BDWB), ALU.add)

        def s2(c):
            yc = y.v(c * n, (c + 1) * n)
            ubf, dg = ubfb[c % 2], dgb[c % 2]
            by = P.bank()
            for jj in range(31):
                P.mm(ps(by, 0, n), dg.v(jj * 128, (jj + 1) * 128), ubf.v(jj, jj + n), start=(jj == 0), stop=(jj == 30))
            P.act(yc, ps(by, 0, n), AF.Identity, bias=cf(c, R_BDWB))

        if sample:
            for c in range(8):
                s1(c)
        else:
            s1(0)
            for c in range(8):
                if c + 1 < 8:
                    s1(c + 1)
                s2(c)
        P.act(sq.v3(0, 8, n), y.v3(0, 8, n), AF.Square)
        P.cp("dve", ybf.v3(0, 8, n), y.v3(0, 8, n))
        b1, b2 = P.bank(), P.bank()
        for c in range(8):
            P.mm(ps(b1, 0, n), ones_bf.v(0, 128), ybf.v(c * n, (c + 1) * n), start=(c == 0), stop=(c == 7))
        for c in range(8):
            P.mm(ps(b2, 0, n), ones_bf.v(0, 128), sq.v(c * n, (c + 1) * n), start=(c == 0), stop=(c == 7))
        mean, msq, var = scr[0].v(0, n), scr[3].v(0, n), scr[4].v(0, n)
        P.ts("dve", mean, ps(b1, 0, n), 1.0 / D, ALU.mult)
        P.tt("dve", msq, mean, mean, ALU.mult)
        P.stt(var, ps(b2, 0, n), 1.0 / D, msq, ALU.mult, ALU.subtract)
        P.act(var, var, AF.Ln, bias=1e-5)
        P.act(var, var, AF.Exp, scale=-0.5)
        for c in range(8):
            yc = y.v(c * n, (c + 1) * n)
            P.tt("dve", yc, yc, mean, ALU.subtract)
            P.tt("dve", yc, yc, var, ALU.mult)
            P.act(z.v(c * n, (c + 1) * n), yc, AF.Silu, bias=cf(c, R_LNB), scale=cf(c, R_LNG))
        w2t = [W.get([(b_w2[:, hf * 512:(hf + 1) * 512].rearrange("(k p) n -> p k n", p=128), 8, 512)])[0]
               for hf in range(2)]
        for k in range(8):
            for o in range(8):
                P.mm(ps(o, 0, n), wsub(w2t[o // 4], k, (o % 4) * 128, (o % 4 + 1) * 128), z.v(k * n, (k + 1) * n),
                     start=(k == 0), stop=(k == 7))
        for o in range(8):
            resid_add(o, t0, n, o, cf(o, R_BB2))
        if sample:
            out_rows(lambda c: us.v(c * NS, (c + 1) * NS), NS, cbs_d[:, 29, :])

    def mixer_c(layer, t0, n, sample):
        A.top = 0
        h = A.alloc(8 * n, BF16)
        z = A.alloc(8 * n, BF16)
        sq = z
        nscr = 6
        scrp = [A.alloc(512, F32) for _ in range(nscr)]
        sc = {"i": 0}

        def scratch():
            m = scrp[sc["i"] % nscr]
            sc["i"] += 1
            return m

        rmsnorm(t0, n, R_NMIX + layer, lambda c: h.v(c * n, (c + 1) * n), sq, scratch())
        if C_STOP == 0:
            return
        osq = A.alloc(512, BF16)
        qt = A.alloc(512, BF16)
        kt = A.alloc(512, BF16)
        if not sample:
            nst = n // 128
            logf = A.alloc(nst * 1024, F32)
            khat = A.alloc(nst * 1024, BF16)
            vtok = A.alloc(nst * 1024, BF16)
            scm = [A.alloc(128, BF16) for _ in range(4)]
            vmb = [A.alloc(512, BF16) for _ in range(2)]
            epd = [A.alloc(512, F32) for _ in range(2)]
            qtd = [qt, A.alloc(512, BF16)]
            ktd = [kt, A.alloc(512, BF16)]
            its = [(hg, st_) for hg in range(2) for st_ in range(nst)]
            prs = [its[i:i + 2] for i in range(0, len(its), 2)]
            wtok = {}
            tst = {}

            def tbanks(p, i):
                return (p % 2) * 4 + 2 * i, (p % 2) * 4 + 2 * i + 1

            def PA(p):
                for i, (hg, st_) in enumerate(prs[p]):
                    if hg not in wtok:
                        (wf_,) = W.get([(c_wq[:, 1024 + hg * 512:1024 + (hg + 1) * 512].rearrange("(k p) n -> p k n", p=128), 8, 512)])
                        (wi_,) = W.get([(c_wq[:, 2048 + hg * 512:2048 + (hg + 1) * 512].rearrange("(k p) n -> p k n", p=128), 8, 512)])
                        wtok[hg] = (wf_, wi_)
                    wf_, wi_ = wtok[hg]
                    bf_, bi = tbanks(p, i)
                    for k in range(8):
                        P.mm(ps(bf_), h.v(k * n + st_ * 128, k * n + (st_ + 1) * 128), wsub(wf_, k, 0, 512),
                             start=(k == 0), stop=(k == 7))
                    for k in range(8):
                        P.mm(ps(bi), h.v(k * n + st_ * 128, k * n + (st_ + 1) * 128), wsub(wi_, k, 0, 512),
                             start=(k == 0), stop=(k == 7))

            def SA(p):
                for i, (hg, st_) in enumerate(prs[p]):
                    bf_, bi = tbanks(p, i)
                    lo = st_ * 1024 + hg * 512
                    s1 = scratch().v(0, 512)
                    tst[(p, i)] = [s1]
                    P.act(s1, ps(bf_), AF.Sigmoid)
                    P.cp("act", vtok.v(lo, lo + 512), ps(bi))
                    P.tt("dve", s1, s1, OMLB.v(hg * 512, (hg + 1) * 512), ALU.mult)
                    P.tt("dve", s1, s1, LBB.v(hg * 512, (hg + 1) * 512), ALU.add)

            def SB(p):
                for i, (hg, st_) in enumerate(prs[p]):
                    lo = st_ * 1024 + hg * 512
                    P.act(logf.v(lo, lo + 512), tst[(p, i)][0], AF.Ln)
                for i, (hg, st_) in enumerate(prs[p]):
                    bf_, bi = tbanks(p, i)
                    lo = st_ * 1024 + hg * 512
                    s2 = scratch().v(0, 512)
                    tst[(p, i)].append(s2)
                    P.ts("dve", s2, tst[(p, i)][0], -1.0, ALU.mult, 1.0, ALU.add)
                    P.mm(ps(bf_), trirev.v(0, 128), logf.v(lo, lo + 512))

            def SC(p):
                for i, (hg, st_) in enumerate(prs[p]):
                    bf_, bi = tbanks(p, i)
                    lo = st_ * 1024 + hg * 512
                    s3 = scratch().v(0, 512)
                    P.act(s3, ps(bf_), AF.Exp)
                    P.tt("dve", khat.v(lo, lo + 512), tst[(p, i)][1], s3, ALU.mult)

            PA(0)
            for p in range(len(prs)):
                if p + 1 < len(prs):
                    PA(p + 1)
                SA(p)
                SB(p)
                SC(p)
        else:
            ktok = A.alloc(1024, F32)
            vtk = A.alloc(1024, F32)
            vd = A.alloc(NS * 128, F32)
            dm = A.alloc(NS * 128, F32)
            s0b = [A.alloc(1024, F32) for _ in range(4)]
            snb = [A.alloc(1024, F32) for _ in range(4)]

            def load_s0(hd_):
                for half_ in range(2):
                    s0v_ = s0b[(hd_ % 2) * 2 + half_].v3(0, 8, 128)
                    P.dma("sp", s0v_.ap, sth_d[half_ * 8:(half_ + 1) * 8, hd_, :, :].rearrange("s k v -> k s v"), [], [s0v_])

            load_s0(0)
            qs = A.alloc(NS, F32)
            fgf = A.alloc(NS, F32)
            dmv = dm.v3(0, NS, 128, 0, NS)
            P.memset("pool", dmv, 1.0)
            P.asel(dmv, dmv, [[1, NS], [0, 128]], ALU.is_equal, 0.0, 0, -1)
            for hg in range(2):
                (wf,) = W.get([(c_wq[:, 1024 + hg * 512:1024 + (hg + 1) * 512].rearrange("(k p) n -> p k n", p=128), 8, 512)])
                (wi,) = W.get([(c_wq[:, 2048 + hg * 512:2048 + (hg + 1) * 512].rearrange("(k p) n -> p k n", p=128), 8, 512)])
                bf_, bi = P.bank(), P.bank()
                for k in range(8):
                    P.mm(ps(bf_, 0, 512, 0, NS), h.v(k * n, k * n + NS), wsub(wf, k, 0, 512), start=(k == 0), stop=(k == 7))
                for k in range(8):
                    P.mm(ps(bi, 0, 512, 0, NS), h.v(k * n, k * n + NS), wsub(wi, k, 0, 512), start=(k == 0), stop=(k == 7))
                s1 = scratch().v(0, 512, 0, NS)
                P.act(s1, ps(bf_, 0, 512, 0, NS), AF.Sigmoid, scale=-1.0)
                P.tt("dve", ktok.v(hg * 512, (hg + 1) * 512, 0, NS), s1, OMLB.v(hg * 512, (hg + 1) * 512, 0, NS), ALU.mult)
                P.cp("act", vtk.v(hg * 512, (hg + 1) * 512, 0, NS), ps(bi, 0, 512, 0, NS))
        if C_STOP == 1:
            return
        if sample:
            NH = 8 * NS
            for hd in range(8):
                if hd % 4 == 0:
                    hq = hd // 4
                    wts = [W.get([(c_wq[:, g * 1024 + hq * 512:g * 1024 + (hq + 1) * 512].rearrange("(k p) n -> p k n", p=128), 8, 512)])[0]
                           for g in (0, 1, 3)]
                cc0 = (hd % 4) * 128
                for gi in range(3):
                    for k in range(8):
                        P.mm(ps(gi, hd * NS, (hd + 1) * NS), wsub(wts[gi], k, cc0, cc0 + 128), h.v(k * n, k * n + NS),
                             start=(k == 0), stop=(k == 7))
            qs_all = A.alloc(NH, F32)
            sg_all = A.alloc(NH, F32)
            fg_all = A.alloc(NH, F32)
            P.act(qs_all.v(0, NH), ps(0, 0, NH), AF.Silu)
            P.act(sg_all.v(0, NH), ps(2, 0, NH), AF.Silu)
            P.act(fg_all.v(0, NH), ps(1, 0, NH), AF.Sigmoid, scale=-1.0)
            P.ts("dve", qs_all.v(0, NH), qs_all.v(0, NH), 128.0 ** -0.5, ALU.mult)
            P.tt("dve", fg_all.v3(0, 8, NS), fg_all.v3(0, 8, NS),
                 V(OML.ap[:, 0:8].unsqueeze(2).to_broadcast([128, 8, NS]), [OML.b(0, 8)]), ALU.mult)
            P.ts("dve", fg_all.v(0, NH), fg_all.v(0, NH), -1.0, ALU.mult, 1.0, ALU.add)
            BO = 4
            for hd in range(8):
                if hd + 1 < 8:
                    load_s0(hd + 1)
                vb = V(vtk.ap[0:NS, hd * 128:(hd + 1) * 128].unsqueeze(1).to_broadcast([NS, NS, 128]),
                       [vtk.b(hd * 128, (hd + 1) * 128)])
                P.tt("dve", vd.v3(0, NS, 128, 0, NS), vb, dmv, ALU.mult)
                for half in range(2):
                    s0 = s0b[(hd % 2) * 2 + half]
                    sn = snb[(hd % 2) * 2 + half]
                    for i2 in range(2):
                        bkv = 6 + i2
                        P.mm(ps(bkv), ktok.v(hd * 128, (hd + 1) * 128, 0, NS),
                             vd.v((half * 8 + i2 * 4) * 128, (half * 8 + i2 * 4 + 4) * 128, 0, NS))
                    for jj in range(8):
                        col = hd * NS + half * 8 + jj
                        P.stt(sn.v(jj * 128, (jj + 1) * 128), s0.v(jj * 128, (jj + 1) * 128), fg_all.v(col, col + 1),
                              ps(6 + jj // 4, (jj % 4) * 128, (jj % 4 + 1) * 128), ALU.mult, ALU.add)
                    snv = sn.v3(0, 8, 128)
                    P.dma("sp", hgs_d[half * 8:(half + 1) * 8, hd, :, :].rearrange("s k v -> k s v"), snv.ap, [snv], [])
                    for jj in range(8):
                        col = hd * NS + half * 8 + jj
                        P.mm(ps(BO, col, col + 1), sn.v(jj * 128, (jj + 1) * 128), qs_all.v(col, col + 1))
            osq_all = A.alloc(NH, BF16)
            sd_all = A.alloc(NH, F32)
            P.act(osq_all.v(0, NH), ps(BO, 0, NH), AF.Square)
            P.mm(ps(5, 0, NH), ones_bf.v(0, 128), osq_all.v(0, NH))
            P.act(sd_all.v(0, NH), ps(5, 0, NH), AF.Ln, bias=1e-6, scale=1.0 / 128)
            P.act(sd_all.v(0, NH), sd_all.v(0, NH), AF.Exp, scale=-0.5)
            P.stt(sd_all.v(0, NH), ps(BO, 0, NH), cf(0, R_GN), sd_all.v(0, NH), ALU.mult, ALU.mult)
            P.tt("dve", z.v(0, NH), sd_all.v(0, NH), sg_all.v(0, NH), ALU.mult)

        def prep_pair(hds, wts):
            sqv, sgv, env = [], [], []
            for i, hd in enumerate(hds):
                cc0 = (hd % 4) * 128
                proj(2 * i, wts[0], h, n, cc0)
                proj(2 * i + 1, wts[1], h, n, cc0)
                for st_ in range(nst):
                    lo = st_ * 1024 + hd * 128
                    P.mm(ps(6 + i, st_ * 128, (st_ + 1) * 128), logf.v(lo, lo + 128), maskbd.v(0, 128))
            for i, hd in enumerate(hds):
                sqv.append(scratch().v(0, n))
                P.act(sqv[i], ps(2 * i, 0, n), AF.Silu)
            for i, hd in enumerate(hds):
                sgv.append(scratch().v(0, n))
                P.act(sgv[i], ps(2 * i + 1, 0, n), AF.Sigmoid, scale=-1.0)
            for i, hd in enumerate(hds):
                env.append(scratch().v(0, n))
                P.act(epd[i].v(0, n), ps(6 + i, 0, n), AF.Exp)
                P.act(env[i], ps(6 + i, 0, n), AF.Exp, scale=-1.0)
            for i, hd in enumerate(hds):
                omlv = V(OML.ap[:, hd:hd + 1], [OML.b(0, 8)])
                P.stt(qtd[i].v(0, n), sqv[i], 128.0 ** -0.5, epd[i].v(0, n), ALU.mult, ALU.mult)
                P.stt(ktd[i].v(0, n), sgv[i], omlv, env[i], ALU.mult, ALU.mult)

        def chain(hd, BO_, BKV_, epm, qt, kt, scm2, vm):
            for st_ in range(nst):
                c0 = st_ * 128
                lo = st_ * 1024 + hd * 128
                P.mm(ps(BS, 0, 128), kt.v(c0, c0 + 128), qt.v(c0, c0 + 128))
                sm = scm2[st_ % 2].v(0, 128)
                P.tt("dve", sm, ps(BS, 0, 128), maskbd.v(0, 128), ALU.mult)
                P.mm(ps(BO_, c0, c0 + 128), vtok.v(lo, lo + 128), sm, start=True, stop=False)
                P.tt("dve", vm.v3(0, 4, 128),
                     V(vtok.ap[:, lo:lo + 128].unsqueeze(1).to_broadcast([128, 4, 128]), [vtok.b(lo, lo + 128)]),
                     V(ind4.ap[:, 0:4].unsqueeze(2).to_broadcast([128, 4, 128]), [ind4.b(0, 4)]), ALU.mult)
                P.mm(ps(BKV_, 0, 512), khat.v(lo, lo + 128), vm.v(0, 512))
                yield
                for j in range(4):
                    par = sbf_par[hd]
                    so = hd * 256 + par * 128
                    P.mm(ps(BO_, c0 + 32 * j, c0 + 32 * j + 32), SBF.v(so, so + 128),
                         qt.v(c0 + 32 * j, c0 + 32 * j + 32), start=False, stop=(j == 3))
                    col = c0 + 32 * j + 31
                    s32v = S32.v(hd * 128, (hd + 1) * 128)
                    P.stt(s32v, s32v, epm.v(col, col + 1), ps(BKV_, j * 128, (j + 1) * 128), ALU.mult, ALU.add)
                    par ^= 1
                    sbf_par[hd] = par
                    so = hd * 256 + par * 128
                    P.cp("act", SBF.v(so, so + 128), s32v)
                    yield

        def norm_pair(hds, wts):
            sdv, sgv = [], []
            for i, hd in enumerate(hds):
                proj(2 * i, wts[2], h, n, (hd % 4) * 128)
                P.act(osqd[i].v(0, n), ps(4 + i, 0, n), AF.Square)
                P.mm(ps(2 * i + 1, 0, n), ones_bf.v(0, 128), osqd[i].v(0, n))
            for i, hd in enumerate(hds):
                sdv.append(scratch().v(0, n))
                P.act(sdv[i], ps(2 * i + 1, 0, n), AF.Ln, bias=1e-6, scale=1.0 / 128)
            for i, hd in enumerate(hds):
                P.act(sdv[i], sdv[i], AF.Exp, scale=-0.5)
            for i, hd in enumerate(hds):
                sgv.append(scratch().v(0, n))
                P.act(sgv[i], ps(2 * i, 0, n), AF.Silu)
            for i, hd in enumerate(hds):
                t1 = scratch().v(0, n)
                P.stt(t1, ps(4 + i, 0, n), cf(0, R_GN), sdv[i], ALU.mult, ALU.mult)
                P.tt("dve", z.v(hd * n, (hd + 1) * n), t1, sgv[i], ALU.mult)

        if not sample:
            BS = 3
            osqd = [osq, A.alloc(512, BF16)]
            for pair in range(4):
                if pair % 2 == 0:
                    hq = pair // 2
                    wts = [W.get([(c_wq[:, g * 1024 + hq * 512:g * 1024 + (hq + 1) * 512].rearrange("(k p) n -> p k n", p=128), 8, 512)])[0]
                           for g in (0, 1, 3)]
                hds = (2 * pair, 2 * pair + 1)
                prep_pair(hds, wts)
                gens = [chain(hds[i], 4 + i, 6 + i, epd[i], qtd[i], ktd[i], scm[2 * i:2 * i + 2], vmb[i])
                        for i in range(2)]
                alive = list(gens)
                while alive:
                    for g in list(alive):
                        try:
                            next(g)
                        except StopIteration:
                            alive.remove(g)
                norm_pair(hds, wts)
        out_proj(c_wo, z, t0, n)

    def ffn(layer):
        A.top = 0
        h = A.alloc(8 * NT, BF16)
        actb = A.alloc(6 * NT, BF16)
        sq = A.alloc(8 * 512, BF16)
        scr = [A.alloc(512, F32) for _ in range(4)]
        alltiles = TILES + [STILE]
        for (t0, n) in alltiles:
            rmsnorm(t0, n, R_NFFN + layer,
                    lambda c, t0=t0, n=n: V(h.ap[:, c * NT + t0:c * NT + t0 + n], [h.b(c * NT + t0, c * NT + t0 + n)]),
                    sq, scr[0])
        wgu = f_wgu[layer]
        wdn = f_wd[layer]
        si = 0
        if FFN_STOP == 0:
            return
        for gi, (g0, gs) in enumerate([(0, 6), (6, 6), (12, 5), (17, 5)]):
            cc = g0
            while cc < g0 + gs:
                nch = min(4, g0 + gs - cc)
                (wg,) = W.get([(wgu[:, cc * 128:(cc + nch) * 128].rearrange("(k p) n -> p k n", p=128), 8, nch * 128)])
                (wu,) = W.get([(wgu[:, DFF + cc * 128:DFF + (cc + nch) * 128].rearrange("(k p) n -> p k n", p=128), 8, nch * 128)])
                order = [(ci, tl) for ci in range(nch) for tl in alltiles]
                if cc == 0:
                    order = [(ci, tl) for tl in alltiles for ci in range(nch)]
                for ci, (t0, n) in order:
                    a_i = cc + ci - g0
                    if True:
                        bg, bu = P.bank(), P.bank()
                        for k in range(8):
                            P.mm(ps(bg, 0, n), wsub(wg, k, ci * 128, (ci + 1) * 128),
                                 h.v(k * NT + t0, k * NT + t0 + n), start=(k == 0), stop=(k == 7))
                        for k in range(8):
                            P.mm(ps(bu, 0, n), wsub(wu, k, ci * 128, (ci + 1) * 128),
                                 h.v(k * NT + t0, k * NT + t0 + n), start=(k == 0), stop=(k == 7))
                        sg = scr[1 + si % 3].v(0, n)
                        si += 1
                        P.act(sg, ps(bg, 0, n), AF.Silu)
                        P.tt("dve", actb.v(a_i * NT + t0, a_i * NT + t0 + n), sg, ps(bu, 0, n), ALU.mult)
                cc += nch
            if FFN_STOP == 1:
                return
            if layer == N_LAYERS - 1 and gi == 3 and FFN_STOP is None:
                wt2 = [W.get([(wdn[g0 * 128:(g0 + gs) * 128, hf * 512:(hf + 1) * 512].rearrange("(j p) n -> p j n", p=128), gs, 512)])[0]
                       for hf in range(2)]
                keep_top = A.top
                A.top = 0
                A_stage[0], A_stage[1] = A.alloc(1024, F32), A.alloc(1024, F32)
                yfin = A.alloc(8 * 512, F32)
                def dproj(t0, n):
                    for o in range(8):
                        bk = P.bank()
                        for jx in range(gs):
                            P.mm(ps(bk, 0, n), wsub(wt2[o // 4], jx, (o % 4) * 128, (o % 4 + 1) * 128),
                                 actb.v(jx * NT + t0, jx * NT + t0 + n), start=(jx == 0), stop=(jx == gs - 1))
                        resid_add(o, t0, n, bk)

                def fin_tile(t0, n):
                    rmsnorm(t0, n, R_NFIN, lambda c, n=n: yfin.v(c * 512, c * 512 + n), sq, scr[0])
                    A.top = 24576
                    if n == 512:
                        for q in range(4):
                            i = t0 // 128 + q
                            out_rows(lambda c, q=q: yfin.v(c * 512 + q * 128, c * 512 + (q + 1) * 128), 128,
                                     yp_d[i * 128:(i + 1) * 128, :])
                    else:
                        out_rows(lambda c: yfin.v(c * 512, c * 512 + NS), NS, ys_d[:, :])

                dproj(*alltiles[0])
                for ti_ in range(len(alltiles)):
                    if ti_ + 1 < len(alltiles):
                        dproj(*alltiles[ti_ + 1])
                    fin_tile(*alltiles[ti_])
                A.top = keep_top
                fin["done"] = True
                continue
            for half in range(2):
                (wt,) = W.get([(wdn[g0 * 128:(g0 + gs) * 128, half * 512:(half + 1) * 512].rearrange("(j p) n -> p j n", p=128), gs, 512)])
                for o4 in range(4):
                    o = half * 4 + o4
                    for (t0, n) in alltiles:
                        bk = P.bank()
                        for jx in range(gs):
                            P.mm(ps(bk, 0, n), wsub(wt, jx, o4 * 128, (o4 + 1) * 128),
                                 actb.v(jx * NT + t0, jx * NT + t0 + n), start=(jx == 0), stop=(jx == gs - 1))
                        resid_add(o, t0, n, bk)
            if FFN_STOP is not None and FFN_STOP >= 2 and gi == FFN_STOP - 2:
                return

    fin = {"done": False}
    for layer in range(N_LAYERS):
        kind, j = layer % 3, layer // 3
        if kind == 0:
            P.memset("dve", carryA.v(0, 16), 0.0)
        if kind == 0 and not (SKIP_MIX or kind in SKIP_KINDS):
            for (t0, n) in TILES[:3]:
                mixer_a(layer, j, [(t0, n, False)])
            mixer_a(layer, j, [(TILES[3][0], TILES[3][1], False), (STILE[0], STILE[1], True)])
        for (t0, n) in TILES + [STILE]:
            sample = (t0 == NPR)
            if SKIP_MIX or kind in SKIP_KINDS or kind == 0:
                continue
            if kind == 2 and C_TILES == 'p0' and t0 != 0:
                continue
            if kind == 2 and C_TILES == 's' and not sample:
                continue
            if kind == 0:
                mixer_a(layer, j, t0, n, sample)
            elif kind == 1:
                mixer_b(layer, t0, n, sample)
            else:
                mixer_c(layer, t0, n, sample)
        A.top = 0
        A_stage[0], A_stage[1] = A.alloc(1024, F32), A.alloc(1024, F32)
        if kind == 0:
            out_rows(lambda c: carryA.v(c * 2, c * 2 + 2), 2, cap_d[j, :, :])
        elif kind == 1:
            out_rows(lambda c: carryB.v(c * 30, c * 30 + 30), 30, cbp_d[:, :])
        else:
            sv = S32.v3(0, 8, 128)
            P.dma("sp", hgp_d[:, :, :].rearrange("h k v -> k h v"), sv.ap, [sv], [])
        if not SKIP_FFN:
            ffn(layer)

    if fin["done"]:
        return done()
    A.top = 0
    A_stage[0], A_stage[1] = A.alloc(1024, F32), A.alloc(1024, F32)
    sq = A.alloc(8 * 512, BF16)
    scr0 = A.alloc(512, F32)
    yf = [A.alloc(8 * 512, F32) for _ in range(2)]
    fi = 0
    for T in range(4):
        yb = yf[fi % 2]
        fi += 1
        rmsnorm(T * 512, 512, R_NFIN, lambda c, yb=yb: yb.v(c * 512, (c + 1) * 512), sq, scr0)
        for q in range(4):
            i = T * 4 + q
            out_rows(lambda c, yb=yb, q=q: yb.v(c * 512 + q * 128, c * 512 + (q + 1) * 128), 128,
                     yp_d[i * 128:(i + 1) * 128, :])
    if STOP_AFTER == 8:
        return done()
    yb = yf[fi % 2]
    rmsnorm(NPR, NS, R_NFIN, lambda c: yb.v(c * NS, (c + 1) * NS), sq, scr0)
    if STOP_AFTER == 9:
        return done()
    out_rows(lambda c: yb.v(c * NS, (c + 1) * NS), NS, ys_d[:, :])

    return done()


_CACHE = {}


def kernel(x_prompt, x_sample, state_conva, state_convb, state_hgrn, norm_mix, a_w_in, a_conv_w, a_w_out,
           b_w_pw1, b_b_pw1, b_dw_w, b_dw_b, b_ln_g, b_ln_b, b_w_pw2, b_b_pw2, c_lower_bounds, c_w_qfig,
           c_gnorm, c_w_out, norm_ffn, ffn_w_gate_up, ffn_w_down, norm_final):
    f = lambda a: np.ascontiguousarray(np.asarray(a, dtype=np.float32))
    nco = 8
    vec = np.zeros((NVEC, D), np.float32)
    vec[R_NMIX:R_NMIX + 4] = f(norm_mix)
    vec[R_NFFN:R_NFFN + 4] = f(norm_ffn)
    vec[R_NFIN] = f(norm_final)
    vec[R_ACONV:R_ACONV + 6] = f(a_conv_w).reshape(6, D)
    vec[R_BB1:R_BB1 + 2] = f(b_b_pw1).reshape(2, D)
    vec[R_BDW:R_BDW + 31] = f(b_dw_w).reshape(31, D)
    vec[R_BDWB] = f(b_dw_b).reshape(D)
    vec[R_LNG] = f(b_ln_g).reshape(D)
    vec[R_LNB] = f(b_ln_b).reshape(D)
    vec[R_BB2] = f(b_b_pw2).reshape(D)
    vec[R_CLB:R_CLB + 4] = f(c_lower_bounds)
    vec[R_GN, 0:128] = f(c_gnorm).reshape(128)
    shared = {
        "vec": vec, "a_w_in": f(a_w_in), "a_w_out": f(a_w_out), "b_w_pw1": f(b_w_pw1)[0], "b_w_pw2": f(b_w_pw2)[0],
        "c_w_qfig": f(c_w_qfig)[0], "c_w_out": f(c_w_out)[0], "ffn_w_gate_up": f(ffn_w_gate_up),
        "ffn_w_down": f(ffn_w_down),
    }
    xp, xs, sa, sb, sh = f(x_prompt), f(x_sample), f(state_conva), f(state_convb), f(state_hgrn)
    in_maps = []
    for c in range(nco):
        s0, s1 = c * NS, (c + 1) * NS
        m = dict(shared)
        m["xp"] = xp[c]
        m["xs"] = np.ascontiguousarray(xs[s0:s1, 0, :])
        m["sta"] = np.ascontiguousarray(sa[:, s0:s1]).reshape(2, 2 * NS, D)
        m["stb"] = np.ascontiguousarray(sb[0, s0:s1]).reshape(NS * 30, D)
        m["sth"] = np.ascontiguousarray(sh[0, s0:s1])
        in_maps.append(m)
    if "nc" not in _CACHE:
        _CACHE["nc"] = build_program()
    nc = _CACHE["nc"]
    if DEBUG_CORES is not None:
        res = run_bass_kernel_spmd(nc, in_maps[:DEBUG_CORES], core_ids=list(range(DEBUG_CORES)))
        r = list(res.results) + [res.results[0]] * (nco - DEBUG_CORES)
    else:
        res = run_bass_kernel_spmd(nc, in_maps, core_ids=list(range(nco)))
        r = res.results
    y_prompt = np.stack([r[c]["yp"] for c in range(nco)], 0)
    y_sample = np.concatenate([r[c]["ys"] for c in range(nco)], 0).reshape(128, 1, D)
    conva_prompt = np.stack([r[c]["cap"] for c in range(nco)], 1)
    conva_sample = np.concatenate([r[c]["cas"].reshape(2, NS, 2, D) for c in range(nco)], 1)
    convb_prompt = np.stack([r[c]["cbp"] for c in range(nco)], 0)[None]
    convb_sample = np.concatenate([r[c]["cbs"] for c in range(nco)], 0)[None]
    hgrn_prompt = np.stack([r[c]["hgp"] for c in range(nco)], 0)[None]
    hgrn_sample = np.concatenate([r[c]["hgs"] for c in range(nco)], 0)[None]
    return tuple(np.asarray(a, np.float32) for a in (y_prompt, y_sample, conva_prompt, conva_sample, convb_prompt,
                                                     convb_sample, hgrn_prompt, hgrn_sample))
```

```python
import numpy as np
import concourse.bass as bass
import concourse.mybir as mybir
from concourse.bass_utils import run_bass_kernel_spmd

F32 = mybir.dt.float32
BF16 = mybir.dt.bfloat16
AF = mybir.ActivationFunctionType
ALU = mybir.AluOpType
AX = mybir.AxisListType

D = 1024
NPR = 2048
NS = 16
NT = NPR + NS
DFF = 2816
TILES = [(0, 512), (512, 512), (1024, 512), (1536, 512)]
STILE = (2048, 16)
R_SLOTS = 6
SLOT = 4096
PREFETCH = 3
SAME_ENG_SYNC = ('pool', 'dve', 'act')
N_LAYERS = 4
STOP_AFTER = None
SKIP_FFN = False
DG_ENG = 'pool'
SKIP_MIX = False
DEBUG_CORES = None
FFN_STOP = None
SKIP_KINDS = ()
C_STOP = None
C_TILES = None
KDMA = 8
ARENA_BYTES = 75776
NVEC = 64
R_NMIX, R_NFFN, R_NFIN, R_ACONV, R_BB1, R_BDW, R_BDWB, R_LNG, R_LNB, R_BB2, R_CLB, R_GN = 0, 4, 8, 9, 15, 17, 48, 49, 50, 51, 52, 56


class Buf:
    __slots__ = ("base", "lo", "hi", "lw", "rd", "rdd", "ov")

    def __init__(self, base, lo, hi):
        self.base, self.lo, self.hi = base, lo, hi
        self.lw = None
        self.rd = {}
        self.rdd = []
        self.ov = []


class V:
    __slots__ = ("ap", "bufs", "wt")

    def __init__(self, ap, bufs, wt=None):
        self.ap, self.bufs, self.wt = ap, bufs, wt


class Op:
    __slots__ = ("eng", "fn", "reads", "writes", "key", "dma", "deps", "tick", "sem", "val", "pre",
                 "needinc", "wreads", "wtile")


class Prog:
    def __init__(self, nc):
        self.nc = nc
        self.ops = []
        self.bufmap = {}
        self.bybase = {}
        self._bank = 0

    def buf(self, base, lo, hi):
        k = (base, lo, hi)
        b = self.bufmap.get(k)
        if b is None:
            b = Buf(base, lo, hi)
            lst = self.bybase.setdefault(base, [])
            for o in lst:
                if o.lo < hi and lo < o.hi:
                    o.ov.append(b)
                    b.ov.append(o)
            lst.append(b)
            self.bufmap[k] = b
        return b

    def add(self, eng, fn, reads, writes, dma=False, register=True):
        op = Op()
        op.eng, op.fn, op.dma = eng, fn, dma
        op.reads = [b for v in reads for b in v.bufs]
        op.writes = [b for v in writes for b in v.bufs]
        op.wreads = [v.wt for v in reads if v.wt is not None]
        op.wtile = None
        op.needinc = False
        op.tick = 0
        op.pre = 0
        op.key = float(len(self.ops))
        if register:
            self.ops.append(op)
        return op

    def bank(self):
        b = self._bank
        self._bank = (b + 1) % 8
        return b

    def mm(self, out, lhsT, rhs, start=True, stop=True, tp=None):
        kw = {} if tp is None else {"tile_position": tp}
        self.add("pe", lambda e: e.matmul(out.ap, lhsT=lhsT.ap, rhs=rhs.ap, start=start, stop=stop, **kw),
                 [lhsT, rhs], [out])

    def tr(self, out, in_, ident):
        self.add("pe", lambda e: e.transpose(out.ap, in_.ap, ident.ap), [in_, ident], [out])

    def act(self, out, in_, func, bias=None, scale=None):
        reads = [in_]
        kw = {}
        if bias is not None:
            if isinstance(bias, V):
                reads.append(bias)
                kw["bias"] = bias.ap
            else:
                kw["bias"] = float(bias)
        if scale is not None:
            if isinstance(scale, V):
                reads.append(scale)
                kw["scale"] = scale.ap
            else:
                kw["scale"] = float(scale)
        self.add("act", lambda e: e.activation(out=out.ap, in_=in_.ap, func=func, **kw), reads, [out])

    def tt(self, eng, out, in0, in1, op):
        self.add(eng, lambda e: e.tensor_tensor(out=out.ap, in0=in0.ap, in1=in1.ap, op=op), [in0, in1], [out])

    def stt(self, out, in0, scalar, in1, op0, op1):
        reads = [in0, in1]
        s = scalar
        if isinstance(scalar, V):
            reads.append(scalar)
            s = scalar.ap
        self.add("dve", lambda e: e.scalar_tensor_tensor(out=out.ap, in0=in0.ap, scalar=s, in1=in1.ap,
                                                         op0=op0, op1=op1), reads, [out])

    def ts(self, eng, out, in0, s1, op0, s2=None, op1=None):
        reads = [in0]
        a1, a2 = s1, s2
        if isinstance(s1, V):
            reads.append(s1)
            a1 = s1.ap
        if isinstance(s2, V):
            reads.append(s2)
            a2 = s2.ap
        if op1 is None:
            self.add(eng, lambda e: e.tensor_scalar(out=out.ap, in0=in0.ap, scalar1=a1, scalar2=None, op0=op0),
                     reads, [out])
        else:
            self.add(eng, lambda e: e.tensor_scalar(out=out.ap, in0=in0.ap, scalar1=a1, scalar2=a2, op0=op0,
                                                    op1=op1), reads, [out])

    def cp(self, eng, out, in_):
        if eng == "act":
            self.add("act", lambda e: e.copy(out=out.ap, in_=in_.ap), [in_], [out])
        else:
            self.add(eng, lambda e: e.tensor_copy(out=out.ap, in_=in_.ap), [in_], [out])

    def recip(self, out, in_):
        self.add("dve", lambda e: e.reciprocal(out=out.ap, in_=in_.ap), [in_], [out])

    def reduce_add(self, out, in_):
        self.add("dve", lambda e: e.tensor_reduce(out=out.ap, in_=in_.ap, axis=AX.X, op=ALU.add), [in_], [out])

    def memset(self, eng, out, val):
        self.add(eng, lambda e: e.memset(out.ap, val), [], [out])

    def asel(self, out, in_, pattern, cmp, fill, base, cm):
        self.add("pool", lambda e: e.affine_select(out=out.ap, in_=in_.ap, pattern=pattern, compare_op=cmp,
                                                   fill=fill, base=base, channel_multiplier=cm), [in_], [out])

    def dma(self, q, out_ap, in_ap, reads, writes, register=True):
        return self.add(q, lambda e: e.dma_start(out=out_ap, in_=in_ap), reads, writes, dma=True,
                        register=register)

    def finalize(self, extra_ops):
        nc = self.nc
        ops = sorted(self.ops + extra_ops, key=lambda o: o.key)
        slot_cur = {}
        for op in ops:
            deps = set()
            for b in op.reads:
                for o in [b] + b.ov:
                    if o.lw is not None:
                        deps.add(o.lw)
            for b in op.writes:
                for o in [b] + b.ov:
                    if o.lw is not None:
                        deps.add(o.lw)
                    deps.update(o.rd.values())
                    deps.update(o.rdd)
            if op.wtile is not None:
                slot_cur[op.wtile[0]] = op.wtile[1]
            for (s, i) in op.wreads:
                assert slot_cur.get(s) == i, ("weight ring hazard", s, i, slot_cur.get(s))
            for b in op.reads:
                if op.dma:
                    b.rdd.append(op)
                else:
                    b.rd[op.eng] = op
            for b in op.writes:
                b.lw = op
                b.rd = {}
                b.rdd = []
            deps.discard(op)
            keep = []
            for d in deps:
                if d.eng == op.eng and not d.dma and not op.dma:
                    if op.eng == "pe" or op.eng not in SAME_ENG_SYNC:
                        continue
                keep.append(d)
            op.deps = keep
            for d in keep:
                if not d.dma:
                    d.needinc = True
        cnt = {}
        qcnt = {}
        for op in ops:
            if op.dma:
                n = qcnt.get(op.eng, 0)
                qcnt[op.eng] = n + 1
                op.sem = (op.eng, n % KDMA)
                op.val = 16 * (n // KDMA + 1)
                op.pre = 16 * (n // KDMA)
            elif op.needinc:
                cnt[op.eng] = cnt.get(op.eng, 0) + 1
                op.tick = cnt[op.eng]
        self.sorted_ops = ops
        self.qcnt = qcnt
        return ops

    def emit(self):
        nc = self.nc
        ops = self.sorted_ops
        engs = {"pe": [], "act": [], "dve": [], "pool": [], "sp": []}
        for op in ops:
            engs[op.eng].append(op)
        import contextlib
        with contextlib.ExitStack() as es:
            esem = {k: es.enter_context(nc.semaphore("e_" + k)) for k in ("pe", "act", "dve", "pool")}
            dsem = {}
            for q in ("sp", "pool"):
                for i in range(KDMA):
                    dsem[(q, i)] = es.enter_context(nc.semaphore("d_%s%d" % (q, i)))
            block = es.enter_context(nc.Block())
            qcnt = self.qcnt

            def run(e, name):
                waited = {}
                for op in engs[name]:
                    need = {}
                    for d in op.deps:
                        if d.dma:
                            k = ("d",) + d.sem
                            s, val = dsem[d.sem], d.val
                        else:
                            k = ("e", d.eng)
                            s, val = esem[d.eng], d.tick
                        if need.get(k, (None, 0))[1] < val:
                            need[k] = (s, val)
                    if op.dma and op.pre:
                        k = ("d",) + op.sem
                        if need.get(k, (None, 0))[1] < op.pre:
                            need[k] = (dsem[op.sem], op.pre)
                    for k, (s, val) in need.items():
                        if waited.get(k, 0) < val:
                            e.wait_ge(s, val)
                            waited[k] = val
                    ins = op.fn(e)
                    if op.dma:
                        ins.then_inc(dsem[op.sem], 16)
                    elif op.needinc:
                        ins.then_inc(esem[name], 1)
                if name in ("sp", "pool"):
                    n = qcnt.get(name, 0)
                    for i in range(min(KDMA, n)):
                        tot = (n - i + KDMA - 1) // KDMA
                        e.wait_ge(dsem[(name, i)], 16 * tot)

            @block.tensor
            def _(e):
                run(e, "pe")

            @block.scalar
            def _(e):
                run(e, "act")

            @block.vector
            def _(e):
                run(e, "dve")

            @block.gpsimd
            def _(e):
                run(e, "pool")

            @block.sync
            def _(e):
                run(e, "sp")


class Mem:
    def __init__(self, P, base, ap, boff, esz):
        self.P, self.base, self.ap, self.boff, self.esz = P, base, ap, boff, esz

    def b(self, lo, hi):
        return self.P.buf(self.base, self.boff + lo * self.esz, self.boff + hi * self.esz)

    def v(self, lo, hi, p0=0, p1=128):
        return V(self.ap[p0:p1, lo:hi], [self.b(lo, hi)])

    def v3(self, lo, a, b, p0=0, p1=128):
        return V(self.ap[p0:p1, lo:lo + a * b].rearrange("p (a b) -> p a b", a=a), [self.b(lo, lo + a * b)])


class Arena:
    def __init__(self, P, ap_f32, base, nbytes):
        self.P, self.ap, self.base, self.nbytes = P, ap_f32, base, nbytes
        self.top = 0

    def alloc(self, n, dt):
        esz = 4 if dt == F32 else 2
        nb = (n * esz + 3) // 4 * 4
        off = self.top
        assert off + nb <= self.nbytes, ("arena overflow", off, nb, self.nbytes)
        self.top += nb
        ap = self.ap[:, off // 4:(off + nb) // 4]
        if dt != F32:
            ap = ap.bitcast(dt)
        return Mem(self.P, self.base, ap, off, esz)


class WStream:
    def __init__(self, P, ring):
        self.P, self.ring = P, ring
        self.n = 0
        self.first_use = []
        self.dmas = []

    def get(self, parts):
        P = self.P
        i = self.n
        self.n += 1
        slot = i % R_SLOTS
        base = slot * SLOT
        sbuf = self.ring.b(base, base + SLOT)
        self.first_use.append(len(P.ops))
        off = 0
        outs = []
        for pi, (src, a, b) in enumerate(parts):
            dst = self.ring.ap[:, base + off:base + off + a * b].rearrange("p (a b) -> p a b", a=a)
            v = V(dst, [sbuf], wt=(slot, i))
            op = P.dma("pool", dst, src, [], [V(dst, [sbuf])], register=False)
            op.wtile = (slot, i)
            op.key = (i, pi)
            self.dmas.append(op)
            outs.append(v)
            off += a * b
        assert off <= SLOT
        return outs

    def finish(self):
        for op in self.dmas:
            i, pi = op.key
            j = max(0, i - PREFETCH)
            op.key = self.first_use[j] - 0.5 + (i * 4 + pi) * 1e-6
        return self.dmas


def build_program():
    nc = bass.Bass("TRN2", target_bir_lowering=False)
    P = Prog(nc)

    def din(name, shape):
        return nc.dram_tensor(name, shape, F32, kind="ExternalInput").ap()

    def dout(name, shape):
        return nc.dram_tensor(name, shape, F32, kind="ExternalOutput").ap()

    xp_d = din("xp", [NPR, D])
    xs_d = din("xs", [NS, D])
    sta_d = din("sta", [2, 2 * NS, D])
    stb_d = din("stb", [NS * 30, D])
    sth_d = din("sth", [NS, 8, 128, 128])
    vec_d = din("vec", [NVEC, D])
    a_win = din("a_w_in", [2, D, 3 * D])
    a_wout = din("a_w_out", [2, D, D])
    b_w1 = din("b_w_pw1", [D, 2 * D])
    b_w2 = din("b_w_pw2", [D, D])
    c_wq = din("c_w_qfig", [D, 4 * D])
    c_wo = din("c_w_out", [D, D])
    f_wgu = din("ffn_w_gate_up", [4, D, 2 * DFF])
    f_wd = din("ffn_w_down", [4, DFF, D])
    yp_d = dout("yp", [NPR, D])
    ys_d = dout("ys", [NS, D])
    cap_d = dout("cap", [2, 2, D])
    cas_d = dout("cas", [2, 2 * NS, D])
    cbp_d = dout("cbp", [30, D])
    cbs_d = dout("cbs", [NS, 30, D])
    hgp_d = dout("hgp", [8, 128, 128])
    hgs_d = dout("hgs", [NS, 8, 128, 128])

    import contextlib
    es = contextlib.ExitStack()
    xt = es.enter_context(nc.sbuf_tensor("x", [128, 8, NT], F32))
    ringt = es.enter_context(nc.sbuf_tensor("ring", [128, R_SLOTS * SLOT], BF16))
    arenat = es.enter_context(nc.sbuf_tensor("arena", [128, ARENA_BYTES // 4], F32))
    CONST_F32 = 5396
    constt = es.enter_context(nc.sbuf_tensor("const", [128, CONST_F32], F32))
    pst = es.enter_context(nc.psum_tensor("ps", [128, 8, 512], F32))

    ring = Mem(P, "ring", ringt[:], 0, 2)
    W = WStream(P, ring)
    A = Arena(P, arenat[:], "arena", ARENA_BYTES)
    C = Arena(P, constt[:], "const", CONST_F32 * 4)

    def xv(c, t0, n):
        return V(xt[:, c, t0:t0 + n], [P.buf("x", (c * NT + t0) * 4, (c * NT + t0 + n) * 4)])

    def xall(t0, n, c0=0, c1=8):
        return V(xt[:, c0:c1, t0:t0 + n],
                 [P.buf("x", (c * NT + t0) * 4, (c * NT + t0 + n) * 4) for c in range(c0, c1)])

    def ps(bank, lo=0, hi=512, p0=0, p1=128):
        return V(pst[p0:p1, bank, lo:hi], [P.buf("ps", bank * 2048 + lo * 4, bank * 2048 + hi * 4)])

    ident = C.alloc(128, F32)
    maskbd = C.alloc(128, F32)
    trirev = C.alloc(128, F32)
    ones_bf = C.alloc(128, BF16)
    CF = C.alloc(8 * NVEC, F32)
    LBB = C.alloc(1024, F32)
    OMLB = C.alloc(1024, F32)
    LB = C.alloc(8, F32)
    OML = C.alloc(8, F32)
    carryA = C.alloc(16, F32)
    carryB = C.alloc(240, F32)
    S32 = C.alloc(1024, F32)
    SBF = C.alloc(2048, BF16)
    sbf_par = [0] * 8
    ind4 = C.alloc(4, F32)
    identb = C.alloc(128, BF16)

    def cf(c, r):
        return V(CF.ap[:, c * NVEC + r:c * NVEC + r + 1], [CF.b(c * NVEC, (c + 1) * NVEC)])

    iv = ident.v(0, 128)
    P.memset("pool", iv, 0.0)
    P.asel(iv, iv, [[-1, 128]], ALU.not_equal, 1.0, 0, 1)
    P.cp("dve", identb.v(0, 128), iv)
    mv = maskbd.v(0, 128)
    P.memset("pool", mv, 1.0)
    P.asel(mv, mv, [[1, 128]], ALU.is_ge, 0.0, 0, -1)
    for j in range(4):
        sv = maskbd.v(32 * j, 32 * j + 32)
        P.asel(sv, sv, [[0, 32]], ALU.is_ge, 0.0, -32 * j, 1)
    rv = trirev.v(0, 128)
    P.memset("pool", rv, 1.0)
    P.asel(rv, rv, [[-1, 128]], ALU.is_gt, 0.0, 0, 1)
    for j in range(4):
        sv = trirev.v(32 * j, 32 * j + 32)
        P.asel(sv, sv, [[0, 32]], ALU.is_gt, 0.0, 32 * j + 32, -1)
    i4 = ind4.v(0, 4)
    P.memset("pool", i4, 1.0)
    P.asel(i4, i4, [[-32, 4]], ALU.is_ge, 0.0, 0, 1)
    P.asel(i4, i4, [[32, 4]], ALU.is_ge, 0.0, 31, -1)
    P.memset("dve", ones_bf.v(0, 128), 1.0)
    P.memset("dve", carryA.v(0, 16), 0.0)
    P.memset("dve", carryB.v(0, 240), 0.0)
    P.memset("dve", S32.v(0, 1024), 0.0)
    P.memset("dve", SBF.v(0, 2048), 0.0)

    rr = {"s": 0, "o": 0}

    def done():
        P.finalize(W.finish())
        P.emit()
        es.close()
        return nc

    if STOP_AFTER == 0:
        return done()

    def in_rows(dram_rows, n, dst4):
        st = A_stage[rr["s"] % 2]
        rr["s"] += 1
        sv_ = st.v(0, 1024, 0, n)
        P.dma("sp", sv_.ap, dram_rows, [], [sv_])
        for cg in range(2):
            bk = P.bank()
            for c4 in range(4):
                c = cg * 4 + c4
                P.tr(ps(bk, c4 * 128, c4 * 128 + n), V(st.ap[0:n, c * 128:(c + 1) * 128], [st.b(0, 1024)]),
                     V(ident.ap[0:n, 0:n], [ident.b(0, 128)]))
            src = V(pst[:, bk, :].rearrange("p (a b) -> p a b", a=4)[:, :, 0:n], [P.buf("ps", bk * 2048, bk * 2048 + 2048)])
            P.cp("act" if cg == 0 else "dve", dst4(cg), src)

    def out_rows(src, n, dram_rows):
        st = A_stage[rr["s"] % 2]
        rr["s"] += 1
        if n < 128:
            pad = A.alloc(1024, F32)
            P.memset("dve", pad.v(0, 1024), 0.0)
            for c in range(8):
                P.cp("dve", pad.v(c * 128, c * 128 + n), src(c))
            src = lambda c: pad.v(c * 128, (c + 1) * 128)
        for cg in range(2):
            bk = P.bank()
            for c4 in range(4):
                c = cg * 4 + c4
                P.tr(ps(bk, c4 * 128, c4 * 128 + 128), src(c), iv)
            P.cp("act" if cg == 0 else "dve", st.v(cg * 512, cg * 512 + 512, 0, n), ps(bk, 0, 512, 0, n))
        sv_ = st.v(0, 1024, 0, n)
        P.dma("sp", dram_rows, sv_.ap, [sv_], [])

    A.top = 0
    A_stage = [A.alloc(1024, F32), A.alloc(1024, F32)]
    clb = A.alloc(4096, F32)
    tmp1 = A.alloc(1024, F32)
    st = A_stage[0]
    sv_ = st.v(0, 1024, 0, NVEC)
    P.dma("sp", sv_.ap, vec_d[:, :], [], [sv_])
    bk = P.bank()
    for c in range(8):
        P.tr(ps(bk, c * NVEC, (c + 1) * NVEC), V(st.ap[0:NVEC, c * 128:(c + 1) * 128], [st.b(0, 1024)]),
             V(ident.ap[0:NVEC, 0:NVEC], [ident.b(0, 128)]))
    P.cp("dve", CF.v(0, 8 * NVEC), ps(bk))
    rr["s"] = 1
    if STOP_AFTER == 1:
        return done()
    e4 = A.alloc(32, F32)
    ssum = A.alloc(8, F32)
    CF3 = CF.ap[:, :].rearrange("p (c r) -> p c r", c=8)
    P.act(V(e4.ap[:, 0:32].rearrange("p (c r) -> p c r", c=8), [e4.b(0, 32)]),
          V(CF3[:, :, R_CLB:R_CLB + 4], [CF.b(0, 8 * NVEC)]), AF.Exp)
    e43 = e4.ap[:, 0:32].rearrange("p (c r) -> p c r", c=8)
    P.reduce_add(ssum.v(0, 8), V(e43, [e4.b(0, 32)]))
    P.recip(ssum.v(0, 8), ssum.v(0, 8))
    P.tt("dve", LB.v(0, 8), V(e43[:, :, 1], [e4.b(0, 32)]), V(e43[:, :, 2], [e4.b(0, 32)]), ALU.add)
    P.tt("dve", LB.v(0, 8), LB.v(0, 8), ssum.v(0, 8), ALU.mult)
    P.ts("dve", OML.v(0, 8), LB.v(0, 8), -1.0, ALU.mult, 1.0, ALU.add)
    if STOP_AFTER == 2:
        return done()
    cv = clb.v(0, 4096)
    P.dma("sp", clb.ap[:, 0:4096].rearrange("p (a b) -> p a b", a=4), vec_d[R_CLB:R_CLB + 4, :].partition_broadcast(128),
          [], [cv])
    P.act(cv, cv, AF.Exp)
    P.tt("dve", LBB.v(0, 1024), clb.v(1024, 2048), clb.v(2048, 3072), ALU.add)
    P.tt("dve", tmp1.v(0, 1024), clb.v(0, 1024), clb.v(3072, 4096), ALU.add)
    P.tt("dve", tmp1.v(0, 1024), tmp1.v(0, 1024), LBB.v(0, 1024), ALU.add)
    P.recip(tmp1.v(0, 1024), tmp1.v(0, 1024))
    P.tt("dve", LBB.v(0, 1024), LBB.v(0, 1024), tmp1.v(0, 1024), ALU.mult)
    P.ts("dve", OMLB.v(0, 1024), LBB.v(0, 1024), -1.0, ALU.mult, 1.0, ALU.add)
    if STOP_AFTER == 3:
        return done()
    xq = {"next": 0}

    def load_next_block():
        i = xq["next"]
        if i < 16:
            in_rows(xp_d[i * 128:(i + 1) * 128, :], 128, lambda cg, i=i: xall(i * 128, 128, cg * 4, cg * 4 + 4))
        elif i == 16:
            in_rows(xs_d[:, :], NS, lambda cg: xall(NPR, NS, cg * 4, cg * 4 + 4))
        xq["next"] = i + 1

    lazy_x = N_LAYERS >= 1 and not SKIP_MIX and 0 not in SKIP_KINDS
    for i in range(4 if lazy_x else 17):
        load_next_block()

    if STOP_AFTER == 4:
        return done()
    def rmsnorm(t0, n, grow, hout, sq, scr, out_f32=False):
        P.act(sq.v3(0, 8, n), xall(t0, n), AF.Square)
        bk = P.bank()
        for c in range(8):
            P.mm(ps(bk, 0, n), ones_bf.v(0, 128), sq.v(c * n, (c + 1) * n), start=(c == 0), stop=(c == 7))
        rs = scr.v(0, n)
        P.act(rs, ps(bk, 0, n), AF.Ln, bias=1e-6, scale=1.0 / D)
        P.act(rs, rs, AF.Exp, scale=-0.5)
        for c in range(8):
            P.stt(hout(c), xv(c, t0, n), cf(c, grow), rs, ALU.mult, ALU.mult)

    def resid_add(o, t0, n, bk, bias=None):
        if bias is None:
            P.tt("dve", xv(o, t0, n), xv(o, t0, n), ps(bk, 0, n), ALU.add)
        else:
            P.stt(xv(o, t0, n), ps(bk, 0, n), bias, xv(o, t0, n), ALU.add, ALU.add)

    def out_proj(wd, zmem, t0, n, bias_row=None):
        for half in range(2):
            (wt,) = W.get([(wd[:, half * 512:(half + 1) * 512].rearrange("(k p) n -> p k n", p=128), 8, 512)])
            for o4 in range(4):
                o = half * 4 + o4
                bk = P.bank()
                for k in range(8):
                    P.mm(ps(bk, 0, n), V(wt.ap[:, k, o4 * 128:(o4 + 1) * 128], wt.bufs, wt.wt),
                         zmem.v(k * n, (k + 1) * n), start=(k == 0), stop=(k == 7))
                resid_add(o, t0, n, bk, None if bias_row is None else cf(o, bias_row))

    def wsub(wt, k, lo, hi):
        return V(wt.ap[:, k, lo:hi], wt.bufs, wt.wt)

    def proj(bk, wt, hmem, n, col0=0, ncol=128):
        for k in range(8):
            P.mm(ps(bk, 0, n), wsub(wt, k, col0, col0 + ncol), hmem.v(k * n, (k + 1) * n), start=(k == 0), stop=(k == 7))

    def mixer_a(layer, j, tiles):
        A.top = 0
        stg = [A.alloc(1024, F32), A.alloc(1024, F32)]
        A_stage[0], A_stage[1] = stg
        if any(sm_ for (_, _, sm_) in tiles):
            while xq["next"] <= 16:
                load_next_block()
        ctxs = []
        for (t0, n, sample) in tiles:
            cx = {"t0": t0, "n": n, "sample": sample}
            cx["h"] = A.alloc(8 * n, BF16)
            cx["sq"] = A.alloc(8 * n, BF16)
            cx["z"] = A.alloc(8 * n, BF16)
            cx["scr"] = [A.alloc(n, F32) for _ in range(4)]
            cx["ub"] = [A.alloc(n + 4, F32) for _ in range(2)]
            if sample:
                cx["sta"] = A.alloc(8 * 32, F32)
                cx["newa"] = A.alloc(8 * 32, F32)
                sta_ = cx["sta"]
                in_rows(sta_d[j, :, :], 32, lambda cg, sta_=sta_: sta_.v3(cg * 128, 4, 32))
            h_ = cx["h"]
            rmsnorm(t0, n, R_NMIX + layer, lambda c, h_=h_, n=n: h_.v(c * n, (c + 1) * n), cx["sq"], cx["scr"][0])
            ctxs.append(cx)
        wi = a_win[j]

        def chunk_body(cx, c, wts):
            t0, n, sample = cx["t0"], cx["n"], cx["sample"]
            h, z, scr, ub = cx["h"], cx["z"], cx["scr"], cx["ub"]
            cc0 = (c % 4) * 128
            bb, bc, bh = P.bank(), P.bank(), P.bank()
            proj(bb, wts[0], h, n, cc0)
            proj(bc, wts[1], h, n, cc0)
            proj(bh, wts[2], h, n, cc0)
            cgs = scr[1 + c % 2].v(0, n)
            tmp = scr[3].v(0, n)
            P.cp("act", cgs, ps(bc, 0, n))
            w0, w1, w2 = (cf(c, R_ACONV + 3 * j + r) for r in range(3))
            if not sample:
                u = ub[c % 2]
                P.cp("dve", u.v(0, 2), carryA.v(c * 2, c * 2 + 2))
                P.tt("dve", u.v(2, 2 + n), cgs, ps(bh, 0, n), ALU.mult)
                P.cp("dve", carryA.v(c * 2, c * 2 + 2), u.v(n, n + 2))
                P.ts("dve", tmp, u.v(0, n), w0, ALU.mult)
                P.stt(tmp, u.v(1, n + 1), w1, tmp, ALU.mult, ALU.add)
                P.stt(tmp, u.v(2, n + 2), w2, tmp, ALU.mult, ALU.add)
            else:
                sta, newa = cx["sta"], cx["newa"]
                u = ub[c % 2].v(0, n)
                P.tt("dve", u, cgs, ps(bh, 0, n), ALU.mult)
                s3 = sta.ap[:, c * 32:(c + 1) * 32].rearrange("p (s r) -> p s r", r=2)
                n3 = newa.ap[:, c * 32:(c + 1) * 32].rearrange("p (s r) -> p s r", r=2)
                sb_, nb_ = [sta.b(c * 32, c * 32 + 32)], [newa.b(c * 32, c * 32 + 32)]
                P.ts("dve", tmp, V(s3[:, :, 0], sb_), w0, ALU.mult)
                P.stt(tmp, V(s3[:, :, 1], sb_), w1, tmp, ALU.mult, ALU.add)
                P.stt(tmp, u, w2, tmp, ALU.mult, ALU.add)
                P.cp("dve", V(n3[:, :, 0], nb_), V(s3[:, :, 1], sb_))
                P.cp("dve", V(n3[:, :, 1], nb_), u)
            P.tt("dve", z.v(c * n, (c + 1) * n), tmp, ps(bb, 0, n), ALU.mult)

        for c in range(8):
            if c % 4 == 0:
                cq = c // 4
                wts = [W.get([(wi[:, g * 1024 + cq * 512:g * 1024 + (cq + 1) * 512].rearrange("(k p) n -> p k n", p=128), 8, 512)])[0]
                       for g in range(3)]
            for cx in ctxs:
                chunk_body(cx, c, wts)
            if layer == 0 and c % 2 == 1 and xq["next"] <= 16:
                load_next_block()
        for half in range(2):
            (wt,) = W.get([(a_wout[j][:, half * 512:(half + 1) * 512].rearrange("(k p) n -> p k n", p=128), 8, 512)])
            for o4 in range(4):
                o = half * 4 + o4
                for cx in ctxs:
                    n = cx["n"]
                    bk = P.bank()
                    for k in range(8):
                        P.mm(ps(bk, 0, n), V(wt.ap[:, k, o4 * 128:(o4 + 1) * 128], wt.bufs, wt.wt),
                             cx["z"].v(k * n, (k + 1) * n), start=(k == 0), stop=(k == 7))
                    resid_add(o, cx["t0"], n, bk)
        for cx in ctxs:
            if cx["sample"]:
                newa = cx["newa"]
                out_rows(lambda c, newa=newa: newa.v(c * 32, c * 32 + 32), 32, cas_d[j, :, :])

    def mixer_b(layer, t0, n, sample):
        A.top = 0
        if sample:
            stg = [A.alloc(1024, F32), A.alloc(1024, F32)]
            A_stage[0], A_stage[1] = stg
        h = A.alloc(8 * n, BF16)
        sq = A.alloc(8 * n, BF16)
        ybf = A.alloc(8 * n, BF16)
        z = h
        y = A.alloc(8 * n, F32)
        scr = [A.alloc(512, F32) for _ in range(5)]
        ub = [A.alloc(544, F32) for _ in range(2)]
        if not sample:
            ubfb = [A.alloc(544, BF16) for _ in range(2)]
            dgb = [A.alloc(31 * 128, BF16) for _ in range(2)]
        if sample:
            stb = A.alloc(8 * 480, F32)
            us = A.alloc(8 * NS, F32)
            prod = A.alloc(480, F32)
            for q in range(4):
                in_rows(stb_d[q * 120:(q + 1) * 120, :], 120,
                        lambda cg, q=q: V(stb.ap[:, :].rearrange("p (c m) -> p c m", c=8)[:, cg * 4:cg * 4 + 4, q * 120:(q + 1) * 120],
                                          [stb.b(c * 480 + q * 120, c * 480 + (q + 1) * 120) for c in range(cg * 4, cg * 4 + 4)]))
            P.dma("sp", cbs_d[:, 0:29, :], stb_d[:, :].rearrange("(s r) d -> s r d", r=30)[:, 1:30, :], [], [])
        rmsnorm(t0, n, R_NMIX + layer, lambda c: h.v(c * n, (c + 1) * n), sq, scr[0])
        wst = {}

        def s1(c):
            if c % 4 == 0:
                cq = c // 4
                wst["w"] = [W.get([(b_w1[:, g * 1024 + cq * 512:g * 1024 + (cq + 1) * 512].rearrange("(k p) n -> p k n", p=128), 8, 512)])[0]
                            for g in range(2)]
            wts = wst["w"]
            cc0 = (c % 4) * 128
            ba, bg = P.bank(), P.bank()
            proj(ba, wts[0], h, n, cc0)
            proj(bg, wts[1], h, n, cc0)
            sig = scr[1 + c % 2].v(0, n)
            P.act(sig, ps(bg, 0, n), AF.Sigmoid, bias=cf(c, R_BB1 + 1))
            yc = y.v(c * n, (c + 1) * n)
            if not sample:
                ubf = ubfb[c % 2]
                P.cp("dve", ubf.v(0, 30), carryB.v(c * 30, c * 30 + 30))
                P.stt(ubf.v(30, 30 + n), ps(ba, 0, n), cf(c, R_BB1), sig, ALU.add, ALU.mult)
                P.stt(carryB.v(c * 30, c * 30 + 30), ps(ba, n - 30, n), cf(c, R_BB1),
                      V(sig.ap[:, n - 30:n], sig.bufs), ALU.add, ALU.mult)
                dg = dgb[c % 2]
                P.tt(DG_ENG, dg.v3(0, 31, 128),
                     V(identb.ap[:, 0:128].unsqueeze(1).to_broadcast([128, 31, 128]), [identb.b(0, 128)]),
                     V(CF.ap[:, c * NVEC + R_BDW:c * NVEC + R_BDW + 31].unsqueeze(2).to_broadcast([128, 31, 128]),
                       [CF.b(c * NVEC, (c + 1) * NVEC)]), ALU.mult)
            else:
                u = us.v(c * NS, (c + 1) * NS)
                P.stt(u, ps(ba, 0, n), cf(c, R_BB1), sig, ALU.add, ALU.mult)
                wv = V(CF.ap[:, c * NVEC + R_BDW:c * NVEC + R_BDW + 30].unsqueeze(1).to_broadcast([128, NS, 30]),
                       [CF.b(c * NVEC, (c + 1) * NVEC)])
                P.tt("dve", prod.v3(0, NS, 30), stb.v3(c * 480, NS, 30), wv, ALU.mult)
                red = scr[3].v(0, NS)
                P.reduce_add(red, prod.v3(0, NS, 30))
                P.stt(yc, u, cf(c, R_BDW + 30), red, ALU.mult, ALU.add)
                P.ts("dve", yc, yc, cf(c, R_BDWB), ALU.add)

        def s2(c):
            yc = y.v(c * n, (c + 1) * n)
            ubf, dg = ubfb[c % 2], dgb[c % 2]
            by = P.bank()
            for jj in range(31):
                P.mm(ps(by, 0, n), dg.v(jj * 128, (jj + 1) * 128), ubf.v(jj, jj + n), start=(jj == 0), stop=(jj == 30))
            P.act(yc, ps(by, 0, n), AF.Identity, bias=cf(c, R_BDWB))

        if sample:
            for c in range(8):
                s1(c)
        else:
            s1(0)
            for c in range(8):
                if c + 1 < 8:
                    s1(c + 1)
                s2(c)
        P.act(sq.v3(0, 8, n), y.v3(0, 8, n), AF.Square)
        P.cp("dve", ybf.v3(0, 8, n), y.v3(0, 8, n))
        b1, b2 = P.bank(), P.bank()
        for c in range(8):
            P.mm(ps(b1, 0, n), ones_bf.v(0, 128), ybf.v(c * n, (c + 1) * n), start=(c == 0), stop=(c == 7))
        for c in range(8):
            P.mm(ps(b2, 0, n), ones_bf.v(0, 128), sq.v(c * n, (c + 1) * n), start=(c == 0), stop=(c == 7))
        mean, msq, var = scr[0].v(0, n), scr[3].v(0, n), scr[4].v(0, n)
        P.ts("dve", mean, ps(b1, 0, n), 1.0 / D, ALU.mult)
        P.tt("dve", msq, mean, mean, ALU.mult)
        P.stt(var, ps(b2, 0, n), 1.0 / D, msq, ALU.mult, ALU.subtract)
        P.act(var, var, AF.Ln, bias=1e-5)
        P.act(var, var, AF.Exp, scale=-0.5)
        for c in range(8):
            yc = y.v(c * n, (c + 1) * n)
            P.tt("dve", yc, yc, mean, ALU.subtract)
            P.tt("dve", yc, yc, var, ALU.mult)
            P.act(z.v(c * n, (c + 1) * n), yc, AF.Silu, bias=cf(c, R_LNB), scale=cf(c, R_LNG))
        w2t = [W.get([(b_w2[:, hf * 512:(hf + 1) * 512].rearrange("(k p) n -> p k n", p=128), 8, 512)])[0]
               for hf in range(2)]
        for k in range(8):
            for o in range(8):
                P.mm(ps(o, 0, n), wsub(w2t[o // 4], k, (o % 4) * 128, (o % 4 + 1) * 128), z.v(k * n, (k + 1) * n),
                     start=(k == 0), stop=(k == 7))
        for o in range(8):
            resid_add(o, t0, n, o, cf(o, R_BB2))
        if sample:
            out_rows(lambda c: us.v(c * NS, (c + 1) * NS), NS, cbs_d[:, 29, :])

    def mixer_c(layer, t0, n, sample):
        A.top = 0
        h = A.alloc(8 * n, BF16)
        z = A.alloc(8 * n, BF16)
        sq = z
        nscr = 6
        scrp = [A.alloc(512, F32) for _ in range(nscr)]
        sc = {"i": 0}

        def scratch():
            m = scrp[sc["i"] % nscr]
            sc["i"] += 1
            return m

        rmsnorm(t0, n, R_NMIX + layer, lambda c: h.v(c * n, (c + 1) * n), sq, scratch())
        if C_STOP == 0:
            return
        osq = A.alloc(512, BF16)
        qt = A.alloc(512, BF16)
        kt = A.alloc(512, BF16)
        if not sample:
            nst = n // 128
            logf = A.alloc(nst * 1024, F32)
            khat = A.alloc(nst * 1024, BF16)
            vtok = A.alloc(nst * 1024, BF16)
            scm = [A.alloc(128, BF16) for _ in range(4)]
            vmb = [A.alloc(512, BF16) for _ in range(2)]
            epd = [A.alloc(512, F32) for _ in range(2)]
            qtd = [qt, A.alloc(512, BF16)]
            ktd = [kt, A.alloc(512, BF16)]
            its = [(hg, st_) for hg in range(2) for st_ in range(nst)]
            prs = [its[i:i + 2] for i in range(0, len(its), 2)]
            wtok = {}
            tst = {}

            def tbanks(p, i):
                return (p % 2) * 4 + 2 * i, (p % 2) * 4 + 2 * i + 1

            def PA(p):
                for i, (hg, st_) in enumerate(prs[p]):
                    if hg not in wtok:
                        (wf_,) = W.get([(c_wq[:, 1024 + hg * 512:1024 + (hg + 1) * 512].rearrange("(k p) n -> p k n", p=128), 8, 512)])
                        (wi_,) = W.get([(c_wq[:, 2048 + hg * 512:2048 + (hg + 1) * 512].rearrange("(k p) n -> p k n", p=128), 8, 512)])
                        wtok[hg] = (wf_, wi_)
                    wf_, wi_ = wtok[hg]
                    bf_, bi = tbanks(p, i)
                    for k in range(8):
                        P.mm(ps(bf_), h.v(k * n + st_ * 128, k * n + (st_ + 1) * 128), wsub(wf_, k, 0, 512),
                             start=(k == 0), stop=(k == 7))
                    for k in range(8):
                        P.mm(ps(bi), h.v(k * n + st_ * 128, k * n + (st_ + 1) * 128), wsub(wi_, k, 0, 512),
                             start=(k == 0), stop=(k == 7))

            def SA(p):
                for i, (hg, st_) in enumerate(prs[p]):
                    bf_, bi = tbanks(p, i)
                    lo = st_ * 1024 + hg * 512
                    s1 = scratch().v(0, 512)
                    tst[(p, i)] = [s1]
                    P.act(s1, ps(bf_), AF.Sigmoid)
                    P.cp("act", vtok.v(lo, lo + 512), ps(bi))
                    P.tt("dve", s1, s1, OMLB.v(hg * 512, (hg + 1) * 512), ALU.mult)
                    P.tt("dve", s1, s1, LBB.v(hg * 512, (hg + 1) * 512), ALU.add)

            def SB(p):
                for i, (hg, st_) in enumerate(prs[p]):
                    lo = st_ * 1024 + hg * 512
                    P.act(logf.v(lo, lo + 512), tst[(p, i)][0], AF.Ln)
                for i, (hg, st_) in enumerate(prs[p]):
                    bf_, bi = tbanks(p, i)
                    lo = st_ * 1024 + hg * 512
                    s2 = scratch().v(0, 512)
                    tst[(p, i)].append(s2)
                    P.ts("dve", s2, tst[(p, i)][0], -1.0, ALU.mult, 1.0, ALU.add)
                    P.mm(ps(bf_), trirev.v(0, 128), logf.v(lo, lo + 512))

            def SC(p):
                for i, (hg, st_) in enumerate(prs[p]):
                    bf_, bi = tbanks(p, i)
                    lo = st_ * 1024 + hg * 512
                    s3 = scratch().v(0, 512)
                    P.act(s3, ps(bf_), AF.Exp)
                    P.tt("dve", khat.v(lo, lo + 512), tst[(p, i)][1], s3, ALU.mult)

            PA(0)
            for p in range(len(prs)):
                if p + 1 < len(prs):
                    PA(p + 1)
                SA(p)
                SB(p)
                SC(p)
        else:
            ktok = A.alloc(1024, F32)
            vtk = A.alloc(1024, F32)
            vd = A.alloc(NS * 128, F32)
            dm = A.alloc(NS * 128, F32)
            s0b = [A.alloc(1024, F32) for _ in range(4)]
            snb = [A.alloc(1024, F32) for _ in range(4)]

            def load_s0(hd_):
                for half_ in range(2):
                    s0v_ = s0b[(hd_ % 2) * 2 + half_].v3(0, 8, 128)
                    P.dma("sp", s0v_.ap, sth_d[half_ * 8:(half_ + 1) * 8, hd_, :, :].rearrange("s k v -> k s v"), [], [s0v_])

            load_s0(0)
            qs = A.alloc(NS, F32)
            fgf = A.alloc(NS, F32)
            dmv = dm.v3(0, NS, 128, 0, NS)
            P.memset("pool", dmv, 1.0)
            P.asel(dmv, dmv, [[1, NS], [0, 128]], ALU.is_equal, 0.0, 0, -1)
            for hg in range(2):
                (wf,) = W.get([(c_wq[:, 1024 + hg * 512:1024 + (hg + 1) * 512].rearrange("(k p) n -> p k n", p=128), 8, 512)])
                (wi,) = W.get([(c_wq[:, 2048 + hg * 512:2048 + (hg + 1) * 512].rearrange("(k p) n -> p k n", p=128), 8, 512)])
                bf_, bi = P.bank(), P.bank()
                for k in range(8):
                    P.mm(ps(bf_, 0, 512, 0, NS), h.v(k * n, k * n + NS), wsub(wf, k, 0, 512), start=(k == 0), stop=(k == 7))
                for k in range(8):
                    P.mm(ps(bi, 0, 512, 0, NS), h.v(k * n, k * n + NS), wsub(wi, k, 0, 512), start=(k == 0), stop=(k == 7))
                s1 = scratch().v(0, 512, 0, NS)
                P.act(s1, ps(bf_, 0, 512, 0, NS), AF.Sigmoid, scale=-1.0)
                P.tt("dve", ktok.v(hg * 512, (hg + 1) * 512, 0, NS), s1, OMLB.v(hg * 512, (hg + 1) * 512, 0, NS), ALU.mult)
                P.cp("act", vtk.v(hg * 512, (hg + 1) * 512, 0, NS), ps(bi, 0, 512, 0, NS))
        if C_STOP == 1:
            return
        if sample:
            NH = 8 * NS
            for hd in range(8):
                if hd % 4 == 0:
                    hq = hd // 4
                    wts = [W.get([(c_wq[:, g * 1024 + hq * 512:g * 1024 + (hq + 1) * 512].rearrange("(k p) n -> p k n", p=128), 8, 512)])[0]
                           for g in (0, 1, 3)]
                cc0 = (hd % 4) * 128
                for gi in range(3):
                    for k in range(8):
                        P.mm(ps(gi, hd * NS, (hd + 1) * NS), wsub(wts[gi], k, cc0, cc0 + 128), h.v(k * n, k * n + NS),
                             start=(k == 0), stop=(k == 7))
            qs_all = A.alloc(NH, F32)
            sg_all = A.alloc(NH, F32)
            fg_all = A.alloc(NH, F32)
            P.act(qs_all.v(0, NH), ps(0, 0, NH), AF.Silu)
            P.act(sg_all.v(0, NH), ps(2, 0, NH), AF.Silu)
            P.act(fg_all.v(0, NH), ps(1, 0, NH), AF.Sigmoid, scale=-1.0)
            P.ts("dve", qs_all.v(0, NH), qs_all.v(0, NH), 128.0 ** -0.5, ALU.mult)
            P.tt("dve", fg_all.v3(0, 8, NS), fg_all.v3(0, 8, NS),
                 V(OML.ap[:, 0:8].unsqueeze(2).to_broadcast([128, 8, NS]), [OML.b(0, 8)]), ALU.mult)
            P.ts("dve", fg_all.v(0, NH), fg_all.v(0, NH), -1.0, ALU.mult, 1.0, ALU.add)
            BO = 4
            for hd in range(8):
                if hd + 1 < 8:
                    load_s0(hd + 1)
                vb = V(vtk.ap[0:NS, hd * 128:(hd + 1) * 128].unsqueeze(1).to_broadcast([NS, NS, 128]),
                       [vtk.b(hd * 128, (hd + 1) * 128)])
                P.tt("dve", vd.v3(0, NS, 128, 0, NS), vb, dmv, ALU.mult)
                for half in range(2):
                    s0 = s0b[(hd % 2) * 2 + half]
                    sn = snb[(hd % 2) * 2 + half]
                    for i2 in range(2):
                        bkv = 6 + i2
                        P.mm(ps(bkv), ktok.v(hd * 128, (hd + 1) * 128, 0, NS),
                             vd.v((half * 8 + i2 * 4) * 128, (half * 8 + i2 * 4 + 4) * 128, 0, NS))
                    for jj in range(8):
                        col = hd * NS + half * 8 + jj
                        P.stt(sn.v(jj * 128, (jj + 1) * 128), s0.v(jj * 128, (jj + 1) * 128), fg_all.v(col, col + 1),
                              ps(6 + jj // 4, (jj % 4) * 128, (jj % 4 + 1) * 128), ALU.mult, ALU.add)
                    snv = sn.v3(0, 8, 128)
                    P.dma("sp", hgs_d[half * 8:(half + 1) * 8, hd, :, :].rearrange("s k v -> k s v"), snv.ap, [snv], [])
                    for jj in range(8):
                        col = hd * NS + half * 8 + jj
                        P.mm(ps(BO, col, col + 1), sn.v(jj * 128, (jj + 1) * 128), qs_all.v(col, col + 1))
            osq_all = A.alloc(NH, BF16)
            sd_all = A.alloc(NH, F32)
            P.act(osq_all.v(0, NH), ps(BO, 0, NH), AF.Square)
            P.mm(ps(5, 0, NH), ones_bf.v(0, 128), osq_all.v(0, NH))
            P.act(sd_all.v(0, NH), ps(5, 0, NH), AF.Ln, bias=1e-6, scale=1.0 / 128)
            P.act(sd_all.v(0, NH), sd_all.v(0, NH), AF.Exp, scale=-0.5)
            P.stt(sd_all.v(0, NH), ps(BO, 0, NH), cf(0, R_GN), sd_all.v(0, NH), ALU.mult, ALU.mult)
            P.tt("dve", z.v(0, NH), sd_all.v(0, NH), sg_all.v(0, NH), ALU.mult)

        def prep_pair(hds, wts):
            sqv, sgv, env = [], [], []
            for i, hd in enumerate(hds):
                cc0 = (hd % 4) * 128
                proj(2 * i, wts[0], h, n, cc0)
                proj(2 * i + 1, wts[1], h, n, cc0)
                for st_ in range(nst):
                    lo = st_ * 1024 + hd * 128
                    P.mm(ps(6 + i, st_ * 128, (st_ + 1) * 128), logf.v(lo, lo + 128), maskbd.v(0, 128))
            for i, hd in enumerate(hds):
                sqv.append(scratch().v(0, n))
                P.act(sqv[i], ps(2 * i, 0, n), AF.Silu)
            for i, hd in enumerate(hds):
                sgv.append(scratch().v(0, n))
                P.act(sgv[i], ps(2 * i + 1, 0, n), AF.Sigmoid, scale=-1.0)
            for i, hd in enumerate(hds):
                env.append(scratch().v(0, n))
                P.act(epd[i].v(0, n), ps(6 + i, 0, n), AF.Exp)
                P.act(env[i], ps(6 + i, 0, n), AF.Exp, scale=-1.0)
            for i, hd in enumerate(hds):
                omlv = V(OML.ap[:, hd:hd + 1], [OML.b(0, 8)])
                P.stt(qtd[i].v(0, n), sqv[i], 128.0 ** -0.5, epd[i].v(0, n), ALU.mult, ALU.mult)
                P.stt(ktd[i].v(0, n), sgv[i], omlv, env[i], ALU.mult, ALU.mult)

        def chain(hd, BO_, BKV_, epm, qt, kt, scm2, vm):
            for st_ in range(nst):
                c0 = st_ * 128
                lo = st_ * 1024 + hd * 128
                P.mm(ps(BS, 0, 128), kt.v(c0, c0 + 128), qt.v(c0, c0 + 128))
                sm = scm2[st_ % 2].v(0, 128)
                P.tt("dve", sm, ps(BS, 0, 128), maskbd.v(0, 128), ALU.mult)
                P.mm(ps(BO_, c0, c0 + 128), vtok.v(lo, lo + 128), sm, start=True, stop=False)
                P.tt("dve", vm.v3(0, 4, 128),
                     V(vtok.ap[:, lo:lo + 128].unsqueeze(1).to_broadcast([128, 4, 128]), [vtok.b(lo, lo + 128)]),
                     V(ind4.ap[:, 0:4].unsqueeze(2).to_broadcast([128, 4, 128]), [ind4.b(0, 4)]), ALU.mult)
                P.mm(ps(BKV_, 0, 512), khat.v(lo, lo + 128), vm.v(0, 512))
                yield
                for j in range(4):
                    par = sbf_par[hd]
                    so = hd * 256 + par * 128
                    P.mm(ps(BO_, c0 + 32 * j, c0 + 32 * j + 32), SBF.v(so, so + 128),
                         qt.v(c0 + 32 * j, c0 + 32 * j + 32), start=False, stop=(j == 3))
                    col = c0 + 32 * j + 31
                    s32v = S32.v(hd * 128, (hd + 1) * 128)
                    P.stt(s32v, s32v, epm.v(col, col + 1), ps(BKV_, j * 128, (j + 1) * 128), ALU.mult, ALU.add)
                    par ^= 1
                    sbf_par[hd] = par
                    so = hd * 256 + par * 128
                    P.cp("act", SBF.v(so, so + 128), s32v)
                    yield

        def gproj_gen(hds, wts):
            for i, hd in enumerate(hds):
                cc0 = (hd % 4) * 128
                for k in range(8):
                    P.mm(ps(2 * i, 0, n), wsub(wts[2], k, cc0, cc0 + 128), h.v(k * n, (k + 1) * n),
                         start=(k == 0), stop=(k == 7))
                    yield

        def norm_pair(hds, wts):
            sdv, sgv = [], []
            for i, hd in enumerate(hds):
                P.act(osqd[i].v(0, n), ps(4 + i, 0, n), AF.Square)
                P.mm(ps(2 * i + 1, 0, n), ones_bf.v(0, 128), osqd[i].v(0, n))
            for i, hd in enumerate(hds):
                sdv.append(scratch().v(0, n))
                P.act(sdv[i], ps(2 * i + 1, 0, n), AF.Ln, bias=1e-6, scale=1.0 / 128)
            for i, hd in enumerate(hds):
                P.act(sdv[i], sdv[i], AF.Exp, scale=-0.5)
            for i, hd in enumerate(hds):
                sgv.append(scratch().v(0, n))
                P.act(sgv[i], ps(2 * i, 0, n), AF.Silu)
            for i, hd in enumerate(hds):
                t1 = scratch().v(0, n)
                P.stt(t1, ps(4 + i, 0, n), cf(0, R_GN), sdv[i], ALU.mult, ALU.mult)
                P.tt("dve", z.v(hd * n, (hd + 1) * n), t1, sgv[i], ALU.mult)

        if not sample:
            BS = 3
            osqd = [osq, A.alloc(512, BF16)]
            for pair in range(4):
                if pair % 2 == 0:
                    hq = pair // 2
                    wts = [W.get([(c_wq[:, g * 1024 + hq * 512:g * 1024 + (hq + 1) * 512].rearrange("(k p) n -> p k n", p=128), 8, 512)])[0]
                           for g in (0, 1, 3)]
                hds = (2 * pair, 2 * pair + 1)
                prep_pair(hds, wts)
                gens = [chain(hds[i], 4 + i, 6 + i, epd[i], qtd[i], ktd[i], scm[2 * i:2 * i + 2], vmb[i])
                        for i in range(2)]
                gens.append(gproj_gen(hds, wts))
                alive = list(gens)
                while alive:
                    for g in list(alive):
                        try:
                            next(g)
                        except StopIteration:
                            alive.remove(g)
                norm_pair(hds, wts)
        out_proj(c_wo, z, t0, n)

    def ffn(layer):
        A.top = 0
        h = A.alloc(8 * NT, BF16)
        actb = A.alloc(6 * NT, BF16)
        sq = A.alloc(8 * 512, BF16)
        scr = [A.alloc(512, F32) for _ in range(4)]
        alltiles = TILES + [STILE]
        for (t0, n) in alltiles:
            rmsnorm(t0, n, R_NFFN + layer,
                    lambda c, t0=t0, n=n: V(h.ap[:, c * NT + t0:c * NT + t0 + n], [h.b(c * NT + t0, c * NT + t0 + n)]),
                    sq, scr[0])
        wgu = f_wgu[layer]
        wdn = f_wd[layer]
        si = 0
        if FFN_STOP == 0:
            return
        for gi, (g0, gs) in enumerate([(0, 6), (6, 6), (12, 5), (17, 5)]):
            cc = g0
            while cc < g0 + gs:
                nch = min(4, g0 + gs - cc)
                (wg,) = W.get([(wgu[:, cc * 128:(cc + nch) * 128].rearrange("(k p) n -> p k n", p=128), 8, nch * 128)])
                (wu,) = W.get([(wgu[:, DFF + cc * 128:DFF + (cc + nch) * 128].rearrange("(k p) n -> p k n", p=128), 8, nch * 128)])
                order = [(ci, tl) for ci in range(nch) for tl in alltiles]
                if cc == 0:
                    order = [(ci, tl) for tl in alltiles for ci in range(nch)]
                for ci, (t0, n) in order:
                    a_i = cc + ci - g0
                    if True:
                        bg, bu = P.bank(), P.bank()
                        for k in range(8):
                            P.mm(ps(bg, 0, n), wsub(wg, k, ci * 128, (ci + 1) * 128),
                                 h.v(k * NT + t0, k * NT + t0 + n), start=(k == 0), stop=(k == 7))
                        for k in range(8):
                            P.mm(ps(bu, 0, n), wsub(wu, k, ci * 128, (ci + 1) * 128),
                                 h.v(k * NT + t0, k * NT + t0 + n), start=(k == 0), stop=(k == 7))
                        sg = scr[1 + si % 3].v(0, n)
                        si += 1
                        P.act(sg, ps(bg, 0, n), AF.Silu)
                        P.tt("dve", actb.v(a_i * NT + t0, a_i * NT + t0 + n), sg, ps(bu, 0, n), ALU.mult)
                cc += nch
            if FFN_STOP == 1:
                return
            if layer == N_LAYERS - 1 and gi == 3 and FFN_STOP is None:
                wt2 = [W.get([(wdn[g0 * 128:(g0 + gs) * 128, hf * 512:(hf + 1) * 512].rearrange("(j p) n -> p j n", p=128), gs, 512)])[0]
                       for hf in range(2)]
                keep_top = A.top
                A.top = 0
                A_stage[0], A_stage[1] = A.alloc(1024, F32), A.alloc(1024, F32)
                yfin = A.alloc(8 * 512, F32)
                def dproj(t0, n):
                    for o in range(8):
                        bk = P.bank()
                        for jx in range(gs):
                            P.mm(ps(bk, 0, n), wsub(wt2[o // 4], jx, (o % 4) * 128, (o % 4 + 1) * 128),
                                 actb.v(jx * NT + t0, jx * NT + t0 + n), start=(jx == 0), stop=(jx == gs - 1))
                        resid_add(o, t0, n, bk)

                def fin_tile(t0, n):
                    rmsnorm(t0, n, R_NFIN, lambda c, n=n: yfin.v(c * 512, c * 512 + n), sq, scr[0])
                    A.top = 24576
                    if n == 512:
                        for q in range(4):
                            i = t0 // 128 + q
                            out_rows(lambda c, q=q: yfin.v(c * 512 + q * 128, c * 512 + (q + 1) * 128), 128,
                                     yp_d[i * 128:(i + 1) * 128, :])
                    else:
                        out_rows(lambda c: yfin.v(c * 512, c * 512 + NS), NS, ys_d[:, :])

                dproj(*alltiles[0])
                for ti_ in range(len(alltiles)):
                    if ti_ + 1 < len(alltiles):
                        dproj(*alltiles[ti_ + 1])
                    fin_tile(*alltiles[ti_])
                A.top = keep_top
                fin["done"] = True
                continue
            for half in range(2):
                (wt,) = W.get([(wdn[g0 * 128:(g0 + gs) * 128, half * 512:(half + 1) * 512].rearrange("(j p) n -> p j n", p=128), gs, 512)])
                for o4 in range(4):
                    o = half * 4 + o4
                    for (t0, n) in alltiles:
                        bk = P.bank()
                        for jx in range(gs):
                            P.mm(ps(bk, 0, n), wsub(wt, jx, o4 * 128, (o4 + 1) * 128),
                                 actb.v(jx * NT + t0, jx * NT + t0 + n), start=(jx == 0), stop=(jx == gs - 1))
                        resid_add(o, t0, n, bk)
            if FFN_STOP is not None and FFN_STOP >= 2 and gi == FFN_STOP - 2:
                return

    fin = {"done": False}
    for layer in range(N_LAYERS):
        kind, j = layer % 3, layer // 3
        if kind == 0:
            P.memset("dve", carryA.v(0, 16), 0.0)
        if kind == 0 and not (SKIP_MIX or kind in SKIP_KINDS):
            for (t0, n) in TILES[:3]:
                mixer_a(layer, j, [(t0, n, False)])
            mixer_a(layer, j, [(TILES[3][0], TILES[3][1], False), (STILE[0], STILE[1], True)])
        for (t0, n) in TILES + [STILE]:
            sample = (t0 == NPR)
            if SKIP_MIX or kind in SKIP_KINDS or kind == 0:
                continue
            if kind == 2 and C_TILES == 'p0' and t0 != 0:
                continue
            if kind == 2 and C_TILES == 's' and not sample:
                continue
            if kind == 0:
                mixer_a(layer, j, t0, n, sample)
            elif kind == 1:
                mixer_b(layer, t0, n, sample)
            else:
                mixer_c(layer, t0, n, sample)
        A.top = 0
        A_stage[0], A_stage[1] = A.alloc(1024, F32), A.alloc(1024, F32)
        if kind == 0:
            out_rows(lambda c: carryA.v(c * 2, c * 2 + 2), 2, cap_d[j, :, :])
        elif kind == 1:
            out_rows(lambda c: carryB.v(c * 30, c * 30 + 30), 30, cbp_d[:, :])
        else:
            sv = S32.v3(0, 8, 128)
            P.dma("sp", hgp_d[:, :, :].rearrange("h k v -> k h v"), sv.ap, [sv], [])
        if not SKIP_FFN:
            ffn(layer)

    if fin["done"]:
        return done()
    A.top = 0
    A_stage[0], A_stage[1] = A.alloc(1024, F32), A.alloc(1024, F32)
    sq = A.alloc(8 * 512, BF16)
    scr0 = A.alloc(512, F32)
    yf = [A.alloc(8 * 512, F32) for _ in range(2)]
    fi = 0
    for T in range(4):
        yb = yf[fi % 2]
        fi += 1
        rmsnorm(T * 512, 512, R_NFIN, lambda c, yb=yb: yb.v(c * 512, (c + 1) * 512), sq, scr0)
        for q in range(4):
            i = T * 4 + q
            out_rows(lambda c, yb=yb, q=q: yb.v(c * 512 + q * 128, c * 512 + (q + 1) * 128), 128,
                     yp_d[i * 128:(i + 1) * 128, :])
    if STOP_AFTER == 8:
        return done()
    yb = yf[fi % 2]
    rmsnorm(NPR, NS, R_NFIN, lambda c: yb.v(c * NS, (c + 1) * NS), sq, scr0)
    if STOP_AFTER == 9:
        return done()
    out_rows(lambda c: yb.v(c * NS, (c + 1) * NS), NS, ys_d[:, :])

    return done()


_CACHE = {}


def kernel(x_prompt, x_sample, state_conva, state_convb, state_hgrn, norm_mix, a_w_in, a_conv_w, a_w_out,
           b_w_pw1, b_b_pw1, b_dw_w, b_dw_b, b_ln_g, b_ln_b, b_w_pw2, b_b_pw2, c_lower_bounds, c_w_qfig,
           c_gnorm, c_w_out, norm_ffn, ffn_w_gate_up, ffn_w_down, norm_final):
    f = lambda a: np.ascontiguousarray(np.asarray(a, dtype=np.float32))
    nco = 8
    vec = np.zeros((NVEC, D), np.float32)
    vec[R_NMIX:R_NMIX + 4] = f(norm_mix)
    vec[R_NFFN:R_NFFN + 4] = f(norm_ffn)
    vec[R_NFIN] = f(norm_final)
    vec[R_ACONV:R_ACONV + 6] = f(a_conv_w).reshape(6, D)
    vec[R_BB1:R_BB1 + 2] = f(b_b_pw1).reshape(2, D)
    vec[R_BDW:R_BDW + 31] = f(b_dw_w).reshape(31, D)
    vec[R_BDWB] = f(b_dw_b).reshape(D)
    vec[R_LNG] = f(b_ln_g).reshape(D)
    vec[R_LNB] = f(b_ln_b).reshape(D)
    vec[R_BB2] = f(b_b_pw2).reshape(D)
    vec[R_CLB:R_CLB + 4] = f(c_lower_bounds)
    vec[R_GN, 0:128] = f(c_gnorm).reshape(128)
    shared = {
        "vec": vec, "a_w_in": f(a_w_in), "a_w_out": f(a_w_out), "b_w_pw1": f(b_w_pw1)[0], "b_w_pw2": f(b_w_pw2)[0],
        "c_w_qfig": f(c_w_qfig)[0], "c_w_out": f(c_w_out)[0], "ffn_w_gate_up": f(ffn_w_gate_up),
        "ffn_w_down": f(ffn_w_down),
    }
    xp, xs, sa, sb, sh = f(x_prompt), f(x_sample), f(state_conva), f(state_convb), f(state_hgrn)
    in_maps = []
    for c in range(nco):
        s0, s1 = c * NS, (c + 1) * NS
        m = dict(shared)
        m["xp"] = xp[c]
        m["xs"] = np.ascontiguousarray(xs[s0:s1, 0, :])
        m["sta"] = np.ascontiguousarray(sa[:, s0:s1]).reshape(2, 2 * NS, D)
        m["stb"] = np.ascontiguousarray(sb[0, s0:s1]).reshape(NS * 30, D)
        m["sth"] = np.ascontiguousarray(sh[0, s0:s1])
        in_maps.append(m)
    if "nc" not in _CACHE:
        _CACHE["nc"] = build_program()
    nc = _CACHE["nc"]
    if DEBUG_CORES is not None:
        res = run_bass_kernel_spmd(nc, in_maps[:DEBUG_CORES], core_ids=list(range(DEBUG_CORES)))
        r = list(res.results) + [res.results[0]] * (nco - DEBUG_CORES)
    else:
        res = run_bass_kernel_spmd(nc, in_maps, core_ids=list(range(nco)))
        r = res.results
    y_prompt = np.stack([r[c]["yp"] for c in range(nco)], 0)
    y_sample = np.concatenate([r[c]["ys"] for c in range(nco)], 0).reshape(128, 1, D)
    conva_prompt = np.stack([r[c]["cap"] for c in range(nco)], 1)
    conva_sample = np.concatenate([r[c]["cas"].reshape(2, NS, 2, D) for c in range(nco)], 1)
    convb_prompt = np.stack([r[c]["cbp"] for c in range(nco)], 0)[None]
    convb_sample = np.concatenate([r[c]["cbs"] for c in range(nco)], 0)[None]
    hgrn_prompt = np.stack([r[c]["hgp"] for c in range(nco)], 0)[None]
    hgrn_sample = np.concatenate([r[c]["hgs"] for c in range(nco)], 0)[None]
    return tuple(np.asarray(a, np.float32) for a in (y_prompt, y_sample, conva_prompt, conva_sample, convb_prompt,
                                                     convb_sample, hgrn_prompt, hgrn_sample))
```

```python
import numpy as np
import concourse.bass as bass
import concourse.mybir as mybir
from concourse.bass_utils import run_bass_kernel_spmd

F32 = mybir.dt.float32
BF16 = mybir.dt.bfloat16
AF = mybir.ActivationFunctionType
ALU = mybir.AluOpType
AX = mybir.AxisListType

D = 1024
NPR = 2048
NS = 16
NT = NPR + NS
DFF = 2816
TILES = [(0, 512), (512, 512), (1024, 512), (1536, 512)]
STILE = (2048, 16)
R_SLOTS = 6
SLOT = 4096
PREFETCH = 3
SAME_ENG_SYNC = ('pool', 'dve', 'act')
N_LAYERS = 4
STOP_AFTER = None
SKIP_FFN = False
DG_ENG = 'pool'
SKIP_MIX = False
DEBUG_CORES = None
FFN_STOP = None
SKIP_KINDS = ()
C_STOP = None
C_TILES = None
KDMA = 8
ARENA_BYTES = 75776
NVEC = 64
R_NMIX, R_NFFN, R_NFIN, R_ACONV, R_BB1, R_BDW, R_BDWB, R_LNG, R_LNB, R_BB2, R_CLB, R_GN = 0, 4, 8, 9, 15, 17, 48, 49, 50, 51, 52, 56


class Buf:
    __slots__ = ("base", "lo", "hi", "lw", "rd", "rdd", "ov")

    def __init__(self, base, lo, hi):
        self.base, self.lo, self.hi = base, lo, hi
        self.lw = None
        self.rd = {}
        self.rdd = []
        self.ov = []


class V:
    __slots__ = ("ap", "bufs", "wt")

    def __init__(self, ap, bufs, wt=None):
        self.ap, self.bufs, self.wt = ap, bufs, wt


class Op:
    __slots__ = ("eng", "fn", "reads", "writes", "key", "dma", "deps", "tick", "sem", "val", "pre",
                 "needinc", "wreads", "wtile")


class Prog:
    def __init__(self, nc):
        self.nc = nc
        self.ops = []
        self.bufmap = {}
        self.bybase = {}
        self._bank = 0

    def buf(self, base, lo, hi):
        k = (base, lo, hi)
        b = self.bufmap.get(k)
        if b is None:
            b = Buf(base, lo, hi)
            lst = self.bybase.setdefault(base, [])
            for o in lst:
                if o.lo < hi and lo < o.hi:
                    o.ov.append(b)
                    b.ov.append(o)
            lst.append(b)
            self.bufmap[k] = b
        return b

    def add(self, eng, fn, reads, writes, dma=False, register=True):
        op = Op()
        op.eng, op.fn, op.dma = eng, fn, dma
        op.reads = [b for v in reads for b in v.bufs]
        op.writes = [b for v in writes for b in v.bufs]
        op.wreads = [v.wt for v in reads if v.wt is not None]
        op.wtile = None
        op.needinc = False
        op.tick = 0
        op.pre = 0
        op.key = float(len(self.ops))
        if register:
            self.ops.append(op)
        return op

    def bank(self):
        b = self._bank
        self._bank = (b + 1) % 8
        return b

    def mm(self, out, lhsT, rhs, start=True, stop=True, tp=None):
        kw = {} if tp is None else {"tile_position": tp}
        self.add("pe", lambda e: e.matmul(out.ap, lhsT=lhsT.ap, rhs=rhs.ap, start=start, stop=stop, **kw),
                 [lhsT, rhs], [out])

    def tr(self, out, in_, ident):
        self.add("pe", lambda e: e.transpose(out.ap, in_.ap, ident.ap), [in_, ident], [out])

    def act(self, out, in_, func, bias=None, scale=None):
        reads = [in_]
        kw = {}
        if bias is not None:
            if isinstance(bias, V):
                reads.append(bias)
                kw["bias"] = bias.ap
            else:
                kw["bias"] = float(bias)
        if scale is not None:
            if isinstance(scale, V):
                reads.append(scale)
                kw["scale"] = scale.ap
            else:
                kw["scale"] = float(scale)
        self.add("act", lambda e: e.activation(out=out.ap, in_=in_.ap, func=func, **kw), reads, [out])

    def tt(self, eng, out, in0, in1, op):
        self.add(eng, lambda e: e.tensor_tensor(out=out.ap, in0=in0.ap, in1=in1.ap, op=op), [in0, in1], [out])

    def stt(self, out, in0, scalar, in1, op0, op1):
        reads = [in0, in1]
        s = scalar
        if isinstance(scalar, V):
            reads.append(scalar)
            s = scalar.ap
        self.add("dve", lambda e: e.scalar_tensor_tensor(out=out.ap, in0=in0.ap, scalar=s, in1=in1.ap,
                                                         op0=op0, op1=op1), reads, [out])

    def ts(self, eng, out, in0, s1, op0, s2=None, op1=None):
        reads = [in0]
        a1, a2 = s1, s2
        if isinstance(s1, V):
            reads.append(s1)
            a1 = s1.ap
        if isinstance(s2, V):
            reads.append(s2)
            a2 = s2.ap
        if op1 is None:
            self.add(eng, lambda e: e.tensor_scalar(out=out.ap, in0=in0.ap, scalar1=a1, scalar2=None, op0=op0),
                     reads, [out])
        else:
            self.add(eng, lambda e: e.tensor_scalar(out=out.ap, in0=in0.ap, scalar1=a1, scalar2=a2, op0=op0,
                                                    op1=op1), reads, [out])

    def cp(self, eng, out, in_):
        if eng == "act":
            self.add("act", lambda e: e.copy(out=out.ap, in_=in_.ap), [in_], [out])
        else:
            self.add(eng, lambda e: e.tensor_copy(out=out.ap, in_=in_.ap), [in_], [out])

    def recip(self, out, in_):
        self.add("dve", lambda e: e.reciprocal(out=out.ap, in_=in_.ap), [in_], [out])

    def reduce_add(self, out, in_):
        self.add("dve", lambda e: e.tensor_reduce(out=out.ap, in_=in_.ap, axis=AX.X, op=ALU.add), [in_], [out])

    def memset(self, eng, out, val):
        self.add(eng, lambda e: e.memset(out.ap, val), [], [out])

    def asel(self, out, in_, pattern, cmp, fill, base, cm):
        self.add("pool", lambda e: e.affine_select(out=out.ap, in_=in_.ap, pattern=pattern, compare_op=cmp,
                                                   fill=fill, base=base, channel_multiplier=cm), [in_], [out])

    def dma(self, q, out_ap, in_ap, reads, writes, register=True):
        return self.add(q, lambda e: e.dma_start(out=out_ap, in_=in_ap), reads, writes, dma=True,
                        register=register)

    def finalize(self, extra_ops):
        nc = self.nc
        ops = sorted(self.ops + extra_ops, key=lambda o: o.key)
        slot_cur = {}
        for op in ops:
            deps = set()
            for b in op.reads:
                for o in [b] + b.ov:
                    if o.lw is not None:
                        deps.add(o.lw)
            for b in op.writes:
                for o in [b] + b.ov:
                    if o.lw is not None:
                        deps.add(o.lw)
                    deps.update(o.rd.values())
                    deps.update(o.rdd)
            if op.wtile is not None:
                slot_cur[op.wtile[0]] = op.wtile[1]
            for (s, i) in op.wreads:
                assert slot_cur.get(s) == i, ("weight ring hazard", s, i, slot_cur.get(s))
            for b in op.reads:
                if op.dma:
                    b.rdd.append(op)
                else:
                    b.rd[op.eng] = op
            for b in op.writes:
                b.lw = op
                b.rd = {}
                b.rdd = []
            deps.discard(op)
            keep = []
            for d in deps:
                if d.eng == op.eng and not d.dma and not op.dma:
                    if op.eng == "pe" or op.eng not in SAME_ENG_SYNC:
                        continue
                keep.append(d)
            op.deps = keep
            for d in keep:
                if not d.dma:
                    d.needinc = True
        cnt = {}
        qcnt = {}
        for op in ops:
            if op.dma:
                n = qcnt.get(op.eng, 0)
                qcnt[op.eng] = n + 1
                op.sem = (op.eng, n % KDMA)
                op.val = 16 * (n // KDMA + 1)
                op.pre = 16 * (n // KDMA)
            elif op.needinc:
                cnt[op.eng] = cnt.get(op.eng, 0) + 1
                op.tick = cnt[op.eng]
        self.sorted_ops = ops
        self.qcnt = qcnt
        return ops

    def emit(self):
        nc = self.nc
        ops = self.sorted_ops
        engs = {"pe": [], "act": [], "dve": [], "pool": [], "sp": []}
        for op in ops:
            engs[op.eng].append(op)
        import contextlib
        with contextlib.ExitStack() as es:
            esem = {k: es.enter_context(nc.semaphore("e_" + k)) for k in ("pe", "act", "dve", "pool")}
            dsem = {}
            for q in ("sp", "pool"):
                for i in range(KDMA):
                    dsem[(q, i)] = es.enter_context(nc.semaphore("d_%s%d" % (q, i)))
            block = es.enter_context(nc.Block())
            qcnt = self.qcnt

            def run(e, name):
                waited = {}
                for op in engs[name]:
                    need = {}
                    for d in op.deps:
                        if d.dma:
                            k = ("d",) + d.sem
                            s, val = dsem[d.sem], d.val
                        else:
                            k = ("e", d.eng)
                            s, val = esem[d.eng], d.tick
                        if need.get(k, (None, 0))[1] < val:
                            need[k] = (s, val)
                    if op.dma and op.pre:
                        k = ("d",) + op.sem
                        if need.get(k, (None, 0))[1] < op.pre:
                            need[k] = (dsem[op.sem], op.pre)
                    for k, (s, val) in need.items():
                        if waited.get(k, 0) < val:
                            e.wait_ge(s, val)
                            waited[k] = val
                    ins = op.fn(e)
                    if op.dma:
                        ins.then_inc(dsem[op.sem], 16)
                    elif op.needinc:
                        ins.then_inc(esem[name], 1)
                if name in ("sp", "pool"):
                    n = qcnt.get(name, 0)
                    for i in range(min(KDMA, n)):
                        tot = (n - i + KDMA - 1) // KDMA
                        e.wait_ge(dsem[(name, i)], 16 * tot)

            @block.tensor
            def _(e):
                run(e, "pe")

            @block.scalar
            def _(e):
                run(e, "act")

            @block.vector
            def _(e):
                run(e, "dve")

            @block.gpsimd
            def _(e):
                run(e, "pool")

            @block.sync
            def _(e):
                run(e, "sp")


class Mem:
    def __init__(self, P, base, ap, boff, esz):
        self.P, self.base, self.ap, self.boff, self.esz = P, base, ap, boff, esz

    def b(self, lo, hi):
        return self.P.buf(self.base, self.boff + lo * self.esz, self.boff + hi * self.esz)

    def v(self, lo, hi, p0=0, p1=128):
        return V(self.ap[p0:p1, lo:hi], [self.b(lo, hi)])

    def v3(self, lo, a, b, p0=0, p1=128):
        return V(self.ap[p0:p1, lo:lo + a * b].rearrange("p (a b) -> p a b", a=a), [self.b(lo, lo + a * b)])


class Arena:
    def __init__(self, P, ap_f32, base, nbytes):
        self.P, self.ap, self.base, self.nbytes = P, ap_f32, base, nbytes
        self.top = 0

    def alloc(self, n, dt):
        esz = 4 if dt == F32 else 2
        nb = (n * esz + 3) // 4 * 4
        off = self.top
        assert off + nb <= self.nbytes, ("arena overflow", off, nb, self.nbytes)
        self.top += nb
        ap = self.ap[:, off // 4:(off + nb) // 4]
        if dt != F32:
            ap = ap.bitcast(dt)
        return Mem(self.P, self.base, ap, off, esz)


class WStream:
    def __init__(self, P, ring):
        self.P, self.ring = P, ring
        self.n = 0
        self.first_use = []
        self.dmas = []

    def get(self, parts):
        P = self.P
        i = self.n
        self.n += 1
        slot = i % R_SLOTS
        base = slot * SLOT
        sbuf = self.ring.b(base, base + SLOT)
        self.first_use.append(len(P.ops))
        off = 0
        outs = []
        for pi, (src, a, b) in enumerate(parts):
            dst = self.ring.ap[:, base + off:base + off + a * b].rearrange("p (a b) -> p a b", a=a)
            v = V(dst, [sbuf], wt=(slot, i))
            op = P.dma("pool", dst, src, [], [V(dst, [sbuf])], register=False)
            op.wtile = (slot, i)
            op.key = (i, pi)
            self.dmas.append(op)
            outs.append(v)
            off += a * b
        assert off <= SLOT
        return outs

    def finish(self):
        for op in self.dmas:
            i, pi = op.key
            j = max(0, i - PREFETCH)
            op.key = self.first_use[j] - 0.5 + (i * 4 + pi) * 1e-6
        return self.dmas


def build_program():
    nc = bass.Bass("TRN2", target_bir_lowering=False)
    P = Prog(nc)

    def din(name, shape):
        return nc.dram_tensor(name, shape, F32, kind="ExternalInput").ap()

    def dout(name, shape):
        return nc.dram_tensor(name, shape, F32, kind="ExternalOutput").ap()

    xp_d = din("xp", [NPR, D])
    xs_d = din("xs", [NS, D])
    sta_d = din("sta", [2, 2 * NS, D])
    stb_d = din("stb", [NS * 30, D])
    sth_d = din("sth", [NS, 8, 128, 128])
    vec_d = din("vec", [NVEC, D])
    a_win = din("a_w_in", [2, D, 3 * D])
    a_wout = din("a_w_out", [2, D, D])
    b_w1 = din("b_w_pw1", [D, 2 * D])
    b_w2 = din("b_w_pw2", [D, D])
    c_wq = din("c_w_qfig", [D, 4 * D])
    c_wo = din("c_w_out", [D, D])
    f_wgu = din("ffn_w_gate_up", [4, D, 2 * DFF])
    f_wd = din("ffn_w_down", [4, DFF, D])
    yp_d = dout("yp", [NPR, D])
    ys_d = dout("ys", [NS, D])
    cap_d = dout("cap", [2, 2, D])
    cas_d = dout("cas", [2, 2 * NS, D])
    cbp_d = dout("cbp", [30, D])
    cbs_d = dout("cbs", [NS, 30, D])
    hgp_d = dout("hgp", [8, 128, 128])
    hgs_d = dout("hgs", [NS, 8, 128, 128])

    import contextlib
    es = contextlib.ExitStack()
    xt = es.enter_context(nc.sbuf_tensor("x", [128, 8, NT], F32))
    ringt = es.enter_context(nc.sbuf_tensor("ring", [128, R_SLOTS * SLOT], BF16))
    arenat = es.enter_context(nc.sbuf_tensor("arena", [128, ARENA_BYTES // 4], F32))
    CONST_F32 = 5396
    constt = es.enter_context(nc.sbuf_tensor("const", [128, CONST_F32], F32))
    pst = es.enter_context(nc.psum_tensor("ps", [128, 8, 512], F32))

    ring = Mem(P, "ring", ringt[:], 0, 2)
    W = WStream(P, ring)
    A = Arena(P, arenat[:], "arena", ARENA_BYTES)
    C = Arena(P, constt[:], "const", CONST_F32 * 4)

    def xv(c, t0, n):
        return V(xt[:, c, t0:t0 + n], [P.buf("x", (c * NT + t0) * 4, (c * NT + t0 + n) * 4)])

    def xall(t0, n, c0=0, c1=8):
        return V(xt[:, c0:c1, t0:t0 + n],
                 [P.buf("x", (c * NT + t0) * 4, (c * NT + t0 + n) * 4) for c in range(c0, c1)])

    def ps(bank, lo=0, hi=512, p0=0, p1=128):
        return V(pst[p0:p1, bank, lo:hi], [P.buf("ps", bank * 2048 + lo * 4, bank * 2048 + hi * 4)])

    ident = C.alloc(128, F32)
    maskbd = C.alloc(128, F32)
    trirev = C.alloc(128, F32)
    ones_bf = C.alloc(128, BF16)
    CF = C.alloc(8 * NVEC, F32)
    LBB = C.alloc(1024, F32)
    OMLB = C.alloc(1024, F32)
    LB = C.alloc(8, F32)
    OML = C.alloc(8, F32)
    carryA = C.alloc(16, F32)
    carryB = C.alloc(240, F32)
    S32 = C.alloc(1024, F32)
    SBF = C.alloc(2048, BF16)
    sbf_par = [0] * 8
    ind4 = C.alloc(4, F32)
    identb = C.alloc(128, BF16)

    def cf(c, r):
        return V(CF.ap[:, c * NVEC + r:c * NVEC + r + 1], [CF.b(c * NVEC, (c + 1) * NVEC)])

    iv = ident.v(0, 128)
    P.memset("pool", iv, 0.0)
    P.asel(iv, iv, [[-1, 128]], ALU.not_equal, 1.0, 0, 1)
    P.cp("dve", identb.v(0, 128), iv)
    mv = maskbd.v(0, 128)
    P.memset("pool", mv, 1.0)
    P.asel(mv, mv, [[1, 128]], ALU.is_ge, 0.0, 0, -1)
    for j in range(4):
        sv = maskbd.v(32 * j, 32 * j + 32)
        P.asel(sv, sv, [[0, 32]], ALU.is_ge, 0.0, -32 * j, 1)
    rv = trirev.v(0, 128)
    P.memset("pool", rv, 1.0)
    P.asel(rv, rv, [[-1, 128]], ALU.is_gt, 0.0, 0, 1)
    for j in range(4):
        sv = trirev.v(32 * j, 32 * j + 32)
        P.asel(sv, sv, [[0, 32]], ALU.is_gt, 0.0, 32 * j + 32, -1)
    i4 = ind4.v(0, 4)
    P.memset("pool", i4, 1.0)
    P.asel(i4, i4, [[-32, 4]], ALU.is_ge, 0.0, 0, 1)
    P.asel(i4, i4, [[32, 4]], ALU.is_ge, 0.0, 31, -1)
    P.memset("dve", ones_bf.v(0, 128), 1.0)
    P.memset("dve", carryA.v(0, 16), 0.0)
    P.memset("dve", carryB.v(0, 240), 0.0)
    P.memset("dve", S32.v(0, 1024), 0.0)
    P.memset("dve", SBF.v(0, 2048), 0.0)

    rr = {"s": 0, "o": 0}

    def done():
        P.finalize(W.finish())
        P.emit()
        es.close()
        return nc

    if STOP_AFTER == 0:
        return done()

    def in_rows(dram_rows, n, dst4):
        st = A_stage[rr["s"] % 2]
        rr["s"] += 1
        sv_ = st.v(0, 1024, 0, n)
        P.dma("sp", sv_.ap, dram_rows, [], [sv_])
        for cg in range(2):
            bk = P.bank()
            for c4 in range(4):
                c = cg * 4 + c4
                P.tr(ps(bk, c4 * 128, c4 * 128 + n), V(st.ap[0:n, c * 128:(c + 1) * 128], [st.b(0, 1024)]),
                     V(ident.ap[0:n, 0:n], [ident.b(0, 128)]))
            src = V(pst[:, bk, :].rearrange("p (a b) -> p a b", a=4)[:, :, 0:n], [P.buf("ps", bk * 2048, bk * 2048 + 2048)])
            P.cp("act" if cg == 0 else "dve", dst4(cg), src)

    def out_rows(src, n, dram_rows):
        st = A_stage[rr["s"] % 2]
        rr["s"] += 1
        if n < 128:
            pad = A.alloc(1024, F32)
            P.memset("dve", pad.v(0, 1024), 0.0)
            for c in range(8):
                P.cp("dve", pad.v(c * 128, c * 128 + n), src(c))
            src = lambda c: pad.v(c * 128, (c + 1) * 128)
        for cg in range(2):
            bk = P.bank()
            for c4 in range(4):
                c = cg * 4 + c4
                P.tr(ps(bk, c4 * 128, c4 * 128 + 128), src(c), iv)
            P.cp("act" if cg == 0 else "dve", st.v(cg * 512, cg * 512 + 512, 0, n), ps(bk, 0, 512, 0, n))
        sv_ = st.v(0, 1024, 0, n)
        P.dma("sp", dram_rows, sv_.ap, [sv_], [])

    A.top = 0
    A_stage = [A.alloc(1024, F32), A.alloc(1024, F32)]
    clb = A.alloc(4096, F32)
    tmp1 = A.alloc(1024, F32)
    st = A_stage[0]
    sv_ = st.v(0, 1024, 0, NVEC)
    P.dma("sp", sv_.ap, vec_d[:, :], [], [sv_])
    bk = P.bank()
    for c in range(8):
        P.tr(ps(bk, c * NVEC, (c + 1) * NVEC), V(st.ap[0:NVEC, c * 128:(c + 1) * 128], [st.b(0, 1024)]),
             V(ident.ap[0:NVEC, 0:NVEC], [ident.b(0, 128)]))
    P.cp("dve", CF.v(0, 8 * NVEC), ps(bk))
    rr["s"] = 1
    if STOP_AFTER == 1:
        return done()
    e4 = A.alloc(32, F32)
    ssum = A.alloc(8, F32)
    CF3 = CF.ap[:, :].rearrange("p (c r) -> p c r", c=8)
    P.act(V(e4.ap[:, 0:32].rearrange("p (c r) -> p c r", c=8), [e4.b(0, 32)]),
          V(CF3[:, :, R_CLB:R_CLB + 4], [CF.b(0, 8 * NVEC)]), AF.Exp)
    e43 = e4.ap[:, 0:32].rearrange("p (c r) -> p c r", c=8)
    P.reduce_add(ssum.v(0, 8), V(e43, [e4.b(0, 32)]))
    P.recip(ssum.v(0, 8), ssum.v(0, 8))
    P.tt("dve", LB.v(0, 8), V(e43[:, :, 1], [e4.b(0, 32)]), V(e43[:, :, 2], [e4.b(0, 32)]), ALU.add)
    P.tt("dve", LB.v(0, 8), LB.v(0, 8), ssum.v(0, 8), ALU.mult)
    P.ts("dve", OML.v(0, 8), LB.v(0, 8), -1.0, ALU.mult, 1.0, ALU.add)
    if STOP_AFTER == 2:
        return done()
    cv = clb.v(0, 4096)
    P.dma("sp", clb.ap[:, 0:4096].rearrange("p (a b) -> p a b", a=4), vec_d[R_CLB:R_CLB + 4, :].partition_broadcast(128),
          [], [cv])
    P.act(cv, cv, AF.Exp)
    P.tt("dve", LBB.v(0, 1024), clb.v(1024, 2048), clb.v(2048, 3072), ALU.add)
    P.tt("dve", tmp1.v(0, 1024), clb.v(0, 1024), clb.v(3072, 4096), ALU.add)
    P.tt("dve", tmp1.v(0, 1024), tmp1.v(0, 1024), LBB.v(0, 1024), ALU.add)
    P.recip(tmp1.v(0, 1024), tmp1.v(0, 1024))
    P.tt("dve", LBB.v(0, 1024), LBB.v(0, 1024), tmp1.v(0, 1024), ALU.mult)
    P.ts("dve", OMLB.v(0, 1024), LBB.v(0, 1024), -1.0, ALU.mult, 1.0, ALU.add)
    if STOP_AFTER == 3:
        return done()
    xq = {"next": 0}

    def load_next_block():
        i = xq["next"]
        if i < 16:
            in_rows(xp_d[i * 128:(i + 1) * 128, :], 128, lambda cg, i=i: xall(i * 128, 128, cg * 4, cg * 4 + 4))
        elif i == 16:
            in_rows(xs_d[:, :], NS, lambda cg: xall(NPR, NS, cg * 4, cg * 4 + 4))
        xq["next"] = i + 1

    lazy_x = N_LAYERS >= 1 and not SKIP_MIX and 0 not in SKIP_KINDS
    for i in range(4 if lazy_x else 17):
        load_next_block()

    if STOP_AFTER == 4:
        return done()
    def rmsnorm(t0, n, grow, hout, sq, scr, out_f32=False):
        P.act(sq.v3(0, 8, n), xall(t0, n), AF.Square)
        bk = P.bank()
        for c in range(8):
            P.mm(ps(bk, 0, n), ones_bf.v(0, 128), sq.v(c * n, (c + 1) * n), start=(c == 0), stop=(c == 7))
        rs = scr.v(0, n)
        P.act(rs, ps(bk, 0, n), AF.Ln, bias=1e-6, scale=1.0 / D)
        P.act(rs, rs, AF.Exp, scale=-0.5)
        for c in range(8):
            P.stt(hout(c), xv(c, t0, n), cf(c, grow), rs, ALU.mult, ALU.mult)

    def resid_add(o, t0, n, bk, bias=None):
        if bias is None:
            P.tt("dve", xv(o, t0, n), xv(o, t0, n), ps(bk, 0, n), ALU.add)
        else:
            P.stt(xv(o, t0, n), ps(bk, 0, n), bias, xv(o, t0, n), ALU.add, ALU.add)

    def out_proj(wd, zmem, t0, n, bias_row=None):
        for half in range(2):
            (wt,) = W.get([(wd[:, half * 512:(half + 1) * 512].rearrange("(k p) n -> p k n", p=128), 8, 512)])
            for o4 in range(4):
                o = half * 4 + o4
                bk = P.bank()
                for k in range(8):
                    P.mm(ps(bk, 0, n), V(wt.ap[:, k, o4 * 128:(o4 + 1) * 128], wt.bufs, wt.wt),
                         zmem.v(k * n, (k + 1) * n), start=(k == 0), stop=(k == 7))
                resid_add(o, t0, n, bk, None if bias_row is None else cf(o, bias_row))

    def wsub(wt, k, lo, hi):
        return V(wt.ap[:, k, lo:hi], wt.bufs, wt.wt)

    def proj(bk, wt, hmem, n, col0=0, ncol=128):
        for k in range(8):
            P.mm(ps(bk, 0, n), wsub(wt, k, col0, col0 + ncol), hmem.v(k * n, (k + 1) * n), start=(k == 0), stop=(k == 7))

    def mixer_a(layer, j, tiles):
        A.top = 0
        stg = [A.alloc(1024, F32), A.alloc(1024, F32)]
        A_stage[0], A_stage[1] = stg
        if any(sm_ for (_, _, sm_) in tiles):
            while xq["next"] <= 16:
                load_next_block()
        ctxs = []
        for (t0, n, sample) in tiles:
            cx = {"t0": t0, "n": n, "sample": sample}
            cx["h"] = A.alloc(8 * n, BF16)
            cx["sq"] = A.alloc(8 * n, BF16)
            cx["z"] = A.alloc(8 * n, BF16)
            cx["scr"] = [A.alloc(n, F32) for _ in range(4)]
            cx["ub"] = [A.alloc(n + 4, F32) for _ in range(2)]
            if sample:
                cx["sta"] = A.alloc(8 * 32, F32)
                cx["newa"] = A.alloc(8 * 32, F32)
                sta_ = cx["sta"]
                in_rows(sta_d[j, :, :], 32, lambda cg, sta_=sta_: sta_.v3(cg * 128, 4, 32))
            h_ = cx["h"]
            rmsnorm(t0, n, R_NMIX + layer, lambda c, h_=h_, n=n: h_.v(c * n, (c + 1) * n), cx["sq"], cx["scr"][0])
            ctxs.append(cx)
        wi = a_win[j]

        def chunk_body(cx, c, wts):
            t0, n, sample = cx["t0"], cx["n"], cx["sample"]
            h, z, scr, ub = cx["h"], cx["z"], cx["scr"], cx["ub"]
            cc0 = (c % 4) * 128
            bb, bc, bh = P.bank(), P.bank(), P.bank()
            proj(bb, wts[0], h, n, cc0)
            proj(bc, wts[1], h, n, cc0)
            proj(bh, wts[2], h, n, cc0)
            cgs = scr[1 + c % 2].v(0, n)
            tmp = scr[3].v(0, n)
            P.cp("act", cgs, ps(bc, 0, n))
            w0, w1, w2 = (cf(c, R_ACONV + 3 * j + r) for r in range(3))
            if not sample:
                u = ub[c % 2]
                P.cp("dve", u.v(0, 2), carryA.v(c * 2, c * 2 + 2))
                P.tt("dve", u.v(2, 2 + n), cgs, ps(bh, 0, n), ALU.mult)
                P.cp("dve", carryA.v(c * 2, c * 2 + 2), u.v(n, n + 2))
                P.ts("dve", tmp, u.v(0, n), w0, ALU.mult)
                P.stt(tmp, u.v(1, n + 1), w1, tmp, ALU.mult, ALU.add)
                P.stt(tmp, u.v(2, n + 2), w2, tmp, ALU.mult, ALU.add)
            else:
                sta, newa = cx["sta"], cx["newa"]
                u = ub[c % 2].v(0, n)
                P.tt("dve", u, cgs, ps(bh, 0, n), ALU.mult)
                s3 = sta.ap[:, c * 32:(c + 1) * 32].rearrange("p (s r) -> p s r", r=2)
                n3 = newa.ap[:, c * 32:(c + 1) * 32].rearrange("p (s r) -> p s r", r=2)
                sb_, nb_ = [sta.b(c * 32, c * 32 + 32)], [newa.b(c * 32, c * 32 + 32)]
                P.ts("dve", tmp, V(s3[:, :, 0], sb_), w0, ALU.mult)
                P.stt(tmp, V(s3[:, :, 1], sb_), w1, tmp, ALU.mult, ALU.add)
                P.stt(tmp, u, w2, tmp, ALU.mult, ALU.add)
                P.cp("dve", V(n3[:, :, 0], nb_), V(s3[:, :, 1], sb_))
                P.cp("dve", V(n3[:, :, 1], nb_), u)
            P.tt("dve", z.v(c * n, (c + 1) * n), tmp, ps(bb, 0, n), ALU.mult)

        for c in range(8):
            if c % 4 == 0:
                cq = c // 4
                wts = [W.get([(wi[:, g * 1024 + cq * 512:g * 1024 + (cq + 1) * 512].rearrange("(k p) n -> p k n", p=128), 8, 512)])[0]
                       for g in range(3)]
            for cx in ctxs:
                chunk_body(cx, c, wts)
            if layer == 0 and c % 2 == 1 and xq["next"] <= 16:
                load_next_block()
        for half in range(2):
            (wt,) = W.get([(a_wout[j][:, half * 512:(half + 1) * 512].rearrange("(k p) n -> p k n", p=128), 8, 512)])
            for o4 in range(4):
                o = half * 4 + o4
                for cx in ctxs:
                    n = cx["n"]
                    bk = P.bank()
                    for k in range(8):
                        P.mm(ps(bk, 0, n), V(wt.ap[:, k, o4 * 128:(o4 + 1) * 128], wt.bufs, wt.wt),
                             cx["z"].v(k * n, (k + 1) * n), start=(k == 0), stop=(k == 7))
                    resid_add(o, cx["t0"], n, bk)
        for cx in ctxs:
            if cx["sample"]:
                newa = cx["newa"]
                out_rows(lambda c, newa=newa: newa.v(c * 32, c * 32 + 32), 32, cas_d[j, :, :])

    def mixer_b(layer, t0, n, sample):
        A.top = 0
        if sample:
            stg = [A.alloc(1024, F32), A.alloc(1024, F32)]
            A_stage[0], A_stage[1] = stg
        h = A.alloc(8 * n, BF16)
        sq = A.alloc(8 * n, BF16)
        ybf = A.alloc(8 * n, BF16)
        z = h
        y = A.alloc(8 * n, F32)
        scr = [A.alloc(512, F32) for _ in range(5)]
        ub = [A.alloc(544, F32) for _ in range(2)]
        if not sample:
            ubfb = [A.alloc(544, BF16) for _ in range(2)]
            dgb = [A.alloc(31 * 128, BF16) for _ in range(2)]
        if sample:
            stb = A.alloc(8 * 480, F32)
            us = A.alloc(8 * NS, F32)
            prod = A.alloc(480, F32)
            for q in range(4):
                in_rows(stb_d[q * 120:(q + 1) * 120, :], 120,
                        lambda cg, q=q: V(stb.ap[:, :].rearrange("p (c m) -> p c m", c=8)[:, cg * 4:cg * 4 + 4, q * 120:(q + 1) * 120],
                                          [stb.b(c * 480 + q * 120, c * 480 + (q + 1) * 120) for c in range(cg * 4, cg * 4 + 4)]))
            P.dma("sp", cbs_d[:, 0:29, :], stb_d[:, :].rearrange("(s r) d -> s r d", r=30)[:, 1:30, :], [], [])
        rmsnorm(t0, n, R_NMIX + layer, lambda c: h.v(c * n, (c + 1) * n), sq, scr[0])
        wst = {}

        def s1(c):
            if c % 4 == 0:
                cq = c // 4
                wst["w"] = [W.get([(b_w1[:, g * 1024 + cq * 512:g * 1024 + (cq + 1) * 512].rearrange("(k p) n -> p k n", p=128), 8, 512)])[0]
                            for g in range(2)]
            wts = wst["w"]
            cc0 = (c % 4) * 128
            ba, bg = P.bank(), P.bank()
            proj(ba, wts[0], h, n, cc0)
            proj(bg, wts[1], h, n, cc0)
            sig = scr[1 + c % 2].v(0, n)
            P.act(sig, ps(bg, 0, n), AF.Sigmoid, bias=cf(c, R_BB1 + 1))
            yc = y.v(c * n, (c + 1) * n)
            if not sample:
                ubf = ubfb[c % 2]
                P.cp("dve", ubf.v(0, 30), carryB.v(c * 30, c * 30 + 30))
                P.stt(ubf.v(30, 30 + n), ps(ba, 0, n), cf(c, R_BB1), sig, ALU.add, ALU.mult)
                P.stt(carryB.v(c * 30, c * 30 + 30), ps(ba, n - 30, n), cf(c, R_BB1),
                      V(sig.ap[:, n - 30:n], sig.bufs), ALU.add, ALU.mult)
                dg = dgb[c % 2]
                P.tt(DG_ENG, dg.v3(0, 31, 128),
                     V(identb.ap[:, 0:128].unsqueeze(1).to_broadcast([128, 31, 128]), [identb.b(0, 128)]),
                     V(CF.ap[:, c * NVEC + R_BDW:c * NVEC + R_BDW + 31].unsqueeze(2).to_broadcast([128, 31, 128]),
                       [CF.b(c * NVEC, (c + 1) * NVEC)]), ALU.mult)
            else:
                u = us.v(c * NS, (c + 1) * NS)
                P.stt(u, ps(ba, 0, n), cf(c, R_BB1), sig, ALU.add, ALU.mult)
                wv = V(CF.ap[:, c * NVEC + R_BDW:c * NVEC + R_BDW + 30].unsqueeze(1).to_broadcast([128, NS, 30]),
                       [CF.b(c * NVEC, (c + 1) * NVEC)])
                P.tt("dve", prod.v3(0, NS, 30), stb.v3(c * 480, NS, 30), wv, ALU.mult)
                red = scr[3].v(0, NS)
                P.reduce_add(red, prod.v3(0, NS, 30))
                P.stt(yc, u, cf(c, R_BDW + 30), red, ALU.mult, ALU.add)
                P.ts("dve", yc, yc, cf(c, R_BDWB), ALU.add)

        def s2(c):
            yc = y.v(c * n, (c + 1) * n)
            ubf, dg = ubfb[c % 2], dgb[c % 2]
            by = P.bank()
            for jj in range(31):
                P.mm(ps(by, 0, n), dg.v(jj * 128, (jj + 1) * 128), ubf.v(jj, jj + n), start=(jj == 0), stop=(jj == 30))
            P.act(yc, ps(by, 0, n), AF.Identity, bias=cf(c, R_BDWB))

        if sample:
            for c in range(8):
                s1(c)
        else:
            s1(0)
            for c in range(8):
                if c + 1 < 8:
                    s1(c + 1)
                s2(c)
        P.act(sq.v3(0, 8, n), y.v3(0, 8, n), AF.Square)
        P.cp("dve", ybf.v3(0, 8, n), y.v3(0, 8, n))
        b1, b2 = P.bank(), P.bank()
        for c in range(8):
            P.mm(ps(b1, 0, n), ones_bf.v(0, 128), ybf.v(c * n, (c + 1) * n), start=(c == 0), stop=(c == 7))
        for c in range(8):
            P.mm(ps(b2, 0, n), ones_bf.v(0, 128), sq.v(c * n, (c + 1) * n), start=(c == 0), stop=(c == 7))
        mean, msq, var = scr[0].v(0, n), scr[3].v(0, n), scr[4].v(0, n)
        P.ts("dve", mean, ps(b1, 0, n), 1.0 / D, ALU.mult)
        P.tt("dve", msq, mean, mean, ALU.mult)
        P.stt(var, ps(b2, 0, n), 1.0 / D, msq, ALU.mult, ALU.subtract)
        P.act(var, var, AF.Ln, bias=1e-5)
        P.act(var, var, AF.Exp, scale=-0.5)
        for c in range(8):
            yc = y.v(c * n, (c + 1) * n)
            P.tt("dve", yc, yc, mean, ALU.subtract)
            P.tt("dve", yc, yc, var, ALU.mult)
            P.act(z.v(c * n, (c + 1) * n), yc, AF.Silu, bias=cf(c, R_LNB), scale=cf(c, R_LNG))
        w2t = [W.get([(b_w2[:, hf * 512:(hf + 1) * 512].rearrange("(k p) n -> p k n", p=128), 8, 512)])[0]
               for hf in range(2)]
        for k in range(8):
            for o in range(8):
                P.mm(ps(o, 0, n), wsub(w2t[o // 4], k, (o % 4) * 128, (o % 4 + 1) * 128), z.v(k * n, (k + 1) * n),
                     start=(k == 0), stop=(k == 7))
        for o in range(8):
            resid_add(o, t0, n, o, cf(o, R_BB2))
        if sample:
            out_rows(lambda c: us.v(c * NS, (c + 1) * NS), NS, cbs_d[:, 29, :])

    def mixer_c(layer, t0, n, sample):
        A.top = 0
        h = A.alloc(8 * n, BF16)
        z = A.alloc(8 * n, BF16)
        sq = z
        nscr = 6
        scrp = [A.alloc(512, F32) for _ in range(nscr)]
        sc = {"i": 0}

        def scratch():
            m = scrp[sc["i"] % nscr]
            sc["i"] += 1
            return m

        rmsnorm(t0, n, R_NMIX + layer, lambda c: h.v(c * n, (c + 1) * n), sq, scratch())
        if C_STOP == 0:
            return
        osq = A.alloc(512, BF16)
        qt = A.alloc(512, BF16)
        kt = A.alloc(512, BF16)
        if not sample:
            nst = n // 128
            logf = A.alloc(nst * 1024, F32)
            khat = A.alloc(nst * 1024, BF16)
            vtok = A.alloc(nst * 1024, BF16)
            scm = [A.alloc(128, BF16) for _ in range(4)]
            vmb = [A.alloc(512, BF16) for _ in range(2)]
            epd = [A.alloc(512, F32) for _ in range(2)]
            qtd = [qt, A.alloc(512, BF16)]
            ktd = [kt, A.alloc(512, BF16)]
            its = [(hg, st_) for hg in range(2) for st_ in range(nst)]
            prs = [its[i:i + 2] for i in range(0, len(its), 2)]
            wtok = {}
            tst = {}

            def tbanks(p, i):
                return (p % 2) * 4 + 2 * i, (p % 2) * 4 + 2 * i + 1

            def PA(p):
                for i, (hg, st_) in enumerate(prs[p]):
                    if hg not in wtok:
                        (wf_,) = W.get([(c_wq[:, 1024 + hg * 512:1024 + (hg + 1) * 512].rearrange("(k p) n -> p k n", p=128), 8, 512)])
                        (wi_,) = W.get([(c_wq[:, 2048 + hg * 512:2048 + (hg + 1) * 512].rearrange("(k p) n -> p k n", p=128), 8, 512)])
                        wtok[hg] = (wf_, wi_)
                    wf_, wi_ = wtok[hg]
                    bf_, bi = tbanks(p, i)
                    for k in range(8):
                        P.mm(ps(bf_), h.v(k * n + st_ * 128, k * n + (st_ + 1) * 128), wsub(wf_, k, 0, 512),
                             start=(k == 0), stop=(k == 7))
                    for k in range(8):
                        P.mm(ps(bi), h.v(k * n + st_ * 128, k * n + (st_ + 1) * 128), wsub(wi_, k, 0, 512),
                             start=(k == 0), stop=(k == 7))

            def SA(p):
                for i, (hg, st_) in enumerate(prs[p]):
                    bf_, bi = tbanks(p, i)
                    lo = st_ * 1024 + hg * 512
                    s1 = scratch().v(0, 512)
                    tst[(p, i)] = [s1]
                    P.act(s1, ps(bf_), AF.Sigmoid)
                    P.cp("act", vtok.v(lo, lo + 512), ps(bi))
                    P.tt("dve", s1, s1, OMLB.v(hg * 512, (hg + 1) * 512), ALU.mult)
                    P.tt("dve", s1, s1, LBB.v(hg * 512, (hg + 1) * 512), ALU.add)

            def SB(p):
                for i, (hg, st_) in enumerate(prs[p]):
                    lo = st_ * 1024 + hg * 512
                    P.act(logf.v(lo, lo + 512), tst[(p, i)][0], AF.Ln)
                for i, (hg, st_) in enumerate(prs[p]):
                    bf_, bi = tbanks(p, i)
                    lo = st_ * 1024 + hg * 512
                    s2 = scratch().v(0, 512)
                    tst[(p, i)].append(s2)
                    P.ts("dve", s2, tst[(p, i)][0], -1.0, ALU.mult, 1.0, ALU.add)
                    P.mm(ps(bf_), trirev.v(0, 128), logf.v(lo, lo + 512))

            def SC(p):
                for i, (hg, st_) in enumerate(prs[p]):
                    bf_, bi = tbanks(p, i)
                    lo = st_ * 1024 + hg * 512
                    s3 = scratch().v(0, 512)
                    P.act(s3, ps(bf_), AF.Exp)
                    P.tt("dve", khat.v(lo, lo + 512), tst[(p, i)][1], s3, ALU.mult)

            PA(0)
            for p in range(len(prs)):
                if p + 1 < len(prs):
                    PA(p + 1)
                SA(p)
                SB(p)
                SC(p)
        else:
            ktok = A.alloc(1024, F32)
            vtk = A.alloc(1024, F32)
            vd = A.alloc(NS * 128, F32)
            dm = A.alloc(NS * 128, F32)
            s0b = [A.alloc(1024, F32) for _ in range(4)]
            snb = [A.alloc(1024, F32) for _ in range(4)]

            def load_s0(hd_):
                for half_ in range(2):
                    s0v_ = s0b[(hd_ % 2) * 2 + half_].v3(0, 8, 128)
                    P.dma("sp", s0v_.ap, sth_d[half_ * 8:(half_ + 1) * 8, hd_, :, :].rearrange("s k v -> k s v"), [], [s0v_])

            load_s0(0)
            qs = A.alloc(NS, F32)
            fgf = A.alloc(NS, F32)
            dmv = dm.v3(0, NS, 128, 0, NS)
            P.memset("pool", dmv, 1.0)
            P.asel(dmv, dmv, [[1, NS], [0, 128]], ALU.is_equal, 0.0, 0, -1)
            for hg in range(2):
                (wf,) = W.get([(c_wq[:, 1024 + hg * 512:1024 + (hg + 1) * 512].rearrange("(k p) n -> p k n", p=128), 8, 512)])
                (wi,) = W.get([(c_wq[:, 2048 + hg * 512:2048 + (hg + 1) * 512].rearrange("(k p) n -> p k n", p=128), 8, 512)])
                bf_, bi = P.bank(), P.bank()
                for k in range(8):
                    P.mm(ps(bf_, 0, 512, 0, NS), h.v(k * n, k * n + NS), wsub(wf, k, 0, 512), start=(k == 0), stop=(k == 7))
                for k in range(8):
                    P.mm(ps(bi, 0, 512, 0, NS), h.v(k * n, k * n + NS), wsub(wi, k, 0, 512), start=(k == 0), stop=(k == 7))
                s1 = scratch().v(0, 512, 0, NS)
                P.act(s1, ps(bf_, 0, 512, 0, NS), AF.Sigmoid, scale=-1.0)
                P.tt("dve", ktok.v(hg * 512, (hg + 1) * 512, 0, NS), s1, OMLB.v(hg * 512, (hg + 1) * 512, 0, NS), ALU.mult)
                P.cp("act", vtk.v(hg * 512, (hg + 1) * 512, 0, NS), ps(bi, 0, 512, 0, NS))
        if C_STOP == 1:
            return
        if sample:
            NH = 8 * NS
            for hd in range(8):
                if hd % 4 == 0:
                    hq = hd // 4
                    wts = [W.get([(c_wq[:, g * 1024 + hq * 512:g * 1024 + (hq + 1) * 512].rearrange("(k p) n -> p k n", p=128), 8, 512)])[0]
                           for g in (0, 1, 3)]
                cc0 = (hd % 4) * 128
                for gi in range(3):
                    for k in range(8):
                        P.mm(ps(gi, hd * NS, (hd + 1) * NS), wsub(wts[gi], k, cc0, cc0 + 128), h.v(k * n, k * n + NS),
                             start=(k == 0), stop=(k == 7))
            qs_all = A.alloc(NH, F32)
            sg_all = A.alloc(NH, F32)
            fg_all = A.alloc(NH, F32)
            P.act(qs_all.v(0, NH), ps(0, 0, NH), AF.Silu)
            P.act(sg_all.v(0, NH), ps(2, 0, NH), AF.Silu)
            P.act(fg_all.v(0, NH), ps(1, 0, NH), AF.Sigmoid, scale=-1.0)
            P.ts("dve", qs_all.v(0, NH), qs_all.v(0, NH), 128.0 ** -0.5, ALU.mult)
            P.tt("dve", fg_all.v3(0, 8, NS), fg_all.v3(0, 8, NS),
                 V(OML.ap[:, 0:8].unsqueeze(2).to_broadcast([128, 8, NS]), [OML.b(0, 8)]), ALU.mult)
            P.ts("dve", fg_all.v(0, NH), fg_all.v(0, NH), -1.0, ALU.mult, 1.0, ALU.add)
            BO = 4
            for hd in range(8):
                if hd + 1 < 8:
                    load_s0(hd + 1)
                vb = V(vtk.ap[0:NS, hd * 128:(hd + 1) * 128].unsqueeze(1).to_broadcast([NS, NS, 128]),
                       [vtk.b(hd * 128, (hd + 1) * 128)])
                P.tt("dve", vd.v3(0, NS, 128, 0, NS), vb, dmv, ALU.mult)
                for half in range(2):
                    s0 = s0b[(hd % 2) * 2 + half]
                    sn = snb[(hd % 2) * 2 + half]
                    for i2 in range(2):
                        bkv = 6 + i2
                        P.mm(ps(bkv), ktok.v(hd * 128, (hd + 1) * 128, 0, NS),
                             vd.v((half * 8 + i2 * 4) * 128, (half * 8 + i2 * 4 + 4) * 128, 0, NS))
                    for jj in range(8):
                        col = hd * NS + half * 8 + jj
                        P.stt(sn.v(jj * 128, (jj + 1) * 128), s0.v(jj * 128, (jj + 1) * 128), fg_all.v(col, col + 1),
                              ps(6 + jj // 4, (jj % 4) * 128, (jj % 4 + 1) * 128), ALU.mult, ALU.add)
                    snv = sn.v3(0, 8, 128)
                    P.dma("sp", hgs_d[half * 8:(half + 1) * 8, hd, :, :].rearrange("s k v -> k s v"), snv.ap, [snv], [])
                    for jj in range(8):
                        col = hd * NS + half * 8 + jj
                        P.mm(ps(BO, col, col + 1), sn.v(jj * 128, (jj + 1) * 128), qs_all.v(col, col + 1))
            osq_all = A.alloc(NH, BF16)
            sd_all = A.alloc(NH, F32)
            P.act(osq_all.v(0, NH), ps(BO, 0, NH), AF.Square)
            P.mm(ps(5, 0, NH), ones_bf.v(0, 128), osq_all.v(0, NH))
            P.act(sd_all.v(0, NH), ps(5, 0, NH), AF.Ln, bias=1e-6, scale=1.0 / 128)
            P.act(sd_all.v(0, NH), sd_all.v(0, NH), AF.Exp, scale=-0.5)
            P.stt(sd_all.v(0, NH), ps(BO, 0, NH), cf(0, R_GN), sd_all.v(0, NH), ALU.mult, ALU.mult)
            P.tt("dve", z.v(0, NH), sd_all.v(0, NH), sg_all.v(0, NH), ALU.mult)

        def prep_pair(hds, wts):
            sqv, sgv, env = [], [], []
            for i, hd in enumerate(hds):
                cc0 = (hd % 4) * 128
                proj(2 * i, wts[0], h, n, cc0)
                proj(2 * i + 1, wts[1], h, n, cc0)
                for st_ in range(nst):
                    lo = st_ * 1024 + hd * 128
                    P.mm(ps(6 + i, st_ * 128, (st_ + 1) * 128), logf.v(lo, lo + 128), maskbd.v(0, 128))
            for i, hd in enumerate(hds):
                sqv.append(scratch().v(0, n))
                P.act(sqv[i], ps(2 * i, 0, n), AF.Silu)
            for i, hd in enumerate(hds):
                sgv.append(scratch().v(0, n))
                P.act(sgv[i], ps(2 * i + 1, 0, n), AF.Sigmoid, scale=-1.0)
            for i, hd in enumerate(hds):
                env.append(scratch().v(0, n))
                P.act(epd[i].v(0, n), ps(6 + i, 0, n), AF.Exp)
                P.act(env[i], ps(6 + i, 0, n), AF.Exp, scale=-1.0)
            for i, hd in enumerate(hds):
                omlv = V(OML.ap[:, hd:hd + 1], [OML.b(0, 8)])
                P.stt(qtd[i].v(0, n), sqv[i], 128.0 ** -0.5, epd[i].v(0, n), ALU.mult, ALU.mult)
                P.stt(ktd[i].v(0, n), sgv[i], omlv, env[i], ALU.mult, ALU.mult)

        def chain(hd, BO_, BKV_, epm, qt, kt, scm2, vm):
            for st_ in range(nst):
                c0 = st_ * 128
                lo = st_ * 1024 + hd * 128
                P.mm(ps(BS, 0, 128), kt.v(c0, c0 + 128), qt.v(c0, c0 + 128))
                sm = scm2[st_ % 2].v(0, 128)
                P.tt("dve", sm, ps(BS, 0, 128), maskbd.v(0, 128), ALU.mult)
                P.mm(ps(BO_, c0, c0 + 128), vtok.v(lo, lo + 128), sm, start=True, stop=False)
                P.tt("pool", vm.v3(0, 4, 128),
                     V(vtok.ap[:, lo:lo + 128].unsqueeze(1).to_broadcast([128, 4, 128]), [vtok.b(lo, lo + 128)]),
                     V(ind4.ap[:, 0:4].unsqueeze(2).to_broadcast([128, 4, 128]), [ind4.b(0, 4)]), ALU.mult)
                P.mm(ps(BKV_, 0, 512), khat.v(lo, lo + 128), vm.v(0, 512))
                yield
                for j in range(4):
                    par = sbf_par[hd]
                    so = hd * 256 + par * 128
                    P.mm(ps(BO_, c0 + 32 * j, c0 + 32 * j + 32), SBF.v(so, so + 128),
                         qt.v(c0 + 32 * j, c0 + 32 * j + 32), start=False, stop=(j == 3))
                    col = c0 + 32 * j + 31
                    s32v = S32.v(hd * 128, (hd + 1) * 128)
                    P.stt(s32v, s32v, epm.v(col, col + 1), ps(BKV_, j * 128, (j + 1) * 128), ALU.mult, ALU.add)
                    par ^= 1
                    sbf_par[hd] = par
                    so = hd * 256 + par * 128
                    P.cp("act", SBF.v(so, so + 128), s32v)
                    yield

        def gproj_gen(hds, wts):
            for i, hd in enumerate(hds):
                cc0 = (hd % 4) * 128
                for k in range(8):
                    P.mm(ps(2 * i, 0, n), wsub(wts[2], k, cc0, cc0 + 128), h.v(k * n, (k + 1) * n),
                         start=(k == 0), stop=(k == 7))
                    yield

        def norm_pair(hds, wts):
            sdv, sgv = [], []
            for i, hd in enumerate(hds):
                P.act(osqd[i].v(0, n), ps(4 + i, 0, n), AF.Square)
                P.mm(ps(2 * i + 1, 0, n), ones_bf.v(0, 128), osqd[i].v(0, n))
            for i, hd in enumerate(hds):
                sdv.append(scratch().v(0, n))
                P.act(sdv[i], ps(2 * i + 1, 0, n), AF.Ln, bias=1e-6, scale=1.0 / 128)
            for i, hd in enumerate(hds):
                P.act(sdv[i], sdv[i], AF.Exp, scale=-0.5)
            for i, hd in enumerate(hds):
                sgv.append(scratch().v(0, n))
                P.act(sgv[i], ps(2 * i, 0, n), AF.Silu)
            for i, hd in enumerate(hds):
                t1 = scratch().v(0, n)
                P.stt(t1, ps(4 + i, 0, n), cf(0, R_GN), sdv[i], ALU.mult, ALU.mult)
                P.tt("dve", z.v(hd * n, (hd + 1) * n), t1, sgv[i], ALU.mult)

        if not sample:
            BS = 3
            osqd = [osq, A.alloc(512, BF16)]
            for pair in range(4):
                if pair % 2 == 0:
                    hq = pair // 2
                    wts = [W.get([(c_wq[:, g * 1024 + hq * 512:g * 1024 + (hq + 1) * 512].rearrange("(k p) n -> p k n", p=128), 8, 512)])[0]
                           for g in (0, 1, 3)]
                hds = (2 * pair, 2 * pair + 1)
                prep_pair(hds, wts)
                gens = [chain(hds[i], 4 + i, 6 + i, epd[i], qtd[i], ktd[i], scm[2 * i:2 * i + 2], vmb[i])
                        for i in range(2)]
                gens.append(gproj_gen(hds, wts))
                alive = list(gens)
                while alive:
                    for g in list(alive):
                        try:
                            next(g)
                        except StopIteration:
                            alive.remove(g)
                norm_pair(hds, wts)
        out_proj(c_wo, z, t0, n)

    def ffn(layer):
        A.top = 0
        h = A.alloc(8 * NT, BF16)
        actb = A.alloc(6 * NT, BF16)
        sq = A.alloc(8 * 512, BF16)
        scr = [A.alloc(512, F32) for _ in range(4)]
        alltiles = TILES + [STILE]
        for (t0, n) in alltiles:
            rmsnorm(t0, n, R_NFFN + layer,
                    lambda c, t0=t0, n=n: V(h.ap[:, c * NT + t0:c * NT + t0 + n], [h.b(c * NT + t0, c * NT + t0 + n)]),
                    sq, scr[0])
        wgu = f_wgu[layer]
        wdn = f_wd[layer]
        si = 0
        if FFN_STOP == 0:
            return
        for gi, (g0, gs) in enumerate([(0, 6), (6, 6), (12, 5), (17, 5)]):
            cc = g0
            while cc < g0 + gs:
                nch = min(4, g0 + gs - cc)
                (wg,) = W.get([(wgu[:, cc * 128:(cc + nch) * 128].rearrange("(k p) n -> p k n", p=128), 8, nch * 128)])
                (wu,) = W.get([(wgu[:, DFF + cc * 128:DFF + (cc + nch) * 128].rearrange("(k p) n -> p k n", p=128), 8, nch * 128)])
                order = [(ci, tl) for ci in range(nch) for tl in alltiles]
                if cc == 0:
                    order = [(ci, tl) for tl in alltiles for ci in range(nch)]
                for ci, (t0, n) in order:
                    a_i = cc + ci - g0
                    if True:
                        bg, bu = P.bank(), P.bank()
                        for k in range(8):
                            P.mm(ps(bg, 0, n), wsub(wg, k, ci * 128, (ci + 1) * 128),
                                 h.v(k * NT + t0, k * NT + t0 + n), start=(k == 0), stop=(k == 7))
                        for k in range(8):
                            P.mm(ps(bu, 0, n), wsub(wu, k, ci * 128, (ci + 1) * 128),
                                 h.v(k * NT + t0, k * NT + t0 + n), start=(k == 0), stop=(k == 7))
                        sg = scr[1 + si % 3].v(0, n)
                        si += 1
                        P.act(sg, ps(bg, 0, n), AF.Silu)
                        P.tt("dve", actb.v(a_i * NT + t0, a_i * NT + t0 + n), sg, ps(bu, 0, n), ALU.mult)
                cc += nch
            if FFN_STOP == 1:
                return
            if layer == N_LAYERS - 1 and gi == 3 and FFN_STOP is None:
                wt2 = [W.get([(wdn[g0 * 128:(g0 + gs) * 128, hf * 512:(hf + 1) * 512].rearrange("(j p) n -> p j n", p=128), gs, 512)])[0]
                       for hf in range(2)]
                keep_top = A.top
                A.top = 0
                A_stage[0], A_stage[1] = A.alloc(1024, F32), A.alloc(1024, F32)
                yfin = A.alloc(8 * 512, F32)
                def dproj(t0, n):
                    for o in range(8):
                        bk = P.bank()
                        for jx in range(gs):
                            P.mm(ps(bk, 0, n), wsub(wt2[o // 4], jx, (o % 4) * 128, (o % 4 + 1) * 128),
                                 actb.v(jx * NT + t0, jx * NT + t0 + n), start=(jx == 0), stop=(jx == gs - 1))
                        resid_add(o, t0, n, bk)

                def fin_tile(t0, n):
                    rmsnorm(t0, n, R_NFIN, lambda c, n=n: yfin.v(c * 512, c * 512 + n), sq, scr[0])
                    A.top = 24576
                    if n == 512:
                        for q in range(4):
                            i = t0 // 128 + q
                            out_rows(lambda c, q=q: yfin.v(c * 512 + q * 128, c * 512 + (q + 1) * 128), 128,
                                     yp_d[i * 128:(i + 1) * 128, :])
                    else:
                        out_rows(lambda c: yfin.v(c * 512, c * 512 + NS), NS, ys_d[:, :])

                dproj(*alltiles[0])
                for ti_ in range(len(alltiles)):
                    if ti_ + 1 < len(alltiles):
                        dproj(*alltiles[ti_ + 1])
                    fin_tile(*alltiles[ti_])
                A.top = keep_top
                fin["done"] = True
                continue
            for half in range(2):
                (wt,) = W.get([(wdn[g0 * 128:(g0 + gs) * 128, half * 512:(half + 1) * 512].rearrange("(j p) n -> p j n", p=128), gs, 512)])
                for o4 in range(4):
                    o = half * 4 + o4
                    for (t0, n) in alltiles:
                        bk = P.bank()
                        for jx in range(gs):
                            P.mm(ps(bk, 0, n), wsub(wt, jx, o4 * 128, (o4 + 1) * 128),
                                 actb.v(jx * NT + t0, jx * NT + t0 + n), start=(jx == 0), stop=(jx == gs - 1))
                        resid_add(o, t0, n, bk)
            if FFN_STOP is not None and FFN_STOP >= 2 and gi == FFN_STOP - 2:
                return

    fin = {"done": False}
    for layer in range(N_LAYERS):
        kind, j = layer % 3, layer // 3
        if kind == 0:
            P.memset("dve", carryA.v(0, 16), 0.0)
        if kind == 0 and not (SKIP_MIX or kind in SKIP_KINDS):
            for (t0, n) in TILES[:3]:
                mixer_a(layer, j, [(t0, n, False)])
            mixer_a(layer, j, [(TILES[3][0], TILES[3][1], False), (STILE[0], STILE[1], True)])
        for (t0, n) in TILES + [STILE]:
            sample = (t0 == NPR)
            if SKIP_MIX or kind in SKIP_KINDS or kind == 0:
                continue
            if kind == 2 and C_TILES == 'p0' and t0 != 0:
                continue
            if kind == 2 and C_TILES == 's' and not sample:
                continue
            if kind == 0:
                mixer_a(layer, j, t0, n, sample)
            elif kind == 1:
                mixer_b(layer, t0, n, sample)
            else:
                mixer_c(layer, t0, n, sample)
        A.top = 0
        A_stage[0], A_stage[1] = A.alloc(1024, F32), A.alloc(1024, F32)
        if kind == 0:
            out_rows(lambda c: carryA.v(c * 2, c * 2 + 2), 2, cap_d[j, :, :])
        elif kind == 1:
            out_rows(lambda c: carryB.v(c * 30, c * 30 + 30), 30, cbp_d[:, :])
        else:
            sv = S32.v3(0, 8, 128)
            P.dma("sp", hgp_d[:, :, :].rearrange("h k v -> k h v"), sv.ap, [sv], [])
        if not SKIP_FFN:
            ffn(layer)

    if fin["done"]:
        return done()
    A.top = 0
    A_stage[0], A_stage[1] = A.alloc(1024, F32), A.alloc(1024, F32)
    sq = A.alloc(8 * 512, BF16)
    scr0 = A.alloc(512, F32)
    yf = [A.alloc(8 * 512, F32) for _ in range(2)]
    fi = 0
    for T in range(4):
        yb = yf[fi % 2]
        fi += 1
        rmsnorm(T * 512, 512, R_NFIN, lambda c, yb=yb: yb.v(c * 512, (c + 1) * 512), sq, scr0)
        for q in range(4):
            i = T * 4 + q
            out_rows(lambda c, yb=yb, q=q: yb.v(c * 512 + q * 128, c * 512 + (q + 1) * 128), 128,
                     yp_d[i * 128:(i + 1) * 128, :])
    if STOP_AFTER == 8:
        return done()
    yb = yf[fi % 2]
    rmsnorm(NPR, NS, R_NFIN, lambda c: yb.v(c * NS, (c + 1) * NS), sq, scr0)
    if STOP_AFTER == 9:
        return done()
    out_rows(lambda c: yb.v(c * NS, (c + 1) * NS), NS, ys_d[:, :])

    return done()


_CACHE = {}


def kernel(x_prompt, x_sample, state_conva, state_convb, state_hgrn, norm_mix, a_w_in, a_conv_w, a_w_out,
           b_w_pw1, b_b_pw1, b_dw_w, b_dw_b, b_ln_g, b_ln_b, b_w_pw2, b_b_pw2, c_lower_bounds, c_w_qfig,
           c_gnorm, c_w_out, norm_ffn, ffn_w_gate_up, ffn_w_down, norm_final):
    f = lambda a: np.ascontiguousarray(np.asarray(a, dtype=np.float32))
    nco = 8
    vec = np.zeros((NVEC, D), np.float32)
    vec[R_NMIX:R_NMIX + 4] = f(norm_mix)
    vec[R_NFFN:R_NFFN + 4] = f(norm_ffn)
    vec[R_NFIN] = f(norm_final)
    vec[R_ACONV:R_ACONV + 6] = f(a_conv_w).reshape(6, D)
    vec[R_BB1:R_BB1 + 2] = f(b_b_pw1).reshape(2, D)
    vec[R_BDW:R_BDW + 31] = f(b_dw_w).reshape(31, D)
    vec[R_BDWB] = f(b_dw_b).reshape(D)
    vec[R_LNG] = f(b_ln_g).reshape(D)
    vec[R_LNB] = f(b_ln_b).reshape(D)
    vec[R_BB2] = f(b_b_pw2).reshape(D)
    vec[R_CLB:R_CLB + 4] = f(c_lower_bounds)
    vec[R_GN, 0:128] = f(c_gnorm).reshape(128)
    shared = {
        "vec": vec, "a_w_in": f(a_w_in), "a_w_out": f(a_w_out), "b_w_pw1": f(b_w_pw1)[0], "b_w_pw2": f(b_w_pw2)[0],
        "c_w_qfig": f(c_w_qfig)[0], "c_w_out": f(c_w_out)[0], "ffn_w_gate_up": f(ffn_w_gate_up),
        "ffn_w_down": f(ffn_w_down),
    }
    xp, xs, sa, sb, sh = f(x_prompt), f(x_sample), f(state_conva), f(state_convb), f(state_hgrn)
    in_maps = []
    for c in range(nco):
        s0, s1 = c * NS, (c + 1) * NS
        m = dict(shared)
        m["xp"] = xp[c]
        m["xs"] = np.ascontiguousarray(xs[s0:s1, 0, :])
        m["sta"] = np.ascontiguousarray(sa[:, s0:s1]).reshape(2, 2 * NS, D)
        m["stb"] = np.ascontiguousarray(sb[0, s0:s1]).reshape(NS * 30, D)
        m["sth"] = np.ascontiguousarray(sh[0, s0:s1])
        in_maps.append(m)
    if "nc" not in _CACHE:
        _CACHE["nc"] = build_program()
    nc = _CACHE["nc"]
    if DEBUG_CORES is not None:
        res = run_bass_kernel_spmd(nc, in_maps[:DEBUG_CORES], core_ids=list(range(DEBUG_CORES)))
        r = list(res.results) + [res.results[0]] * (nco - DEBUG_CORES)
    else:
        res = run_bass_kernel_spmd(nc, in_maps, core_ids=list(range(nco)))
        r = res.results
    y_prompt = np.stack([r[c]["yp"] for c in range(nco)], 0)
    y_sample = np.concatenate([r[c]["ys"] for c in range(nco)], 0).reshape(128, 1, D)
    conva_prompt = np.stack([r[c]["cap"] for c in range(nco)], 1)
    conva_sample = np.concatenate([r[c]["cas"].reshape(2, NS, 2, D) for c in range(nco)], 1)
    convb_prompt = np.stack([r[c]["cbp"] for c in range(nco)], 0)[None]
    convb_sample = np.concatenate([r[c]["cbs"] for c in range(nco)], 0)[None]
    hgrn_prompt = np.stack([r[c]["hgp"] for c in range(nco)], 0)[None]
    hgrn_sample = np.concatenate([r[c]["hgs"] for c in range(nco)], 0)[None]
    return tuple(np.asarray(a, np.float32) for a in (y_prompt, y_sample, conva_prompt, conva_sample, convb_prompt,
                                                     convb_sample, hgrn_prompt, hgrn_sample))
```
